# Optimizing a Trainium2 kernel written in Bass

```python
import jax, jax.numpy as jnp
from jax import lax
import numpy as np

D_MODEL = 1024
BATCH = 8
SEQ = 4096
DEPTH = 1

CHUNK = 128
RET_HEADS = 4
RET_DK = 128
RET_DV = 256
RET_QK_W = RET_HEADS * RET_DK
RET_V_W = RET_HEADS * RET_DV
SGU_GROUPS = 4
SGU_W = 1024
SGU_GW = SGU_W // SGU_GROUPS
D_FF = 2816
CONV_W = 3
ROPE_BASE = 10000.0
EPS = 1e-6
IN_SPLITS = (RET_QK_W, RET_QK_W, RET_V_W, RET_V_W, SGU_W, SGU_W, D_MODEL, D_MODEL)
IN_W = sum(IN_SPLITS)

kernel_name = "hybrid_retention_sgu_convffn_block"


def _rmsnorm(x, g):
    xf = x.astype(jnp.float32)
    y = xf * lax.rsqrt(jnp.mean(xf * xf, axis=-1, keepdims=True) + EPS)
    return (y * g.astype(jnp.float32)).astype(x.dtype)


def _layernorm(x, g=None, b=None):
    xf = x.astype(jnp.float32)
    mu = jnp.mean(xf, axis=-1, keepdims=True)
    var = jnp.mean(jnp.square(xf - mu), axis=-1, keepdims=True)
    y = (xf - mu) * lax.rsqrt(var + EPS)
    if g is not None:
        y = y * g.astype(jnp.float32) + b.astype(jnp.float32)
    return y.astype(x.dtype)


def _rotary(t):
    S, d = t.shape[1], t.shape[-1]
    half = d // 2
    inv_freq = 1.0 / (ROPE_BASE ** (jnp.arange(half, dtype=jnp.float32) / half))
    ang = jnp.arange(S, dtype=jnp.float32)[:, None] * inv_freq[None, :]
    cos = jnp.cos(ang)[None, :, None, :].astype(t.dtype)
    sin = jnp.sin(ang)[None, :, None, :].astype(t.dtype)
    t1, t2 = t[..., :half], t[..., half:]
    return jnp.concatenate([t1 * cos - t2 * sin, t1 * sin + t2 * cos], axis=-1)


def _to_chunks(t):
    B, S, H, d = t.shape
    return t.reshape(B, S // CHUNK, CHUNK, H, d).transpose(0, 3, 1, 2, 4)


def _from_chunks(t):
    B, H, N, C, d = t.shape
    return t.transpose(0, 2, 3, 1, 4).reshape(B, N * C, H, d)


def _retention_one_direction(q, k, v, log_gamma, strict):
    dt = q.dtype
    idx = jnp.arange(CHUNK, dtype=jnp.float32)
    diff = idx[:, None] - idx[None, :]
    mask = (diff > 0) if strict else (diff >= 0)
    lg = log_gamma[:, None, None]
    dmat = jnp.where(mask[None], jnp.exp(jnp.where(mask, diff, 0.0)[None] * lg), 0.0)
    s = jnp.einsum('bhncd,bhnmd->bhncm', q, k) * dmat[None, :, None].astype(dt)
    intra = jnp.einsum('bhncm,bhnme->bhnce', s, v)
    zeta = jnp.exp((CHUNK - 1.0 - idx)[None, :] * log_gamma[:, None]).astype(dt)
    xi = jnp.exp((idx + 1.0)[None, :] * log_gamma[:, None]).astype(dt)
    decay_c = jnp.exp(CHUNK * log_gamma).astype(dt)[None, :, None, None]
    kv = jnp.einsum('bhnmd,bhnme->nbhde', k, v * zeta[None, :, None, :, None])

    def step(state, kv_n):
        return decay_c * state + kv_n, state

    _, r_prev = lax.scan(step, jnp.zeros_like(kv[0]), kv)
    cross = jnp.einsum('bhncd,nbhde->bhnce', q * xi[None, :, None, :, None], r_prev)
    return intra + cross


def _retention_branch(q, k, v, g, ret_decay_logit, w_ret_o):
    B, S, _ = q.shape
    q = _rotary(q.reshape(B, S, RET_HEADS, RET_DK))
    k = _rotary(k.reshape(B, S, RET_HEADS, RET_DK)) * (RET_DK ** -0.5)
    v = v.reshape(B, S, RET_HEADS, RET_DV)
    log_gamma = jax.nn.log_sigmoid(ret_decay_logit.astype(jnp.float32))
    fwd = _retention_one_direction(_to_chunks(q), _to_chunks(k), _to_chunks(v), log_gamma[0], False)
    qb, kb, vb = (jnp.flip(t, axis=1) for t in (q, k, v))
    bwd = _retention_one_direction(_to_chunks(qb), _to_chunks(kb), _to_chunks(vb), log_gamma[1], True)
    o = _from_chunks(fwd) + jnp.flip(_from_chunks(bwd), axis=1)
    o = _layernorm(o).reshape(B, S, RET_V_W)
    return (o * jax.nn.silu(g)) @ w_ret_o


def _sgu_branch(u, sv, sgu_ln_g, sgu_ln_b, w_spatial, b_spatial, w_sgu_o):
    B, S, _ = u.shape
    u = jax.nn.gelu(u)
    sv = _layernorm(jax.nn.gelu(sv), sgu_ln_g, sgu_ln_b)
    svc = sv.reshape(B, S // CHUNK, CHUNK, SGU_GROUPS, SGU_GW)
    mixed = jnp.einsum('gpq,bnqgc->bnpgc', w_spatial, svc) + b_spatial.T[None, None, :, :, None]
    y = u * mixed.reshape(B, S, SGU_W)
    return y @ w_sgu_o


def _conv_gated_mlp(h, w_up, conv_w, conv_b, w_down):
    hid = h @ w_up
    c = hid.shape[-1]
    hid = lax.conv_general_dilated(
        hid, conv_w.reshape(CONV_W, 1, c), window_strides=(1,),
        padding=((CONV_W // 2, CONV_W // 2),),
        dimension_numbers=('NWC', 'WIO', 'NWC'), feature_group_count=c) + conv_b
    a, b = jnp.split(hid, 2, axis=-1)
    return (jax.nn.gelu(a) * b) @ w_down


def setup_inputs(seed: int = 0) -> dict:
    key = jax.random.key(seed)
    ks = jax.random.split(key, 20)
    f32 = jnp.float32
    nrm = lambda k, shape, s: jax.random.normal(k, shape, f32) * s
    gamma0 = 1.0 - 2.0 ** (-5.0 - np.arange(RET_HEADS, dtype=np.float32))
    logit0 = jnp.asarray(np.log(gamma0) - np.log1p(-gamma0), f32)
    return {
        "x": nrm(ks[0], (BATCH, SEQ, D_MODEL), 1.0),
        "g_pre_mix": 1.0 + nrm(ks[1], (D_MODEL,), 0.05),
        "w_in": nrm(ks[2], (D_MODEL, IN_W), D_MODEL ** -0.5),
        "ret_decay_logit": logit0[None, :] + nrm(ks[3], (2, RET_HEADS), 0.1),
        "sgu_ln_g": 1.0 + nrm(ks[4], (SGU_W,), 0.05),
        "sgu_ln_b": nrm(ks[5], (SGU_W,), 0.02),
        "w_spatial": nrm(ks[6], (SGU_GROUPS, CHUNK, CHUNK), CHUNK ** -0.5),
        "b_spatial": nrm(ks[7], (SGU_GROUPS, CHUNK), 0.02),
        "w_ret_o": nrm(ks[8], (RET_V_W, D_MODEL), RET_V_W ** -0.5),
        "w_sgu_o": nrm(ks[9], (SGU_W, D_MODEL), SGU_W ** -0.5),
        "w_out": nrm(ks[10], (D_MODEL, D_MODEL), D_MODEL ** -0.5),
        "g_post_mix": 1.0 + nrm(ks[11], (D_MODEL,), 0.05),
        "g_pre_ffn": 1.0 + nrm(ks[12], (D_MODEL,), 0.05),
        "w_up": nrm(ks[13], (D_MODEL, 2 * D_FF), D_MODEL ** -0.5),
        "conv_w": nrm(ks[14], (CONV_W, 2 * D_FF), CONV_W ** -0.5),
        "conv_b": nrm(ks[15], (2 * D_FF,), 0.02),
        "w_down": nrm(ks[16], (D_FF, D_MODEL), D_FF ** -0.5),
        "g_post_ffn": 1.0 + nrm(ks[17], (D_MODEL,), 0.05),
    }


def reference(x, g_pre_mix, w_in, ret_decay_logit, sgu_ln_g, sgu_ln_b, w_spatial, b_spatial,
              w_ret_o, w_sgu_o, w_out, g_post_mix, g_pre_ffn, w_up, conv_w, conv_b, w_down,
              g_post_ffn):
    for _ in range(DEPTH):
        h = _rmsnorm(x, g_pre_mix)
        proj = h @ w_in
        q, k, v, g, u, sv, a_ret, a_sgu = jnp.split(proj, list(np.cumsum(IN_SPLITS)[:-1]), axis=-1)
        y_ret = _retention_branch(q, k, v, g, ret_decay_logit, w_ret_o)
        y_sgu = _sgu_branch(u, sv, sgu_ln_g, sgu_ln_b, w_spatial, b_spatial, w_sgu_o)
        merged = jax.nn.sigmoid(a_ret) * y_ret + jax.nn.sigmoid(a_sgu) * y_sgu
        x = x + _rmsnorm(merged @ w_out, g_post_mix)
        h2 = _rmsnorm(x, g_pre_ffn)
        x = x + _rmsnorm(_conv_gated_mlp(h2, w_up, conv_w, conv_b, w_down), g_post_ffn)
    return x
```

```python
import os
import numpy as np
from contextlib import ExitStack
import concourse.bass as bass
import concourse.mybir as mybir
from concourse.bass_utils import run_bass_kernel_spmd

F32 = mybir.dt.float32
BF16 = mybir.dt.bfloat16
AF = mybir.ActivationFunctionType
ALU = mybir.AluOpType

S = 4096
D = 1024
NCHUNK = 32
T = 512
NST = S // T
CPS = T // 128
INW = 7168
DFF = 2816
NJ = DFF // 128
EPS = 1e-6
RING = 8
NW = 6
SC = 128.0 ** -0.5
STOP = os.environ.get('KSTOP', 'all')


class Prog:
    def __init__(self, semh):
        self.semh = semh
        self.cnt = {'pe': 0, 'act': 0, 'dve': 0, 'pool': 0}
        self.ring_n = {'sp': 0, 'pool': 0}
        self.ring_use = {'sp': [0] * RING, 'pool': [0] * RING}
        self.streams = ['pe', 'act', 'dve', 'pool', 'sp']
        self.seen = {s: {} for s in self.streams}
        self.ops = {s: [] for s in self.streams}
        self.wr = {}
        self.rd = {}
        self.clock = {}
        self.order = {}
        self.nissued = 0

    def _deps(self, stream, reads, writes, extra=()):
        toks = {}

        def add(tok):
            if tok is None:
                return
            k, v = tok
            if v > toks.get(k, 0):
                toks[k] = v
        for r in reads:
            add(self.wr.get(r))
        for w in writes:
            add(self.wr.get(w))
            for k, v in self.rd.get(w, {}).items():
                add((k, v))
        for t in extra:
            add(t)
        waits = []
        seen = self.seen[stream]
        for k, v in sorted(toks.items(), key=lambda kv: -self.order.get(kv, 0)):
            if k == ('e', 'pe') and stream == 'pe':
                continue
            if seen.get(k, 0) >= v:
                continue
            seen[k] = v
            waits.append((self.semh[k], v))
            for k2, v2 in self.clock.get((k, v), {}).items():
                if v2 > seen.get(k2, 0):
                    seen[k2] = v2
        return waits

    def _commit(self, tok, reads, writes, stream=None):
        k, v = tok
        if stream is not None:
            snap = dict(self.seen[stream])
            snap[k] = max(snap.get(k, 0), v)
            self.clock[tok] = snap
            self.nissued += 1
            self.order[tok] = self.nissued
        for r in reads:
            d = self.rd.setdefault(r, {})
            if v > d.get(k, 0):
                d[k] = v
        for w in writes:
            self.wr[w] = tok
            self.rd[w] = {}

    def op(self, eng, fn, reads=(), writes=()):
        waits = self._deps(eng, reads, writes)
        self.cnt[eng] += 1
        tok = (('e', eng), self.cnt[eng])
        self._commit(tok, reads, writes, eng)
        self.ops[eng].append((waits, fn, (self.semh[('e', eng)], 1)))

    def dma(self, st, fn, reads=(), writes=()):
        n = self.ring_n[st]
        self.ring_n[st] += 1
        slot = n % RING
        prev = self.ring_use[st][slot]
        key = ('r', st, slot)
        extra = [(key, 16 * prev)] if prev > 0 else []
        waits = self._deps(st, reads, writes, extra)
        self.ring_use[st][slot] = prev + 1
        tok = (key, 16 * (prev + 1))
        self._commit(tok, reads, writes, st)
        self.ops[st].append((waits, fn, (self.semh[key], 16)))

    def barrier(self):
        toks = []
        for e, c in self.cnt.items():
            if c > 0:
                toks.append((('e', e), c))
        for st in ('sp', 'pool'):
            for slot in range(RING):
                u = self.ring_use[st][slot]
                if u > 0:
                    toks.append((('r', st, slot), 16 * u))
        for s in self.streams:
            waits = []
            for k, v in toks:
                if self.seen[s].get(k, 0) >= v:
                    continue
                self.seen[s][k] = v
                waits.append((self.semh[k], v))
            self.ops[s].append((waits, None, None))

    def take(self):
        o = self.ops
        self.ops = {s: [] for s in self.streams}
        return o


def _replay(eng, ops, embed=False):
    for waits, fn, inc in ops:
        if fn is None or not embed or not waits:
            for semh, val in waits:
                eng.wait_ge(semh, val)
            if fn is not None:
                ins = fn(eng)
                ins.then_inc(inc[0], inc[1])
        else:
            for semh, val in waits[:-1]:
                eng.wait_ge(semh, val)
            ins = fn(eng)
            ins._wait_ge(waits[-1][0], waits[-1][1])
            ins.then_inc(inc[0], inc[1])


class _PEFirst:
    def __init__(self, eng, wait):
        self.eng = eng
        self.wait = wait

    def _wrap(self, ins):
        if self.wait is not None:
            ins._wait_ge(self.wait[0], self.wait[1])
            self.wait = None
        return ins

    def matmul(self, *a, **k):
        return self._wrap(self.eng.matmul(*a, **k))

    def transpose(self, *a, **k):
        return self._wrap(self.eng.transpose(*a, **k))


def _replay_pe(eng, ops):
    for waits, fn, inc in ops:
        if fn is None or not waits:
            for semh, val in waits:
                eng.wait_ge(semh, val)
            if fn is not None:
                fn(eng).then_inc(inc[0], inc[1])
        else:
            for semh, val in waits[:-1]:
                eng.wait_ge(semh, val)
            prox = _PEFirst(eng, waits[-1])
            ins = fn(prox)
            assert prox.wait is None
            ins.then_inc(inc[0], inc[1])


def _run_block(nc, ops):
    with nc.Block() as block:
        @block.tensor
        def _(e):
            _replay_pe(e, ops['pe'])

        @block.scalar
        def _(e):
            _replay(e, ops['act'], embed=True)

        @block.vector
        def _(e):
            _replay(e, ops['dve'], embed=True)

        @block.gpsimd
        def _(e):
            _replay(e, ops['pool'], embed=True)

        @block.sync
        def _(e):
            _replay(e, ops['sp'], embed=True)


def build_nc():
    nc = bass.Bass("TRN2", target_bir_lowering=False)

    def din(name, shape, dt=F32):
        return nc.dram_tensor(name, list(shape), dt, kind="ExternalInput").ap()

    x = din("x", [S, D])
    w_in = din("w_in", [D, INW])
    w_ret_o = din("w_ret_o", [D, D])
    w_sgu_o = din("w_sgu_o", [D, D])
    w_out = din("w_out", [D, D])
    w_up = din("w_up", [D, 2 * DFF])
    w_down = din("w_down", [DFF, D])
    gcols = din("gcols", [128, 16])
    rows = din("rows", [4, D])
    logits = din("logits", [1, 8])
    wsp = din("wsp", [4, 128, 128])
    bspT = din("bspT", [128, 4])
    cwl = din("cwl", [128, 2 * NJ, 3])
    cbl = din("cbl", [128, 2 * NJ])
    ident = din("ident", [128, 128])
    cos_t = din("cos_t", [S, 64])
    sin_t = din("sin_t", [S, 64])
    dconst = din("dconst", [128, 6, 128])
    cvec = din("cvec", [128, 2])
    out = nc.dram_tensor("out", [S, D], F32, kind="ExternalOutput").ap()

    def dscr(name, shape, dt=BF16):
        return nc.dram_tensor(name, list(shape), dt, kind="Internal").ap()

    w_in_b = dscr("w_in_b", [D, INW])
    wro_b = dscr("wro_b", [D, D])
    wso_b = dscr("wso_b", [D, D])
    wout_b = dscr("wout_b", [D, D])
    wup_b = dscr("wup_b", [D, 2 * DFF])
    wdn_b = dscr("wdn_b", [DFF, D])
    rb_scr = dscr("rb_scr", [NCHUNK, 128, D])
    x1_scr = dscr("x1_scr", [S, D], F32)
    v_scr = dscr("v_scr", [NCHUNK, 128, D])
    k_scr = dscr("k_scr", [NCHUNK, 128, 512])

    with ExitStack() as cm:
        def sb(name, shape, dt=F32, stack=cm):
            return stack.enter_context(nc.sbuf_tensor(name, list(shape), dt))

        semh = {}
        for e in ('pe', 'act', 'dve', 'pool'):
            semh[('e', e)] = cm.enter_context(nc.semaphore("s_" + e))
        for st in ('sp', 'pool'):
            for i in range(RING):
                semh[('r', st, i)] = cm.enter_context(nc.semaphore("r_%s%d" % (st, i)))
        P = Prog(semh)

        NBANK = 8
        banks = [cm.enter_context(nc.psum_tensor("pb%d" % i, [128, 512], F32)) for i in range(NBANK)]
        bank_i = [0]

        pinned = set()

        def getbank(pin=False):
            for _ in range(2 * NBANK):
                i = bank_i[0] % NBANK
                bank_i[0] += 1
                if i not in pinned:
                    break
            else:
                raise RuntimeError("no free PSUM bank")
            if pin:
                pinned.add(i)
            return banks[i], ('ps', i)

        def gettr():
            bt, bk = getbank()
            return bt[:].bitcast(BF16), bk

        identb = sb("identb", [128, 128], BF16)
        gcol = sb("gcol", [128, 16])
        rot = {}

        def rtile(name, shape, dt, n, stack, pfx=""):
            rot[name] = ([sb("%s%s%d" % (pfx, name, i), shape, dt, stack) for i in range(n)], [0])

        def nxt(name):
            tiles, ctr = rot[name]
            i = ctr[0] % len(tiles)
            ctr[0] += 1
            return tiles[i], (name, i)

        rtile("junk", [128, D], BF16, 2, cm)
        P.dma('pool', lambda e: e.dma_start(out=identb[:], in_=ident[:, :]), writes=['identb'])
        P.dma('sp', lambda e: e.dma_start(out=gcol[:], in_=gcols[:, :]), writes=['gcol'])

        def cast(dst, src, c0, c1, key):
            P.dma('pool', lambda e: e.dma_start(out=dst[:, c0:c1], in_=src[:, c0:c1]), writes=[key])

        pending = []
        for g in [4, 5, 0] + list(range(6, 14)):
            pending.append((w_in_b, w_in, g, 'w_in_b'))
        for g in range(2):
            pending.append((wro_b, w_ret_o, g, 'wro_b'))
            pending.append((wso_b, w_sgu_o, g, 'wso_b'))
        for g in range(2):
            pending.append((wout_b, w_out, g, 'wout_b'))
        for g in range(11):
            pending.append((wup_b, w_up, g, 'wup_b'))
        for g in range(2):
            pending.append((wdn_b, w_down, g, 'wdn_b'))

        def emit_casts(k):
            for _ in range(k):
                if pending:
                    dst, src, g, nm = pending.pop(0)
                    cast(dst, src, g * 512, (g + 1) * 512, (nm, g))
        def emit_casts_until(remaining):
            while len(pending) > remaining:
                emit_casts(1)

        with ExitStack() as p1:
            def sb1(name, shape, dt=F32):
                return sb(name, shape, dt, p1)

            lgt = sb1("lgt", [128, 8])
            lg = sb1("lg", [128, 8])
            dc = sb1("dc", [128, 8])
            zf = sb1("zf", [128, 4])
            zb = sb1("zb", [128, 4])
            cv = sb1("cv", [128, 2])
            dcs = sb1("dcs", [128, 6, 128])
            DT = sb1("DT", [128, 4, 128])
            lng = sb1("lng", [128, D])
            lnb = sb1("lnb", [128, D])
            gpost = sb1("gpost", [128, D])
            wspT = sb1("wspT", [128, 4, 128], BF16)
            bsp = sb1("bsp", [128, 4])
            hT = sb1("hT", [128, 8, T], BF16)
            mT = sb1("mT", [128, 8, T], BF16)
            yrT = sb1("yrT", [128, 8, T], BF16)
            gate8 = sb1("gate8", [128, CPS, D], BF16)
            qkT = sb1("qkT", [128, 2, CPS, 4, 128], BF16)
            qx = sb1("qx", [128, 2, CPS, 4, 128], BF16)
            ysT = qkT[:].rearrange("p a c h n -> p (a c h n)").rearrange("p (k t) -> p k t", k=8)
            tas = qx[:].rearrange("p a c h n -> p (a c h n)").rearrange("p (c d) -> p c d", c=CPS)
            kz = sb1("kz", [128, CPS, 512], BF16)
            v8 = sb1("v8", [128, CPS, D], BF16)
            R32 = sb1("R32", [128, 4, 256])
            Rbf = [sb1("Rbf%d" % i, [128, 4, 256], BF16) for i in range(2)]
            wbuf = [sb1("wbuf%d" % i, [128, 8, 512], BF16) for i in range(NW)]
            rtile("ss", [128, 8], F32, 2, p1)
            rtile("st6", [128, 4, 6], F32, 3, p1)
            rtile("mv", [128, 4, 2], F32, 3, p1)
            rtile("st12", [128, 2, 6], F32, 3, p1)
            rtile("mv2", [128, 2], F32, 3, p1)
            rtile("sm", [128, 16], F32, 4, p1)
            rtile("xst", [128, D], F32, 4, p1)
            rtile("xn", [128, D], BF16, 2, p1)
            rtile("tg", [128, 512], F32, 2, p1)
            rtile("krot", [128, 512], F32, 1, p1)
            rtile("ra", [128, 256], F32, 1, p1)
            rtile("rb", [128, 256], F32, 1, p1)
            rtile("tm512", [128, 512], BF16, 2, p1)
            rtile("PT", [128, 512], BF16, 2, p1)
            rtile("on", [128, D], F32, 2, p1)
            rtile("rbc", [128, D], BF16, 3, p1)
            rtile("tm1024", [128, D], BF16, 4, p1)
            rtile("gsv", [128, D], F32, 1, p1)
            rtile("m1", [128, 512], F32, 1, p1)
            rtile("m2", [128, 512], F32, 1, p1)
            rtile("x1t", [128, 512], F32, 1, p1)
            rtile("cs", [128, 2, CPS, 64], F32, 2, p1)

            on0 = rot["on"][0][0]
            gsv0 = rot["gsv"][0][0]
            XIF = on0[:, 0:512].rearrange("p (h n) -> p h n", h=4)
            XIB = on0[:, 512:1024].rearrange("p (h n) -> p h n", h=4)
            wspf = gsv0[:, 0:512].rearrange("p (h n) -> p h n", h=4)
            wspb = gsv0[:, 512:768].bitcast(BF16).rearrange("p (h n) -> p h n", h=4)
            P.dma('sp', lambda e: e.dma_start(out=lgt[:], in_=logits[0:1, :].to_broadcast([128, 8])), writes=['lgt'])
            P.dma('sp', lambda e: e.dma_start(out=cv[:], in_=cvec[:, :]), writes=['cv'])
            P.dma('sp', lambda e: e.dma_start(out=dcs[:], in_=dconst[:, :, :]), writes=['dcs'])
            P.dma('sp', lambda e: e.dma_start(out=lng[:], in_=rows[2:3, :].to_broadcast([128, D])), writes=['lng'])
            P.dma('sp', lambda e: e.dma_start(out=lnb[:], in_=rows[3:4, :].to_broadcast([128, D])), writes=['lnb'])
            P.dma('sp', lambda e: e.dma_start(out=gpost[:], in_=rows[0:1, :].to_broadcast([128, D])), writes=['gpost'])
            P.dma('sp', lambda e: e.dma_start(out=bsp[:], in_=bspT[:, :]), writes=['bsp'])
            P.dma('sp', lambda e: e.dma_start(out=wspf[:], in_=wsp.rearrange("g p q -> p g q")), writes=['wspf'])

            P.op('act', lambda e: e.activation(out=lg[:], in_=lgt[:], func=AF.Exp, scale=-1.0), ['lgt'], ['lg'])
            P.op('dve', lambda e: e.tensor_scalar_add(out=lg[:], in0=lg[:], scalar1=1.0), ['lg'], ['lg'])
            P.op('act', lambda e: e.activation(out=lg[:], in_=lg[:], func=AF.Ln), ['lg'], ['lg'])
            P.op('dve', lambda e: e.tensor_scalar_mul(out=lg[:], in0=lg[:], scalar1=-1.0), ['lg'], ['lg'])
            P.op('act', lambda e: e.activation(out=dc[:], in_=lg[:], func=AF.Exp, scale=128.0), ['lg'], ['dc'])
            for h in range(4):
                P.op('act', lambda e, h=h: e.activation(out=zf[:, h:h + 1], in_=cv[:, 0:1], func=AF.Exp, scale=lg[:, h:h + 1]), ['lg', 'cv'], ['zf'])
                P.op('act', lambda e, h=h: e.activation(out=zb[:, h:h + 1], in_=cv[:, 1:2], func=AF.Exp, scale=lg[:, 4 + h:5 + h]), ['lg', 'cv'], ['zb'])
                P.op('act', lambda e, h=h: e.activation(out=DT[:, h, :], in_=dcs[:, 0, :], func=AF.Exp, scale=lg[:, h:h + 1]), ['lg', 'dcs'], ['DT'])
                P.op('dve', lambda e, h=h: e.tensor_tensor(out=DT[:, h, :], in0=DT[:, h, :], in1=dcs[:, 1, :], op=ALU.mult), ['DT', 'dcs'], ['DT'])
                P.op('act', lambda e, h=h: e.activation(out=XIF[:, h, :], in_=dcs[:, 2, :], func=AF.Exp, scale=lg[:, 4 + h:5 + h]), ['lg', 'dcs', 'DT'], ['XIF'])
                P.op('dve', lambda e, h=h: e.tensor_tensor(out=XIF[:, h, :], in0=XIF[:, h, :], in1=dcs[:, 3, :], op=ALU.mult), ['XIF', 'dcs'], ['XIF'])
                P.op('dve', lambda e, h=h: e.tensor_tensor(out=DT[:, h, :], in0=DT[:, h, :], in1=XIF[:, h, :], op=ALU.add), ['XIF', 'DT'], ['DT'])
            for h in range(4):
                P.op('act', lambda e, h=h: e.activation(out=XIF[:, h, :], in_=dcs[:, 4, :], func=AF.Exp, scale=lg[:, h:h + 1]), ['lg', 'dcs', 'DT'], ['XIF'])
                P.op('act', lambda e, h=h: e.activation(out=XIB[:, h, :], in_=dcs[:, 5, :], func=AF.Exp, scale=lg[:, 4 + h:5 + h]), ['lg', 'dcs'], ['XIB'])
            P.op('dve', lambda e: e.tensor_scalar_mul(out=XIF[:], in0=XIF[:], scalar1=SC), ['XIF'], ['XIF'])
            P.op('dve', lambda e: e.tensor_scalar_mul(out=XIB[:], in0=XIB[:], scalar1=SC), ['XIB'], ['XIB'])
            DX = dcs[:].rearrange("p a n -> p (a n)").bitcast(BF16)[:, 0:1024].rearrange("p (v h n) -> p v h n", v=2, h=4)
            P.op('dve', lambda e: e.tensor_tensor(out=DX[:, 0, :, :], in0=XIF[:], in1=identb[:].unsqueeze(1).to_broadcast([128, 4, 128]), op=ALU.mult),
                 ['XIF', 'identb', 'dcs', 'DT', 'XIB'], ['dcs', 'DX'])
            P.op('dve', lambda e: e.tensor_tensor(out=DX[:, 1, :, :], in0=XIB[:], in1=identb[:].unsqueeze(1).to_broadcast([128, 4, 128]), op=ALU.mult),
                 ['XIB', 'identb', 'dcs', 'DX'], ['dcs', 'DX'])
            P.op('dve', lambda e: e.tensor_copy(out=wspb[:], in_=wspf[:]), ['wspf'], ['wspb'])
            trt, trk = gettr()
            P.op('pe', lambda e: [e.transpose(trt[:, g * 128:(g + 1) * 128], wspb[:, g, :], identb[:]) for g in range(4)][-1],
                 ['wspb', 'identb'], [trk])
            P.op('dve', lambda e: e.tensor_copy(out=wspT[:].rearrange("p g q -> p (g q)"), in_=trt[:, 0:512]), [trk], ['wspT'])

            P.op('dve', lambda e: e.memset(on0[:, 0:1], 0.0), [], ['XIF', 'XIB'] + [(('on', 0), h) for h in range(4)])
            P.op('dve', lambda e: e.memset(gsv0[:, 0:1], 0.0), [], ['wspf', 'wspb'] + [(('gsv', 0), n) for n in range(2)])
            def rstd_a(src_ap, dst_ap, rkeys, wkey, scale):
                P.op('dve', lambda e: e.tensor_scalar(out=dst_ap, in0=src_ap, scalar1=scale, scalar2=EPS,
                                                      op0=ALU.mult, op1=ALU.add), rkeys, [wkey])

            def rstd_b(dst_ap, wkey):
                P.op('act', lambda e: e.activation(out=dst_ap, in_=dst_ap, func=AF.Sqrt), [wkey], [wkey])

            def rstd_c(dst_ap, wkey):
                P.op('dve', lambda e: e.reciprocal(out=dst_ap, in_=dst_ap), [wkey], [wkey])

            def rstd_from(src_ap, dst_ap, n, rkeys, wkey, scale):
                rstd_a(src_ap, dst_ap, rkeys, wkey, scale)
                rstd_b(dst_ap, wkey)
                rstd_c(dst_ap, wkey)

            def load_cs(s):
                cst, csk = nxt("cs")
                P.dma('sp', lambda e: e.dma_start(out=cst[:, 0, :, :], in_=cos_t[s * T:(s + 1) * T, :].rearrange("(c p) j -> p c j", p=128)), writes=[csk])
                P.dma('sp', lambda e: e.dma_start(out=cst[:, 1, :, :], in_=sin_t[s * T:(s + 1) * T, :].rearrange("(c p) j -> p c j", p=128)), reads=[], writes=[(csk, 's')])
                return cst, [csk, (csk, 's')]

            def norm_to_T(xt, xk, ss_col, gofs, dstT, dkey, col0, ncols=128, npart=128, rkey='rstd'):
                xnt, xnk = nxt("xn")
                P.op('act', lambda e: e.activation(out=xnt[0:npart, :], in_=xt[0:npart, :], func=AF.Identity, scale=ss_col),
                     [xk, rkey], [xnk])
                trt, trk = gettr()

                def tr(e):
                    for kb in range(8):
                        ins = e.transpose(trt[:, kb * ncols:(kb + 1) * ncols], xnt[0:npart, kb * 128:(kb + 1) * 128],
                                          identb[0:npart, 0:npart])
                    return ins
                P.op('pe', tr, [xnk, 'identb'], [trk])
                P.op('dve', lambda e: e.tensor_tensor(
                    out=dstT[:, :, col0:col0 + ncols],
                    in0=trt[:, 0:8 * ncols].rearrange("p (k n) -> p k n", k=8),
                    in1=gcol[:, gofs:gofs + 8].unsqueeze(2).to_broadcast([128, 8, ncols]), op=ALU.mult),
                    [trk, 'gcol'], [dkey])

            def norm_part1(s):
                xts = []
                ss, ssk = nxt("ss")
                for c in range(CPS):
                    xt, xk = nxt("xst")
                    r0 = s * T + c * 128
                    P.dma('sp', lambda e, xt=xt, r0=r0: e.dma_start(out=xt[:], in_=x[r0:r0 + 128, :]), writes=[xk])
                    jt, jk = nxt("junk")
                    P.op('act', lambda e, xt=xt, c=c, jt=jt: e.activation(out=jt[:], in_=xt[:], func=AF.Square, accum_out=ss[:, c:c + 1]),
                         [xk], [(ssk, c), jk])
                    xts.append((xt, xk))
                rstd_from(ss[:, 0:CPS], ss[:, 0:CPS], CPS, [(ssk, c) for c in range(CPS)], (ssk, 'rstd'), 1.0 / D)
                return (xts, ss, ssk)

            def norm_part2(st):
                xts, ss, ssk = st
                for c in range(CPS):
                    xt, xk = xts[c]
                    norm_to_T(xt, xk, ss[:, c:c + 1], 0, hT, ('hT', c), c * 128, rkey=(ssk, 'rstd'))

            wl_n = [0]

            def load_w(scr, g, key, nkb=8, rows_rearr="(kb p) c -> p kb c"):
                slot = wl_n[0] % NW
                wl_n[0] += 1
                wt = wbuf[slot]
                assert key in P.wr, key
                P.dma('sp', lambda e: e.dma_start(out=wt[:, 0:nkb, :], in_=scr[:, g * 512:(g + 1) * 512].rearrange(rows_rearr, p=128)),
                      reads=[key], writes=[('wb', slot)])
                return wt, ('wb', slot)

            def proj(c, wt, wk, srcT, skey):
                bt, bk = getbank()

                def mm(e):
                    for kb in range(8):
                        ins = e.matmul(bt[:], lhsT=srcT[:, kb, c * 128:(c + 1) * 128], rhs=wt[:, kb, :],
                                       start=(kb == 0), stop=(kb == 7))
                    return ins
                P.op('pe', mm, [wk] + (skey if isinstance(skey, list) else [skey]), [bk])
                return bt, bk

            def rotary(bt, bk, cst, cskeys, c, dst, dkey):
                b4 = bt[:].rearrange("p (h t j) -> p h t j", h=4, t=2)
                d4 = dst[:].rearrange("p (h t j) -> p h t j", h=4, t=2)
                cosb = cst[:, 0, c, :].unsqueeze(1).to_broadcast([128, 4, 64])
                sinb = cst[:, 1, c, :].unsqueeze(1).to_broadcast([128, 4, 64])
                ra, rak = nxt("ra")
                rb_, rbk = nxt("rb")
                ra3 = ra[:].rearrange("p (h j) -> p h j", h=4)
                rb3 = rb_[:].rearrange("p (h j) -> p h j", h=4)
                P.op('dve', lambda e: e.tensor_tensor(out=ra3, in0=b4[:, :, 0, :], in1=cosb, op=ALU.mult), [bk] + cskeys, [rak])
                P.op('dve', lambda e: e.tensor_tensor(out=rb3, in0=b4[:, :, 1, :], in1=sinb, op=ALU.mult), [bk] + cskeys, [rbk])
                P.op('dve', lambda e: e.tensor_tensor(out=d4[:, :, 0, :], in0=ra3, in1=rb3, op=ALU.subtract), [rak, rbk], [(dkey, 0)])
                P.op('dve', lambda e: e.tensor_tensor(out=ra3, in0=b4[:, :, 0, :], in1=sinb, op=ALU.mult), [bk] + cskeys, [rak])
                P.op('dve', lambda e: e.tensor_tensor(out=rb3, in0=b4[:, :, 1, :], in1=cosb, op=ALU.mult), [bk] + cskeys, [rbk])
                P.op('dve', lambda e: e.tensor_tensor(out=d4[:, :, 1, :], in0=ra3, in1=rb3, op=ALU.add), [rak, rbk], [(dkey, 1)])
                return [(dkey, 0), (dkey, 1)]

            def k_stage(c, wt, wk, cst, cskeys, zcol, kz_eng='pool'):
                bt, bk = proj(c, wt, wk, hT, ('hT', c))
                kr, krk = nxt("krot")
                keys = rotary(bt, bk, cst, cskeys, c, kr, krk)
                P.op(kz_eng, lambda e: e.tensor_tensor(
                    out=kz[:, c, :].rearrange("p (h d) -> p h d", h=4),
                    in0=kr[:].rearrange("p (h d) -> p h d", h=4),
                    in1=zcol.unsqueeze(2).to_broadcast([128, 4, 128]), op=ALU.mult), keys + ['zf', 'zb'], [('kz', c)])
                return kr, keys

            def v_stage(c, n, wt, wk):
                bt, bk = proj(c, wt, wk, hT, ('hT', c))
                P.op('act', lambda e: e.activation(out=v8[:, c, n * 512:(n + 1) * 512], in_=bt[:], func=AF.Identity), [bk], [('v8', c, n)])

            def kv_update(c, dcol0, rbf_dst, rbf_key):
                for hp in range(2):
                    bt, bk = getbank()

                    def mm(e, hp=hp, bt=bt):
                        for hh in range(2):
                            h = hp * 2 + hh
                            ins = e.matmul(bt[:, hh * 256:(hh + 1) * 256], lhsT=kz[:, c, h * 128:(h + 1) * 128],
                                           rhs=v8[:, c, h * 256:(h + 1) * 256], start=True, stop=True)
                        return ins
                    P.op('pe', mm, [('kz', c), ('v8', c, hp)], [bk])
                    for hh in range(2):
                        h = hp * 2 + hh
                        P.op('dve', lambda e, h=h, hh=hh, bt=bt: e.scalar_tensor_tensor(
                            out=R32[:, h, :], in0=R32[:, h, :], scalar=dc[:, dcol0 + h:dcol0 + h + 1],
                            in1=bt[:, hh * 256:(hh + 1) * 256], op0=ALU.mult, op1=ALU.add), [bk, ('R32', h), 'dc'], [('R32', h)])
                    P.op('act', lambda e, hp=hp: e.activation(out=rbf_dst[:, 2 * hp:2 * hp + 2, :], in_=R32[:, 2 * hp:2 * hp + 2, :], func=AF.Identity),
                         [('R32', 2 * hp), ('R32', 2 * hp + 1)], [(rbf_key, hp)])

            P.op('dve', lambda e: e.memset(R32[:], 0.0), [], [('R32', h) for h in range(4)])
            P.op('pool', lambda e: e.memset(Rbf[0][:], 0.0), [], [(('Rbf', 0), 0), (('Rbf', 0), 1)])
            rpar = 0
            if STOP != 'setup':
                def load_w_direct(g):
                    slot = wl_n[0] % NW
                    wl_n[0] += 1
                    wt = wbuf[slot]
                    P.dma('pool', lambda e: e.dma_start(out=wt[:], in_=w_in[:, g * 512:(g + 1) * 512].rearrange("(kb p) c -> p kb c", p=128)),
                          writes=[('wb', slot)])
                    return wt, ('wb', slot)
                wkt, wkk = load_w_direct(1)
                wv0, wv0k = load_w_direct(2)
                wv1, wv1k = load_w_direct(3)
                cs_of = {}
                cs_of[NST - 1] = load_cs(NST - 1)
                norm_part2(norm_part1(NST - 1))
                xts_next = None
                pend = None

                def state_step(i):
                    nonlocal rpar
                    cur = Rbf[rpar]
                    P.dma('sp', lambda e, cur=cur, i=i: e.dma_start(out=rb_scr[i], in_=cur[:].rearrange("p h e -> p (h e)")),
                          reads=[(('Rbf', rpar), 0), (('Rbf', rpar), 1)], writes=[('rb_scr', i)])
                    if i > 0:
                        rpar ^= 1
                        kv_update(i % CPS, 4, Rbf[rpar], ('Rbf', rpar))

                for i in range(NCHUNK - 1, -1, -1):
                    s_, c = divmod(i, CPS)
                    if c == CPS - 1:
                        emit_casts(1)
                        if s_ > 0:
                            cs_of[s_ - 1] = load_cs(s_ - 1)
                            xts_next = norm_part1(s_ - 1)
                    cst, cskeys = cs_of[s_]
                    kr, kkeys = k_stage(c, wkt, wkk, cst, cskeys, zb[:, :], kz_eng='dve')
                    kt, ktk = nxt("tm512")
                    P.op('act', lambda e, kt=kt, kr=kr: e.activation(out=kt[:], in_=kr[:], func=AF.Identity), kkeys, [ktk])
                    P.dma('sp', lambda e, kt=kt, i=i: e.dma_start(out=k_scr[i], in_=kt[:]), reads=[ktk], writes=[('k_scr', i)])
                    v_stage(c, 0, wv0, wv0k)
                    v_stage(c, 1, wv1, wv1k)
                    P.dma('sp', lambda e, c=c, i=i: e.dma_start(out=v_scr[i], in_=v8[:, c, :]),
                          reads=[('v8', c, 0), ('v8', c, 1)], writes=[('v_scr', i)])
                    if c == 0 and s_ > 0:
                        norm_part2(xts_next)
                    if pend is not None:
                        state_step(pend)
                    pend = i
                state_step(pend)

            P.op('dve', lambda e: e.memset(R32[:], 0.0), [('R32', h) for h in range(4)], [('R32', h) for h in range(4)])
            P.op('pool', lambda e: e.memset(Rbf[0][:], 0.0), [], [(('Rbf', 0), 0), (('Rbf', 0), 1)])
            fpar = 0
            RUN1 = STOP not in ('setup', 'p0')
            gl1 = [(w_in_b, 4, 'w_in_b'), (w_in_b, 5, 'w_in_b'),
                   (w_in_b, 0, 'w_in_b'),
                   (w_in_b, 6, 'w_in_b'), (w_in_b, 7, 'w_in_b'),
                   (w_in_b, 8, 'w_in_b'), (w_in_b, 9, 'w_in_b'),
                   (w_in_b, 10, 'w_in_b'), (w_in_b, 11, 'w_in_b'),
                   (w_in_b, 12, 'w_in_b'), (w_in_b, 13, 'w_in_b'),
                   (wro_b, 0, 'wro_b'), (wso_b, 0, 'wso_b'),
                   (wro_b, 1, 'wro_b'), (wso_b, 1, 'wso_b'),
                   (wout_b, 0, 'wout_b'), (wout_b, 1, 'wout_b')]
            NG = len(gl1)
            glist = gl1 * NST
            loaded = {}

            def ensure_cast(key):
                while key not in P.wr and pending:
                    emit_casts(1)

            def Wg(idx, la=3, cap=None):
                hi = idx + la + 1
                if cap is not None:
                    hi = min(hi, cap + 1)
                for j in range(min(hi, len(glist))):
                    if j not in loaded:
                        for jj in range(j, min(j + 5, len(glist))):
                            ensure_cast((glist[jj][2], glist[jj][1]))
                        scr, g, nm = glist[j]
                        loaded[j] = load_w(scr, g, (nm, g))
                return loaded[idx]

            def pipe(n, stages, lag=1, filler=None):
                st = {}
                for t in range(n + (len(stages) - 1) * lag):
                    for k, f in enumerate(stages):
                        idx = t - k * lag
                        if 0 <= idx < n:
                            st[idx] = f(idx, st.get(idx))
                    if filler is not None:
                        filler(t)

            if RUN1:
                cs_cur = load_cs(0)
                norm_part2(norm_part1(0))
            for s in (range(NST) if RUN1 else []):
                W = lambda idx, cap=None, s=s: Wg(s * NG + idx, cap=(None if cap is None else s * NG + cap))
                cst, cskeys = cs_cur

                for n in range(2):
                    wt, wk = W(n)
                    for c in range(CPS):
                        bt, bk = proj(c, wt, wk, hT, ('hT', c))
                        tg, tgk = nxt("tg")
                        P.op('act', lambda e, tg=tg, bt=bt: e.activation(out=tg[:], in_=bt[:], func=AF.Tanh, scale=0.5), [bk], [tgk])
                        P.op('dve', lambda e, tg=tg, bt=bt, c=c, n=n: e.scalar_tensor_tensor(
                            out=gate8[:, c, n * 512:(n + 1) * 512], in0=tg[:], scalar=1.0, in1=bt[:],
                            op0=ALU.add, op1=ALU.mult), [bk, tgk], [('gate8', c, n)])

                wq, wqk = W(2)
                P.dma('sp', lambda e, s=s: e.dma_start(out=v8[:], in_=v_scr[s * CPS:(s + 1) * CPS].rearrange("c p e -> p c e")),
                      reads=[('v_scr', s * CPS + c) for c in range(CPS)], writes=[('v8', c, n) for c in range(CPS) for n in range(2)])

                def qk_A(idx, _):
                    which, c = divmod(idx, CPS)
                    if which == 0:
                        bt, bk = proj(c, wq, wqk, hT, ('hT', c))
                        kr, krk = nxt("krot")
                        keys = rotary(bt, bk, cst, cskeys, c, kr, krk)
                        qt, qtk = nxt("tm512")
                        P.op('act', lambda e, qt=qt, kr=kr: e.activation(out=qt[:], in_=kr[:], func=AF.Identity), keys, [qtk])
                    else:
                        i = s * CPS + c
                        qt, qtk = nxt("tm512")
                        P.dma('sp', lambda e, qt=qt, i=i: e.dma_start(out=qt[:], in_=k_scr[i]), reads=[('k_scr', i)], writes=[qtk])
                        P.op('pool', lambda e, qt=qt, c=c: e.tensor_tensor(
                            out=kz[:, c, :].rearrange("p (h d) -> p h d", h=4),
                            in0=qt[:].rearrange("p (h d) -> p h d", h=4),
                            in1=zf[:, :].unsqueeze(2).to_broadcast([128, 4, 128]), op=ALU.mult), [qtk, 'zf'], [('kz', c)])
                    return (qt, qtk)

                def qk_B(idx, stt):
                    which, c = divmod(idx, CPS)
                    qt, qtk = stt
                    if which == 0:
                        dsts = [(qkT[:, 0, c, :, :], ('qT', c), None), (qx[:, 0, c, :, :], ('qxf', c), 0), (qx[:, 1, c, :, :], ('qxb', c), 1)]
                        for k, (dst, dkey, v) in enumerate(dsts):
                            bt, bk = getbank()

                            def mmq(e, bt=bt, v=v):
                                for h in range(4):
                                    rhs = identb[:] if v is None else DX[:, v, h, :]
                                    ins = e.matmul(bt[:, h * 128:(h + 1) * 128], lhsT=qt[:, h * 128:(h + 1) * 128], rhs=rhs, start=True, stop=True)
                                return ins
                            P.op('pe', mmq, [qtk, 'identb', 'DX'], [bk])
                            eng = 'dve' if k == 2 else 'act'
                            if eng == 'act':
                                P.op('act', lambda e, bt=bt, dst=dst: e.activation(out=dst, in_=bt[:].rearrange("p (h n) -> p h n", h=4), func=AF.Identity),
                                     [bk], [dkey])
                            else:
                                P.op('dve', lambda e, bt=bt, dst=dst: e.tensor_copy(out=dst, in_=bt[:].rearrange("p (h n) -> p h n", h=4)), [bk], [dkey])
                    else:
                        trt, trk = gettr()
                        P.op('pe', lambda e, trt=trt, qt=qt: [e.transpose(trt[:, h * 128:(h + 1) * 128], qt[:, h * 128:(h + 1) * 128], identb[:])
                                                              for h in range(4)][-1], [qtk, 'identb'], [trk])
                        tr3 = trt[:, 0:512].rearrange("p (h n) -> p h n", h=4)
                        P.op('act', lambda e, tr3=tr3, c=c: e.activation(out=qkT[:, 1, c, :, :], in_=tr3, func=AF.Identity), [trk], [('kT', c)])
                    return stt
                pipe(2 * CPS, [qk_A, qk_B])
                W(3)
                emit_casts(2)

                def load_rb(c):
                    i = s * CPS + c
                    rt, rk = nxt("rbc")
                    P.dma('sp', lambda e, rt=rt, i=i: e.dma_start(out=rt[:], in_=rb_scr[i]), reads=[('rb_scr', i)], writes=[rk])
                    return rt, rk
                rbs = {0: load_rb(0), 1: load_rb(1)}

                def ret_A(c, _):
                    nonlocal fpar
                    i = s * CPS + c
                    if c + 2 < CPS:
                        rbs[c + 2] = load_rb(c + 2)
                    rt, rk = rbs[c]
                    rcur = Rbf[fpar]
                    rkeys = [(('Rbf', fpar), 0), (('Rbf', fpar), 1)]
                    if i < NCHUNK - 1:
                        fpar ^= 1
                        kv_update(c, 0, Rbf[fpar], ('Rbf', fpar))
                    stb, stk = getbank()
                    P.op('pe', lambda e, stb=stb, c=c: [e.matmul(stb[:, h * 128:(h + 1) * 128], lhsT=qkT[:, 1, c, h, :], rhs=qkT[:, 0, c, h, :],
                                                                 start=True, stop=True) for h in range(4)][-1],
                         [('qT', c), ('kT', c)], [stk])
                    pt, ptk = nxt("PT")
                    P.op('dve', lambda e, pt=pt, stb=stb: e.tensor_tensor(out=pt[:], in0=stb[:], in1=DT[:].rearrange("p h n -> p (h n)"), op=ALU.mult),
                         [stk, 'DT'], [ptk])
                    obanks = []
                    for hp in range(2):
                        ob, obk = getbank(pin=True)

                        def omm(e, hp=hp, ob=ob, pt=pt, rcur=rcur, c=c, rt=rt):
                            for hh in range(2):
                                h = hp * 2 + hh
                                o_ = ob[:, hh * 256:(hh + 1) * 256]
                                e.matmul(o_, lhsT=pt[:, h * 128:(h + 1) * 128], rhs=v8[:, c, h * 256:(h + 1) * 256], start=True, stop=False)
                                e.matmul(o_, lhsT=qx[:, 0, c, h, :], rhs=rcur[:, h, :], start=False, stop=False)
                                ins = e.matmul(o_, lhsT=qx[:, 1, c, h, :], rhs=rt[:, h * 256:(h + 1) * 256], start=False, stop=True)
                            return ins
                        P.op('pe', omm, [ptk, ('v8', c, hp), ('qxf', c), ('qxb', c), rk] + rkeys, [obk])
                        obanks.append((ob, obk))
                    return obanks

                def ret_B1(c, obanks):
                    sm, smk = nxt("sm")
                    mv, mvk = nxt("mv")
                    st6, s6k = nxt("st6")
                    for h in range(4):
                        ob, obk = obanks[h // 2]
                        osl = ob[:, (h % 2) * 256:(h % 2 + 1) * 256]
                        P.op('dve', lambda e, h=h, osl=osl: e.bn_stats(out=st6[:, h, :], in_=osl), [obk], [(s6k, h)])
                        P.op('dve', lambda e, h=h: e.bn_aggr(out=mv[:, h, :], in_=st6[:, h, :]), [(s6k, h)], [(mvk, h)])
                    rstd_a(mv[:, :, 1], sm[:, 0:4], [(mvk, h) for h in range(4)], (smk, 'rs4'), 1.0)
                    rstd_b(sm[:, 0:4], (smk, 'rs4'))
                    return (obanks, sm, smk, mv, mvk)

                def ret_B2(c, stt):
                    obanks, sm, smk, mv, mvk = stt
                    rstd_c(sm[:, 0:4], (smk, 'rs4'))
                    P.op('dve', lambda e: e.scalar_tensor_tensor(out=sm[:, 4:8], in0=mv[:, :, 0], scalar=-1.0, in1=sm[:, 0:4],
                                                                 op0=ALU.mult, op1=ALU.mult), [(smk, 'rs4')] + [(mvk, h) for h in range(4)], [(smk, 'nmr')])
                    on, onk = nxt("on")
                    for h in range(4):
                        ob, obk = obanks[h // 2]
                        osl = ob[:, (h % 2) * 256:(h % 2 + 1) * 256]
                        P.op('act', lambda e, h=h, osl=osl, on=on: e.activation(out=on[:, h * 256:(h + 1) * 256], in_=osl, func=AF.Identity,
                                                                                bias=sm[:, 4 + h:5 + h], scale=sm[:, h:h + 1]),
                             [obk, (smk, 'rs4'), (smk, 'nmr')], [(onk, h)])
                    yt, ytk = nxt("tm1024")
                    P.op('dve', lambda e, yt=yt, on=on, c=c: e.tensor_tensor(out=yt[:], in0=on[:], in1=gate8[:, c, :], op=ALU.mult),
                         [(onk, h) for h in range(4)] + [('gate8', c, 0), ('gate8', c, 1)], [ytk])
                    for ob, obk in obanks:
                        pinned.discard(obk[1])
                    return (yt, ytk)

                def ret_C(c, stt):
                    yt, ytk = stt
                    trt, trk = gettr()
                    P.op('pe', lambda e, trt=trt, yt=yt: [e.transpose(trt[:, kb * 128:(kb + 1) * 128], yt[:, kb * 128:(kb + 1) * 128], identb[:])
                                                          for kb in range(8)][-1], [ytk, 'identb'], [trk])
                    P.op('act', lambda e, trt=trt, c=c: e.activation(out=yrT[:, :, c * 128:(c + 1) * 128],
                                                                     in_=trt[:].rearrange("p (k n) -> p k n", k=8), func=AF.Identity, scale=0.5),
                         [trk], [('yrT', c)])
                    return stt
                def u_item(c, n):
                    wt, wk = W(3 + n, cap=5)
                    bt, bk = proj(c, wt, wk, hT, ('hT', c))
                    P.op('act', lambda e, bt=bt, c=c, n=n: e.activation(out=gate8[:, c, n * 512:(n + 1) * 512], in_=bt[:], func=AF.Gelu_apprx_tanh),
                         [bk], [('gate8', c, n)])

                def u_fill(t):
                    c = t - 1
                    if 0 <= c < CPS:
                        u_item(c, 0)
                        u_item(c, 1)
                rst = {}
                for t in range(CPS + 2):
                    if 0 <= t - 1 < CPS:
                        rst[t - 1] = ret_B1(t - 1, rst[t - 1])
                    if t < CPS:
                        rst[t] = ret_A(t, None)
                    if 0 <= t - 1 < CPS:
                        rst[t - 1] = ret_B2(t - 1, rst[t - 1])
                    if 0 <= t - 2 < CPS:
                        ret_C(t - 2, rst[t - 2])
                    u_fill(t)

                w0, w0k = W(5)
                w1, w1k = W(6)
                qkkeys = [('qT', c) for c in range(CPS)] + [('kT', c) for c in range(CPS)]

                def sv_A1(c):
                    sm, smk = nxt("sm")
                    mv2, mv2k = nxt("mv2")
                    st12, s12k = nxt("st12")
                    gs, gsk = nxt("gsv")
                    for n, (wt, wk) in enumerate(((w0, w0k), (w1, w1k))):
                        bt, bk = proj(c, wt, wk, hT, ('hT', c))
                        P.op('act', lambda e, bt=bt, gs=gs, n=n: e.activation(out=gs[:, n * 512:(n + 1) * 512], in_=bt[:], func=AF.Gelu_apprx_tanh),
                             [bk], [(gsk, n)])
                        P.op('dve', lambda e, gs=gs, n=n: e.bn_stats(out=st12[:, n, :], in_=gs[:, n * 512:(n + 1) * 512]), [(gsk, n)], [(s12k, n)])
                    P.op('dve', lambda e: e.bn_aggr(out=mv2[:], in_=st12[:].rearrange("p a b -> p (a b)")), [(s12k, 0), (s12k, 1)], [mv2k])
                    rstd_a(mv2[:, 1:2], sm[:, 8:9], [mv2k], (smk, 'rs1'), 1.0)
                    return (sm, smk, mv2, mv2k, gs, gsk)

                def sv_A2(st):
                    sm, smk, mv2, mv2k, gs, gsk = st
                    rstd_b(sm[:, 8:9], (smk, 'rs1'))

                def sv_A3(st):
                    sm, smk, mv2, mv2k, gs, gsk = st
                    rstd_c(sm[:, 8:9], (smk, 'rs1'))
                    P.op('dve', lambda e, gs=gs: e.tensor_scalar(out=gs[:], in0=gs[:], scalar1=mv2[:, 0:1], scalar2=sm[:, 8:9],
                                                                 op0=ALU.subtract, op1=ALU.mult), [(gsk, 0), (gsk, 1), mv2k, (smk, 'rs1')], [(gsk, 0), (gsk, 1)])
                    P.op('dve', lambda e, gs=gs: e.tensor_tensor(out=gs[:], in0=gs[:], in1=lng[:], op=ALU.mult), [(gsk, 0), (gsk, 1), 'lng'], [(gsk, 0), (gsk, 1)])
                    sv, svk = nxt("tm1024")
                    P.op('dve', lambda e, gs=gs, sv=sv: e.tensor_tensor(out=sv[:], in0=gs[:], in1=lnb[:], op=ALU.add), [(gsk, 0), (gsk, 1), 'lnb'], [svk])
                    return (sv, svk)

                def sv_B(c, stt):
                    sv, svk = stt
                    ys, ysk = nxt("tm1024")
                    for gp in range(2):
                        bt, bk = getbank()
                        P.op('pe', lambda e, bt=bt, sv=sv, gp=gp: [e.matmul(bt[:, gg * 256:(gg + 1) * 256], lhsT=wspT[:, gp * 2 + gg, :],
                                                                          rhs=sv[:, (gp * 2 + gg) * 256:(gp * 2 + gg + 1) * 256], start=True, stop=True)
                                                                 for gg in range(2)][-1], [svk, 'wspT'], [bk])
                        for gg in range(2):
                            g = gp * 2 + gg
                            P.op('dve', lambda e, bt=bt, ys=ys, g=g, gg=gg, c=c: e.scalar_tensor_tensor(
                                out=ys[:, g * 256:(g + 1) * 256], in0=bt[:, gg * 256:(gg + 1) * 256], scalar=bsp[:, g:g + 1],
                                in1=gate8[:, c, g * 256:(g + 1) * 256], op0=ALU.add, op1=ALU.mult),
                                [bk, 'bsp', ('gate8', c, g // 2)], [(ysk, g)])
                    return (ys, ysk)

                def sv_C(c, stt):
                    ys, ysk = stt
                    trt, trk = gettr()
                    P.op('pe', lambda e, trt=trt, ys=ys: [e.transpose(trt[:, kb * 128:(kb + 1) * 128], ys[:, kb * 128:(kb + 1) * 128], identb[:])
                                                          for kb in range(8)][-1], [(ysk, g) for g in range(4)] + ['identb'], [trk])
                    P.op('act', lambda e, trt=trt, c=c: e.activation(out=ysT[:, :, c * 128:(c + 1) * 128],
                                                                     in_=trt[:].rearrange("p (k n) -> p k n", k=8), func=AF.Identity),
                         [trk] + qkkeys, [('ysT', c)] + qkkeys)
                    return stt
                qxkeys = [('qxf', c) for c in range(CPS)] + [('qxb', c) for c in range(CPS)]

                def gate_item(g, c):
                    wt, wk = W(7 + g, cap=9 if g < 3 else None)
                    bt, bk = proj(c, wt, wk, hT, ('hT', c))
                    if g < 2:
                        P.op('act', lambda e, bt=bt: e.activation(out=v8[:, c, g * 512:(g + 1) * 512], in_=bt[:], func=AF.Tanh, scale=0.5),
                             [bk], [('v8', c, g)])
                    else:
                        n = g - 2
                        P.op('act', lambda e, bt=bt: e.activation(out=tas[:, c, n * 512:(n + 1) * 512], in_=bt[:], func=AF.Tanh, scale=0.5),
                             [bk] + qxkeys, [('tas', c, n)] + qxkeys)
                g_items = [(g, c) for g in range(3) for c in range(CPS)]

                def g_fill(t):
                    for _ in range(2):
                        if g_items:
                            gate_item(*g_items.pop(0))
                sst = {}
                for t in range(CPS + 2):
                    a1 = sv_A1(t) if t < CPS else None
                    if 0 <= t - 2 < CPS:
                        sv_C(t - 2, sst[t - 2])
                    g_fill(t)
                    if a1 is not None:
                        sv_A2(a1)
                    if 0 <= t - 1 < CPS:
                        sst[t - 1] = sv_B(t - 1, sst[t - 1])
                    if a1 is not None:
                        sst[t] = sv_A3(a1)
                while g_items:
                    gate_item(*g_items.pop(0))
                for c in range(CPS):
                    gate_item(3, c)

                xts_next = None
                if s + 1 < NST:
                    cs_cur = load_cs(s + 1)
                    xts_next = norm_part1(s + 1)

                for n in range(2):
                    wr, wrk = W(11 + 2 * n)
                    ws, wsk = W(12 + 2 * n)
                    for c in range(CPS):
                        br, brk = proj(c, wr, wrk, yrT, ('yrT', c))
                        bs_, bsk = proj(c, ws, wsk, ysT, [('ysT', c)] + qkkeys)
                        m1, m1k = nxt("m1")
                        m2, m2k = nxt("m2")
                        P.op('dve', lambda e, m1=m1, br=br, c=c, n=n: e.scalar_tensor_tensor(
                            out=m1[:], in0=v8[:, c, n * 512:(n + 1) * 512], scalar=1.0, in1=br[:], op0=ALU.add, op1=ALU.mult),
                            [brk, ('v8', c, n)], [m1k])
                        P.op('dve', lambda e, m2=m2, bs_=bs_, c=c, n=n: e.scalar_tensor_tensor(
                            out=m2[:], in0=tas[:, c, n * 512:(n + 1) * 512], scalar=1.0, in1=bs_[:], op0=ALU.add, op1=ALU.mult),
                            [bsk, ('tas', c, n)] + qxkeys, [m2k])
                        P.op('pool', lambda e, m1=m1, m2=m2, c=c, n=n: e.tensor_tensor(out=gate8[:, c, n * 512:(n + 1) * 512], in0=m1[:], in1=m2[:], op=ALU.add),
                             [m1k, m2k], [('gate8', c, n)])
                for c in range(CPS):
                    trt, trk = gettr()
                    P.op('pe', lambda e, trt=trt, c=c: [e.transpose(trt[:, kb * 128:(kb + 1) * 128], gate8[:, c, kb * 128:(kb + 1) * 128], identb[:])
                                                        for kb in range(8)][-1], [('gate8', c, 0), ('gate8', c, 1), 'identb'], [trk])
                    P.op('act', lambda e, trt=trt, c=c: e.activation(out=mT[:, :, c * 128:(c + 1) * 128],
                                                                     in_=trt[:].rearrange("p (k n) -> p k n", k=8), func=AF.Identity, scale=0.5),
                         [trk], [('mT', c)])

                if xts_next is not None:
                    norm_part2(xts_next)

                wo0, wo0k = W(15)
                wo1, wo1k = W(16)
                def wo_X1(c):
                    r0 = s * T + c * 128
                    xt, xk = nxt("on")
                    sm, smk = nxt("sm")
                    P.dma('sp', lambda e, xt=xt, r0=r0: e.dma_start(out=xt[:], in_=x[r0:r0 + 128, :]), writes=[(xk, h) for h in range(4)])
                    pbs = []
                    for n, (wt, wk) in enumerate(((wo0, wo0k), (wo1, wo1k))):
                        bt, bk = proj(c, wt, wk, mT, ('mT', c))
                        jt, jk = nxt("junk")
                        P.op('act', lambda e, bt=bt, n=n, jt=jt, sm=sm: e.activation(out=jt[:, 0:512], in_=bt[:], func=AF.Square, accum_out=sm[:, 10 + n:11 + n]),
                             [bk], [(smk, 'ssq', n), jk])
                        pbs.append((bt, bk))
                    P.op('dve', lambda e, sm=sm: e.tensor_tensor(out=sm[:, 12:13], in0=sm[:, 10:11], in1=sm[:, 11:12], op=ALU.add), [(smk, 'ssq', 0), (smk, 'ssq', 1)], [(smk, 'ssq2')])
                    rstd_a(sm[:, 12:13], sm[:, 13:14], [(smk, 'ssq2')], (smk, 'rsq'), 1.0 / D)
                    return (r0, xt, xk, sm, smk, pbs)

                def wo_X2(st):
                    r0, xt, xk, sm, smk, pbs = st
                    rstd_b(sm[:, 13:14], (smk, 'rsq'))

                def wo_X3(st):
                    r0, xt, xk, sm, smk, pbs = st
                    rstd_c(sm[:, 13:14], (smk, 'rsq'))
                    xkeys = [(xk, h) for h in range(4)]
                    for n in range(2):
                        bt, bk = pbs[n]
                        t1, t1k = nxt("x1t")
                        P.op('dve', lambda e, bt=bt, t1=t1, n=n, sm=sm: e.scalar_tensor_tensor(
                            out=t1[:], in0=bt[:], scalar=sm[:, 13:14], in1=gpost[:, n * 512:(n + 1) * 512], op0=ALU.mult, op1=ALU.mult),
                            [bk, (smk, 'rsq'), 'gpost'], [t1k])
                        P.op('pool', lambda e, t1=t1, xt=xt, n=n: e.tensor_tensor(out=xt[:, n * 512:(n + 1) * 512], in0=xt[:, n * 512:(n + 1) * 512], in1=t1[:], op=ALU.add),
                             [t1k] + xkeys, xkeys)
                    P.dma('sp', lambda e, xt=xt, r0=r0: e.dma_start(out=x1_scr[r0:r0 + 128, :], in_=xt[:]), reads=xkeys, writes=[('x1s', r0 // 128)])

                wst = {0: wo_X1(0)}
                for c in range(CPS):
                    if c + 1 < CPS:
                        wst[c + 1] = wo_X1(c + 1)
                    wo_X2(wst[c])
                    wo_X3(wst[c])

            emit_casts_until(0)
            P.barrier()
            _run_block(nc, P.take())

        with ExitStack() as p2:
            def sb2(name, shape, dt=F32):
                return sb(name, shape, dt, p2)

            WUP = sb2("WUP", [128, 8, 2 * DFF], BF16)
            WDN = sb2("WDN", [128, NJ, D], BF16)
            X1 = [sb2("X1_%d" % i, [128, 2, D]) for i in range(3)]
            HB = [sb2("HB_%d" % i, [128, 8, 258], BF16) for i in range(2)]
            NSP = 4
            actT = sb2("actT", [128, NJ + NSP, 256], BF16)

            def aslot(j, ja):
                return (j * NJ + ja) % (NJ + NSP)
            rtile("s3", [128, 16], F32, 4, p2)
            gpostffn = sb2("gpostffn", [128, D])
            cw = sb2("cw", [128, 2 * NJ, 3])
            cb = sb2("cb", [128, 2 * NJ])
            P.dma('sp', lambda e: e.dma_start(out=gpostffn[:], in_=rows[1:2, :].to_broadcast([128, D])), writes=['gpostffn'])
            P.dma('sp', lambda e: e.dma_start(out=cw[:], in_=cwl[:, :, :]), writes=['cw'])
            P.dma('sp', lambda e: e.dma_start(out=cb[:], in_=cbl[:, :]), writes=['cb'])
            rtile("xn", [128, D], BF16, 2, p2, "q_")
            rtile("acc", [128, 256], F32, 8, p2)
            rtile("ga", [128, 256], F32, 3, p2)
            rtile("yt", [128, 512], F32, 2, p2)

            NT2 = S // 256 if STOP == 'all' else 0

            def hbkeys(b):
                return [(('HB', b), 0), (('HB', b), 1), (('HB', b), 'L'), (('HB', b), 'R')]

            def front(j):
                r0 = j * 256
                x1t = X1[j % 3]
                x1k = ('X1', j % 3)
                hb = HB[j % 2]
                s3, s3k = nxt("s3")
                P.dma('sp', lambda e: e.dma_start(out=x1t[:], in_=x1_scr[r0:r0 + 256, :].rearrange("(c p) d -> p c d", p=128)),
                      reads=[('x1s', r0 // 128), ('x1s', r0 // 128 + 1)], writes=[(x1k, 0), (x1k, 1)])
                for c in range(2):
                    jt, jk = nxt("junk")
                    P.op('act', lambda e, c=c, jt=jt: e.activation(out=jt[:], in_=x1t[:, c, :], func=AF.Square, accum_out=s3[:, c:c + 1]),
                         [(x1k, c)], [(s3k, c), jk])
                rstd_from(s3[:, 0:2], s3[:, 4:6], 2, [(s3k, 0), (s3k, 1)], (s3k, 'rstd'), 1.0 / D)
                for c in range(2):
                    norm_to_T(x1t[:, c, :], (x1k, c), s3[:, 4 + c:5 + c], 8, hb, (('HB', j % 2), c), 1 + c * 128, rkey=(s3k, 'rstd'))

            def halo_copy(dst_b, dst_col, src_b, src_col, dkey, skey):
                P.op('dve', lambda e: e.tensor_copy(out=HB[dst_b][:, :, dst_col:dst_col + 1], in_=HB[src_b][:, :, src_col:src_col + 1]),
                     [skey], [dkey])

            def halo_zero(b, col, dkey):
                P.op('dve', lambda e: e.memset(HB[b][:, :, col:col + 1], 0.0), [], [dkey])

            def up(j, ja_list):
                hb = HB[j % 2]
                hkeys = hbkeys(j % 2)
                for ja in ja_list:
                    accs = []
                    bts = []
                    for half in range(2):
                        ch = half * NJ + ja
                        bt, bk = getbank()

                        def mm(e, bt=bt, ch=ch):
                            for kb in range(8):
                                ins = e.matmul(bt[:, 0:258], lhsT=WUP[:, kb, ch * 128:(ch + 1) * 128], rhs=hb[:, kb, :],
                                               start=(kb == 0), stop=(kb == 7))
                            return ins
                        P.op('pe', mm, hkeys + [('WUP', ch // 4)], [bk])
                        ac, ack = nxt("acc")
                        P.op('act', lambda e, ac=ac, bt=bt, ch=ch: e.activation(out=ac[:], in_=bt[:, 0:256], func=AF.Identity,
                                                                               bias=cb[:, ch:ch + 1], scale=cw[:, ch, 0:1]), [bk, 'cw', 'cb'], [ack])
                        accs.append((ac, ack))
                        bts.append((bt, bk, ch))
                    if pend_gate[0] is not None:
                        pend_gate[0]()
                        pend_gate[0] = None
                    for tap in (1, 2):
                        for half in range(2):
                            ac, ack = accs[half]
                            bt, bk, ch = bts[half]
                            P.op('dve', lambda e, ac=ac, bt=bt, ch=ch, tap=tap: e.scalar_tensor_tensor(
                                out=ac[:], in0=bt[:, tap:tap + 256], scalar=cw[:, ch, tap:tap + 1], in1=ac[:],
                                op0=ALU.mult, op1=ALU.add), [bk, ack, 'cw'], [ack])
                    pend_gate[0] = (lambda accs=accs, sl=aslot(j, ja): gate(accs, sl))

            pend_gate = [None]

            def gate(accs, sl):
                ga, gak = nxt("ga")
                P.op('act', lambda e, ga=ga, a=accs[0][0]: e.activation(out=ga[:], in_=a[:], func=AF.Gelu_apprx_tanh), [accs[0][1]], [gak])
                P.op('pool', lambda e, ga=ga, b=accs[1][0], sl=sl: e.tensor_tensor(out=actT[:, sl, :], in0=ga[:], in1=b[:], op=ALU.mult),
                     [gak, accs[1][1]], [('actT', sl)])

            def flush_gate():
                if pend_gate[0] is not None:
                    pend_gate[0]()
                    pend_gate[0] = None

            def down(j):
                r0 = j * 256
                x1t = X1[j % 3]
                x1k = ('X1', j % 3)
                for c in range(2):
                    pbs = []
                    s3, s3k = nxt("s3")
                    for n in range(2):
                        bt, bk = getbank()

                        def mmd(e, bt=bt, c=c, n=n):
                            for jj in range(NJ):
                                ins = e.matmul(bt[:], lhsT=actT[:, aslot(j, jj), c * 128:(c + 1) * 128], rhs=WDN[:, jj, n * 512:(n + 1) * 512],
                                               start=(jj == 0), stop=(jj == NJ - 1))
                            return ins
                        P.op('pe', mmd, [('actT', aslot(j, jj)) for jj in range(NJ)] + [('WDN', n)], [bk])
                        jt, jk = nxt("junk")
                        P.op('act', lambda e, bt=bt, n=n, jt=jt, s3=s3: e.activation(out=jt[:, 0:512], in_=bt[:], func=AF.Square, accum_out=s3[:, 8 + n:9 + n]),
                             [bk], [(s3k, 'q', n), jk])
                        pbs.append((bt, bk))
                    P.op('dve', lambda e, s3=s3: e.tensor_tensor(out=s3[:, 10:11], in0=s3[:, 8:9], in1=s3[:, 9:10], op=ALU.add), [(s3k, 'q', 0), (s3k, 'q', 1)], [(s3k, 'q2')])
                    rstd_from(s3[:, 10:11], s3[:, 11:12], 1, [(s3k, 'q2')], (s3k, 'rsq'), 1.0 / D)
                    for n in range(2):
                        bt, bk = pbs[n]
                        yt, ytk = nxt("yt")
                        P.op('dve', lambda e, bt=bt, yt=yt, n=n, s3=s3: e.scalar_tensor_tensor(
                            out=yt[:], in0=bt[:], scalar=s3[:, 11:12], in1=gpostffn[:, n * 512:(n + 1) * 512], op0=ALU.mult, op1=ALU.mult),
                            [bk, (s3k, 'rsq'), 'gpostffn'], [ytk])
                        P.op('pool', lambda e, yt=yt, c=c, n=n: e.tensor_tensor(out=x1t[:, c, n * 512:(n + 1) * 512],
                                                                                 in0=x1t[:, c, n * 512:(n + 1) * 512], in1=yt[:], op=ALU.add),
                             [ytk, (x1k, c)], [(x1k, c)])
                    rr = r0 + c * 128
                    P.dma('sp', lambda e, c=c, rr=rr: e.dma_start(out=out[rr:rr + 128, :], in_=x1t[:, c, :]),
                          reads=[(x1k, c)], writes=[('out', rr // 128)])

            def load_wup(g):
                P.dma('sp', lambda e: e.dma_start(out=WUP[:, :, g * 512:(g + 1) * 512],
                                                  in_=wup_b[:, g * 512:(g + 1) * 512].rearrange("(kb p) c -> p kb c", p=128)),
                      reads=[('wup_b', g)], writes=[('WUP', g)])

            def load_wdn(g):
                P.dma('sp', lambda e: e.dma_start(out=WDN[:, :, g * 512:(g + 1) * 512],
                                                  in_=wdn_b[:, g * 512:(g + 1) * 512].rearrange("(j p) c -> p j c", p=128)),
                      reads=[('wdn_b', g)], writes=[('WDN', g)])

            load_wup(0)
            load_wup(5)
            if NT2:
                front(0)
                halo_zero(0, 0, (('HB', 0), 'L'))
                front(1)
                halo_copy(0, 257, 1, 1, (('HB', 0), 'R'), (('HB', 1), 0))
            for g in [1, 6, 2, 7, 3, 8, 4, 9, 10]:
                load_wup(g)
            load_wdn(0)
            load_wdn(1)
            for j in range(NT2):
                up(j, range(NSP if j > 0 else 0, NJ))
                b, nb = j % 2, (j + 1) % 2
                if j + 1 < NT2:
                    halo_copy(nb, 0, b, 256, (('HB', nb), 'L'), (('HB', b), 1))
                if j + 2 < NT2:
                    front(j + 2)
                    halo_copy(nb, 257, b, 1, (('HB', nb), 'R'), (('HB', b), 0))
                elif j + 1 < NT2:
                    halo_zero(nb, 257, (('HB', nb), 'R'))
                if j + 1 < NT2:
                    up(j + 1, range(0, NSP))
                else:
                    flush_gate()
                down(j)
            P.barrier()
            _run_block(nc, P.take())
    return nc


_NC_CACHE = {}


def _consts():
    half = 64
    inv_freq = (1.0 / (np.float32(10000.0) ** (np.arange(half, dtype=np.float32) / np.float32(half)))).astype(np.float32)
    ang = (np.arange(S, dtype=np.float32)[:, None] * inv_freq[None, :]).astype(np.float32)
    cos_t = np.cos(ang).astype(np.float32)
    sin_t = np.sin(ang).astype(np.float32)
    idx = np.arange(128, dtype=np.float32)
    m = idx[:, None]
    n = idx[None, :]
    Ef = np.maximum(n - m, 0.0)
    Mf = (n >= m).astype(np.float32) * np.float32(SC)
    Eb = np.maximum(m - n, 0.0)
    Mb = (m > n).astype(np.float32) * np.float32(SC)
    N1 = np.broadcast_to(n + 1.0, (128, 128))
    N128 = np.broadcast_to(128.0 - n, (128, 128))
    dconst = np.ascontiguousarray(np.stack([Ef, Mf, Eb, Mb, N1, N128], axis=1).astype(np.float32))
    cvec = np.ascontiguousarray(np.stack([127.0 - idx, idx], axis=1).astype(np.float32))
    ident = np.eye(128, dtype=np.float32)
    return cos_t, sin_t, dconst, cvec, ident


def kernel(x, g_pre_mix, w_in, ret_decay_logit, sgu_ln_g, sgu_ln_b, w_spatial, b_spatial,
           w_ret_o, w_sgu_o, w_out, g_post_mix, g_pre_ffn, w_up, conv_w, conv_b, w_down, g_post_ffn):
    f = lambda a: np.ascontiguousarray(np.asarray(a, dtype=np.float32))
    x = f(x)
    cos_t, sin_t, dconst, cvec, ident = _consts()
    gcols = np.ascontiguousarray(np.concatenate([f(g_pre_mix).reshape(8, 128).T, f(g_pre_ffn).reshape(8, 128).T], axis=1))
    rows = np.ascontiguousarray(np.stack([f(g_post_mix), f(g_post_ffn), f(sgu_ln_g), f(sgu_ln_b)], axis=0))
    logits = f(ret_decay_logit).reshape(1, 8)
    bspT = np.ascontiguousarray(f(b_spatial).T)
    cwl = np.ascontiguousarray(f(conv_w).reshape(3, 2 * NJ, 128).transpose(2, 1, 0))
    cbl = np.ascontiguousarray(f(conv_b).reshape(2 * NJ, 128).T)
    if 'nc' not in _NC_CACHE:
        _NC_CACHE['nc'] = build_nc()
    nc = _NC_CACHE['nc']
    shared = dict(w_in=f(w_in), w_ret_o=f(w_ret_o), w_sgu_o=f(w_sgu_o), w_out=f(w_out), w_up=f(w_up), w_down=f(w_down),
                  gcols=gcols, rows=rows, logits=logits, wsp=f(w_spatial), bspT=bspT, cwl=cwl, cbl=cbl, ident=ident,
                  cos_t=cos_t, sin_t=sin_t, dconst=dconst, cvec=cvec)
    in_maps = [dict(shared, x=x[b]) for b in range(8)]
    res = run_bass_kernel_spmd(nc, in_maps, core_ids=list(range(8)))
    return np.stack([np.asarray(r["out"], dtype=np.float32) for r in res.results], axis=0)
```

```python
import os
import numpy as np
from contextlib import ExitStack
import concourse.bass as bass
import concourse.mybir as mybir
from concourse.bass_utils import run_bass_kernel_spmd

F32 = mybir.dt.float32
BF16 = mybir.dt.bfloat16
AF = mybir.ActivationFunctionType
ALU = mybir.AluOpType

S = 4096
D = 1024
NCHUNK = 32
T = 512
NST = S // T
CPS = T // 128
INW = 7168
DFF = 2816
NJ = DFF // 128
EPS = 1e-6
RING = 8
NW = 6
SC = 128.0 ** -0.5
STOP = os.environ.get('KSTOP', 'all')


class Prog:
    def __init__(self, semh):
        self.semh = semh
        self.cnt = {'pe': 0, 'act': 0, 'dve': 0, 'pool': 0}
        self.ring_n = {'sp': 0, 'pool': 0}
        self.ring_use = {'sp': [0] * RING, 'pool': [0] * RING}
        self.streams = ['pe', 'act', 'dve', 'pool', 'sp']
        self.seen = {s: {} for s in self.streams}
        self.ops = {s: [] for s in self.streams}
        self.wr = {}
        self.rd = {}
        self.clock = {}
        self.order = {}
        self.nissued = 0

    def _deps(self, stream, reads, writes, extra=()):
        toks = {}

        def add(tok):
            if tok is None:
                return
            k, v = tok
            if v > toks.get(k, 0):
                toks[k] = v
        for r in reads:
            add(self.wr.get(r))
        for w in writes:
            add(self.wr.get(w))
            for k, v in self.rd.get(w, {}).items():
                add((k, v))
        for t in extra:
            add(t)
        waits = []
        seen = self.seen[stream]
        for k, v in sorted(toks.items(), key=lambda kv: -self.order.get(kv, 0)):
            if k == ('e', 'pe') and stream == 'pe':
                continue
            if seen.get(k, 0) >= v:
                continue
            seen[k] = v
            waits.append((self.semh[k], v))
            for k2, v2 in self.clock.get((k, v), {}).items():
                if v2 > seen.get(k2, 0):
                    seen[k2] = v2
        return waits

    def _commit(self, tok, reads, writes, stream=None):
        k, v = tok
        if stream is not None:
            snap = dict(self.seen[stream])
            snap[k] = max(snap.get(k, 0), v)
            self.clock[tok] = snap
            self.nissued += 1
            self.order[tok] = self.nissued
        for r in reads:
            d = self.rd.setdefault(r, {})
            if v > d.get(k, 0):
                d[k] = v
        for w in writes:
            self.wr[w] = tok
            self.rd[w] = {}

    def op(self, eng, fn, reads=(), writes=()):
        waits = self._deps(eng, reads, writes)
        self.cnt[eng] += 1
        tok = (('e', eng), self.cnt[eng])
        self._commit(tok, reads, writes, eng)
        self.ops[eng].append((waits, fn, (self.semh[('e', eng)], 1)))

    def dma(self, st, fn, reads=(), writes=()):
        n = self.ring_n[st]
        self.ring_n[st] += 1
        slot = n % RING
        prev = self.ring_use[st][slot]
        key = ('r', st, slot)
        extra = [(key, 16 * prev)] if prev > 0 else []
        waits = self._deps(st, reads, writes, extra)
        self.ring_use[st][slot] = prev + 1
        tok = (key, 16 * (prev + 1))
        self._commit(tok, reads, writes, st)
        self.ops[st].append((waits, fn, (self.semh[key], 16)))

    def barrier(self):
        toks = []
        for e, c in self.cnt.items():
            if c > 0:
                toks.append((('e', e), c))
        for st in ('sp', 'pool'):
            for slot in range(RING):
                u = self.ring_use[st][slot]
                if u > 0:
                    toks.append((('r', st, slot), 16 * u))
        for s in self.streams:
            waits = []
            for k, v in toks:
                if self.seen[s].get(k, 0) >= v:
                    continue
                self.seen[s][k] = v
                waits.append((self.semh[k], v))
            self.ops[s].append((waits, None, None))

    def take(self):
        o = self.ops
        self.ops = {s: [] for s in self.streams}
        return o


def _replay(eng, ops, embed=False):
    for waits, fn, inc in ops:
        if fn is None or not embed or not waits:
            for semh, val in waits:
                eng.wait_ge(semh, val)
            if fn is not None:
                ins = fn(eng)
                ins.then_inc(inc[0], inc[1])
        else:
            for semh, val in waits[:-1]:
                eng.wait_ge(semh, val)
            ins = fn(eng)
            ins._wait_ge(waits[-1][0], waits[-1][1])
            ins.then_inc(inc[0], inc[1])


class _PEFirst:
    def __init__(self, eng, wait):
        self.eng = eng
        self.wait = wait

    def _wrap(self, ins):
        if self.wait is not None:
            ins._wait_ge(self.wait[0], self.wait[1])
            self.wait = None
        return ins

    def matmul(self, *a, **k):
        return self._wrap(self.eng.matmul(*a, **k))

    def transpose(self, *a, **k):
        return self._wrap(self.eng.transpose(*a, **k))


def _replay_pe(eng, ops):
    for waits, fn, inc in ops:
        if fn is None or not waits:
            for semh, val in waits:
                eng.wait_ge(semh, val)
            if fn is not None:
                fn(eng).then_inc(inc[0], inc[1])
        else:
            for semh, val in waits[:-1]:
                eng.wait_ge(semh, val)
            prox = _PEFirst(eng, waits[-1])
            ins = fn(prox)
            assert prox.wait is None
            ins.then_inc(inc[0], inc[1])


def _run_block(nc, ops):
    with nc.Block() as block:
        @block.tensor
        def _(e):
            _replay_pe(e, ops['pe'])

        @block.scalar
        def _(e):
            _replay(e, ops['act'], embed=True)

        @block.vector
        def _(e):
            _replay(e, ops['dve'], embed=True)

        @block.gpsimd
        def _(e):
            _replay(e, ops['pool'], embed=True)

        @block.sync
        def _(e):
            _replay(e, ops['sp'], embed=True)


def build_nc():
    nc = bass.Bass("TRN2", target_bir_lowering=False)

    def din(name, shape, dt=F32):
        return nc.dram_tensor(name, list(shape), dt, kind="ExternalInput").ap()

    x = din("x", [S, D])
    w_in = din("w_in", [D, INW])
    w_ret_o = din("w_ret_o", [D, D])
    w_sgu_o = din("w_sgu_o", [D, D])
    w_out = din("w_out", [D, D])
    w_up = din("w_up", [D, 2 * DFF])
    w_down = din("w_down", [DFF, D])
    gcols = din("gcols", [128, 16])
    rows = din("rows", [4, D])
    logits = din("logits", [1, 8])
    wsp = din("wsp", [4, 128, 128])
    bspT = din("bspT", [128, 4])
    cwl = din("cwl", [128, 2 * NJ, 3])
    cbl = din("cbl", [128, 2 * NJ])
    ident = din("ident", [128, 128])
    cos_t = din("cos_t", [S, 64])
    sin_t = din("sin_t", [S, 64])
    dconst = din("dconst", [128, 6, 128])
    cvec = din("cvec", [128, 2])
    out = nc.dram_tensor("out", [S, D], F32, kind="ExternalOutput").ap()

    def dscr(name, shape, dt=BF16):
        return nc.dram_tensor(name, list(shape), dt, kind="Internal").ap()

    w_in_b = dscr("w_in_b", [D, INW])
    wro_b = dscr("wro_b", [D, D])
    wso_b = dscr("wso_b", [D, D])
    wout_b = dscr("wout_b", [D, D])
    wup_b = dscr("wup_b", [D, 2 * DFF])
    wdn_b = dscr("wdn_b", [DFF, D])
    rb_scr = dscr("rb_scr", [NCHUNK, 128, D])
    x1_scr = dscr("x1_scr", [S, D], F32)
    v_scr = dscr("v_scr", [NCHUNK, 128, D])
    k_scr = dscr("k_scr", [NCHUNK, 128, 512])

    with ExitStack() as cm:
        def sb(name, shape, dt=F32, stack=cm):
            return stack.enter_context(nc.sbuf_tensor(name, list(shape), dt))

        semh = {}
        for e in ('pe', 'act', 'dve', 'pool'):
            semh[('e', e)] = cm.enter_context(nc.semaphore("s_" + e))
        for st in ('sp', 'pool'):
            for i in range(RING):
                semh[('r', st, i)] = cm.enter_context(nc.semaphore("r_%s%d" % (st, i)))
        P = Prog(semh)

        NBANK = 8
        banks = [cm.enter_context(nc.psum_tensor("pb%d" % i, [128, 512], F32)) for i in range(NBANK)]
        bank_i = [0]

        pinned = set()

        def getbank(pin=False):
            for _ in range(2 * NBANK):
                i = bank_i[0] % NBANK
                bank_i[0] += 1
                if i not in pinned:
                    break
            else:
                raise RuntimeError("no free PSUM bank")
            if pin:
                pinned.add(i)
            return banks[i], ('ps', i)

        def gettr():
            bt, bk = getbank()
            return bt[:].bitcast(BF16), bk

        identb = sb("identb", [128, 128], BF16)
        gcol = sb("gcol", [128, 16])
        rot = {}

        def rtile(name, shape, dt, n, stack, pfx=""):
            rot[name] = ([sb("%s%s%d" % (pfx, name, i), shape, dt, stack) for i in range(n)], [0])

        def nxt(name):
            tiles, ctr = rot[name]
            i = ctr[0] % len(tiles)
            ctr[0] += 1
            return tiles[i], (name, i)

        rtile("junk", [128, D], BF16, 2, cm)
        P.dma('pool', lambda e: e.dma_start(out=identb[:], in_=ident[:, :]), writes=['identb'])
        P.dma('sp', lambda e: e.dma_start(out=gcol[:], in_=gcols[:, :]), writes=['gcol'])

        def cast(dst, src, c0, c1, key):
            P.dma('pool', lambda e: e.dma_start(out=dst[:, c0:c1], in_=src[:, c0:c1]), writes=[key])

        pending = []
        for g in [4, 5, 0] + list(range(6, 14)):
            pending.append((w_in_b, w_in, g, 'w_in_b'))
        for g in range(2):
            pending.append((wro_b, w_ret_o, g, 'wro_b'))
            pending.append((wso_b, w_sgu_o, g, 'wso_b'))
        for g in range(2):
            pending.append((wout_b, w_out, g, 'wout_b'))
        for g in range(11):
            pending.append((wup_b, w_up, g, 'wup_b'))
        for g in range(2):
            pending.append((wdn_b, w_down, g, 'wdn_b'))

        def emit_casts(k):
            for _ in range(k):
                if pending:
                    dst, src, g, nm = pending.pop(0)
                    cast(dst, src, g * 512, (g + 1) * 512, (nm, g))
        def emit_casts_until(remaining):
            while len(pending) > remaining:
                emit_casts(1)

        with ExitStack() as p1:
            def sb1(name, shape, dt=F32):
                return sb(name, shape, dt, p1)

            lgt = sb1("lgt", [128, 8])
            lg = sb1("lg", [128, 8])
            dc = sb1("dc", [128, 8])
            zf = sb1("zf", [128, 4])
            zb = sb1("zb", [128, 4])
            cv = sb1("cv", [128, 2])
            dcs = sb1("dcs", [128, 6, 128])
            DT = sb1("DT", [128, 4, 128])
            lng = sb1("lng", [128, D])
            lnb = sb1("lnb", [128, D])
            gpost = sb1("gpost", [128, D])
            wspT = sb1("wspT", [128, 4, 128], BF16)
            bsp = sb1("bsp", [128, 4])
            hT = sb1("hT", [128, 8, T], BF16)
            mT = sb1("mT", [128, 8, T], BF16)
            yrT = sb1("yrT", [128, 8, T], BF16)
            gate8 = sb1("gate8", [128, CPS, D], BF16)
            qkT = sb1("qkT", [128, 2, CPS, 4, 128], BF16)
            qx = sb1("qx", [128, 2, CPS, 4, 128], BF16)
            ysT = qkT[:].rearrange("p a c h n -> p (a c h n)").rearrange("p (k t) -> p k t", k=8)
            tas = qx[:].rearrange("p a c h n -> p (a c h n)").rearrange("p (c d) -> p c d", c=CPS)
            kz = sb1("kz", [128, CPS, 512], BF16)
            v8 = sb1("v8", [128, CPS, D], BF16)
            R32 = sb1("R32", [128, 4, 256])
            Rbf = [sb1("Rbf%d" % i, [128, 4, 256], BF16) for i in range(2)]
            wbuf = [sb1("wbuf%d" % i, [128, 8, 512], BF16) for i in range(NW)]
            rtile("ss", [128, 8], F32, 2, p1)
            rtile("st6", [128, 4, 6], F32, 3, p1)
            rtile("mv", [128, 4, 2], F32, 3, p1)
            rtile("st12", [128, 2, 6], F32, 3, p1)
            rtile("mv2", [128, 2], F32, 3, p1)
            rtile("sm", [128, 16], F32, 4, p1)
            rtile("xst", [128, D], F32, 4, p1)
            rtile("xn", [128, D], BF16, 2, p1)
            rtile("tg", [128, 512], F32, 2, p1)
            rtile("krot", [128, 512], F32, 1, p1)
            rtile("ra", [128, 256], F32, 1, p1)
            rtile("rb", [128, 256], F32, 1, p1)
            rtile("tm512", [128, 512], BF16, 2, p1)
            rtile("PT", [128, 512], BF16, 2, p1)
            rtile("on", [128, D], F32, 2, p1)
            rtile("rbc", [128, D], BF16, 3, p1)
            rtile("tm1024", [128, D], BF16, 4, p1)
            rtile("gsv", [128, D], F32, 1, p1)
            rtile("m1", [128, 512], F32, 1, p1)
            rtile("m2", [128, 512], F32, 1, p1)
            rtile("x1t", [128, 512], F32, 1, p1)
            rtile("cs", [128, 2, CPS, 64], F32, 2, p1)

            on0 = rot["on"][0][0]
            gsv0 = rot["gsv"][0][0]
            XIF = on0[:, 0:512].rearrange("p (h n) -> p h n", h=4)
            XIB = on0[:, 512:1024].rearrange("p (h n) -> p h n", h=4)
            wspf = gsv0[:, 0:512].rearrange("p (h n) -> p h n", h=4)
            wspb = gsv0[:, 512:768].bitcast(BF16).rearrange("p (h n) -> p h n", h=4)
            P.dma('sp', lambda e: e.dma_start(out=lgt[:], in_=logits[0:1, :].to_broadcast([128, 8])), writes=['lgt'])
            P.dma('sp', lambda e: e.dma_start(out=cv[:], in_=cvec[:, :]), writes=['cv'])
            P.dma('sp', lambda e: e.dma_start(out=dcs[:], in_=dconst[:, :, :]), writes=['dcs'])
            P.dma('sp', lambda e: e.dma_start(out=lng[:], in_=rows[2:3, :].to_broadcast([128, D])), writes=['lng'])
            P.dma('sp', lambda e: e.dma_start(out=lnb[:], in_=rows[3:4, :].to_broadcast([128, D])), writes=['lnb'])
            P.dma('sp', lambda e: e.dma_start(out=gpost[:], in_=rows[0:1, :].to_broadcast([128, D])), writes=['gpost'])
            P.dma('sp', lambda e: e.dma_start(out=bsp[:], in_=bspT[:, :]), writes=['bsp'])
            P.dma('sp', lambda e: e.dma_start(out=wspf[:], in_=wsp.rearrange("g p q -> p g q")), writes=['wspf'])

            P.op('act', lambda e: e.activation(out=lg[:], in_=lgt[:], func=AF.Exp, scale=-1.0), ['lgt'], ['lg'])
            P.op('dve', lambda e: e.tensor_scalar_add(out=lg[:], in0=lg[:], scalar1=1.0), ['lg'], ['lg'])
            P.op('act', lambda e: e.activation(out=lg[:], in_=lg[:], func=AF.Ln), ['lg'], ['lg'])
            P.op('dve', lambda e: e.tensor_scalar_mul(out=lg[:], in0=lg[:], scalar1=-1.0), ['lg'], ['lg'])
            P.op('act', lambda e: e.activation(out=dc[:], in_=lg[:], func=AF.Exp, scale=128.0), ['lg'], ['dc'])
            for h in range(4):
                P.op('act', lambda e, h=h: e.activation(out=zf[:, h:h + 1], in_=cv[:, 0:1], func=AF.Exp, scale=lg[:, h:h + 1]), ['lg', 'cv'], ['zf'])
                P.op('act', lambda e, h=h: e.activation(out=zb[:, h:h + 1], in_=cv[:, 1:2], func=AF.Exp, scale=lg[:, 4 + h:5 + h]), ['lg', 'cv'], ['zb'])
                P.op('act', lambda e, h=h: e.activation(out=DT[:, h, :], in_=dcs[:, 0, :], func=AF.Exp, scale=lg[:, h:h + 1]), ['lg', 'dcs'], ['DT'])
                P.op('dve', lambda e, h=h: e.tensor_tensor(out=DT[:, h, :], in0=DT[:, h, :], in1=dcs[:, 1, :], op=ALU.mult), ['DT', 'dcs'], ['DT'])
                P.op('act', lambda e, h=h: e.activation(out=XIF[:, h, :], in_=dcs[:, 2, :], func=AF.Exp, scale=lg[:, 4 + h:5 + h]), ['lg', 'dcs', 'DT'], ['XIF'])
                P.op('dve', lambda e, h=h: e.tensor_tensor(out=XIF[:, h, :], in0=XIF[:, h, :], in1=dcs[:, 3, :], op=ALU.mult), ['XIF', 'dcs'], ['XIF'])
                P.op('dve', lambda e, h=h: e.tensor_tensor(out=DT[:, h, :], in0=DT[:, h, :], in1=XIF[:, h, :], op=ALU.add), ['XIF', 'DT'], ['DT'])
            for h in range(4):
                P.op('act', lambda e, h=h: e.activation(out=XIF[:, h, :], in_=dcs[:, 4, :], func=AF.Exp, scale=lg[:, h:h + 1]), ['lg', 'dcs', 'DT'], ['XIF'])
                P.op('act', lambda e, h=h: e.activation(out=XIB[:, h, :], in_=dcs[:, 5, :], func=AF.Exp, scale=lg[:, 4 + h:5 + h]), ['lg', 'dcs'], ['XIB'])
            P.op('dve', lambda e: e.tensor_scalar_mul(out=XIF[:], in0=XIF[:], scalar1=SC), ['XIF'], ['XIF'])
            P.op('dve', lambda e: e.tensor_scalar_mul(out=XIB[:], in0=XIB[:], scalar1=SC), ['XIB'], ['XIB'])
            DX = dcs[:].rearrange("p a n -> p (a n)").bitcast(BF16)[:, 0:1024].rearrange("p (v h n) -> p v h n", v=2, h=4)
            P.op('dve', lambda e: e.tensor_tensor(out=DX[:, 0, :, :], in0=XIF[:], in1=identb[:].unsqueeze(1).to_broadcast([128, 4, 128]), op=ALU.mult),
                 ['XIF', 'identb', 'dcs', 'DT', 'XIB'], ['dcs', 'DX'])
            P.op('dve', lambda e: e.tensor_tensor(out=DX[:, 1, :, :], in0=XIB[:], in1=identb[:].unsqueeze(1).to_broadcast([128, 4, 128]), op=ALU.mult),
                 ['XIB', 'identb', 'dcs', 'DX'], ['dcs', 'DX'])
            P.op('dve', lambda e: e.tensor_copy(out=wspb[:], in_=wspf[:]), ['wspf'], ['wspb'])
            trt, trk = gettr()
            P.op('pe', lambda e: [e.transpose(trt[:, g * 128:(g + 1) * 128], wspb[:, g, :], identb[:]) for g in range(4)][-1],
                 ['wspb', 'identb'], [trk])
            P.op('dve', lambda e: e.tensor_copy(out=wspT[:].rearrange("p g q -> p (g q)"), in_=trt[:, 0:512]), [trk], ['wspT'])

            P.op('dve', lambda e: e.memset(on0[:, 0:1], 0.0), [], ['XIF', 'XIB'] + [(('on', 0), h) for h in range(4)])
            P.op('dve', lambda e: e.memset(gsv0[:, 0:1], 0.0), [], ['wspf', 'wspb'] + [(('gsv', 0), n) for n in range(2)])
            def rstd_a(src_ap, dst_ap, rkeys, wkey, scale):
                P.op('dve', lambda e: e.tensor_scalar(out=dst_ap, in0=src_ap, scalar1=scale, scalar2=EPS,
                                                      op0=ALU.mult, op1=ALU.add), rkeys, [wkey])

            def rstd_b(dst_ap, wkey):
                P.op('act', lambda e: e.activation(out=dst_ap, in_=dst_ap, func=AF.Sqrt), [wkey], [wkey])

            def rstd_c(dst_ap, wkey):
                P.op('dve', lambda e: e.reciprocal(out=dst_ap, in_=dst_ap), [wkey], [wkey])

            def rstd_from(src_ap, dst_ap, n, rkeys, wkey, scale):
                rstd_a(src_ap, dst_ap, rkeys, wkey, scale)
                rstd_b(dst_ap, wkey)
                rstd_c(dst_ap, wkey)

            def load_cs(s):
                cst, csk = nxt("cs")
                P.dma('sp', lambda e: e.dma_start(out=cst[:, 0, :, :], in_=cos_t[s * T:(s + 1) * T, :].rearrange("(c p) j -> p c j", p=128)), writes=[csk])
                P.dma('sp', lambda e: e.dma_start(out=cst[:, 1, :, :], in_=sin_t[s * T:(s + 1) * T, :].rearrange("(c p) j -> p c j", p=128)), reads=[], writes=[(csk, 's')])
                return cst, [csk, (csk, 's')]

            def norm_to_T(xt, xk, ss_col, gofs, dstT, dkey, col0, ncols=128, npart=128, rkey='rstd'):
                xnt, xnk = nxt("xn")
                P.op('act', lambda e: e.activation(out=xnt[0:npart, :], in_=xt[0:npart, :], func=AF.Identity, scale=ss_col),
                     [xk, rkey], [xnk])
                trt, trk = gettr()

                def tr(e):
                    for kb in range(8):
                        ins = e.transpose(trt[:, kb * ncols:(kb + 1) * ncols], xnt[0:npart, kb * 128:(kb + 1) * 128],
                                          identb[0:npart, 0:npart])
                    return ins
                P.op('pe', tr, [xnk, 'identb'], [trk])
                P.op('dve', lambda e: e.tensor_tensor(
                    out=dstT[:, :, col0:col0 + ncols],
                    in0=trt[:, 0:8 * ncols].rearrange("p (k n) -> p k n", k=8),
                    in1=gcol[:, gofs:gofs + 8].unsqueeze(2).to_broadcast([128, 8, ncols]), op=ALU.mult),
                    [trk, 'gcol'], [dkey])

            def norm_part1(s):
                xts = []
                ss, ssk = nxt("ss")
                for c in range(CPS):
                    xt, xk = nxt("xst")
                    r0 = s * T + c * 128
                    P.dma('sp', lambda e, xt=xt, r0=r0: e.dma_start(out=xt[:], in_=x[r0:r0 + 128, :]), writes=[xk])
                    jt, jk = nxt("junk")
                    P.op('act', lambda e, xt=xt, c=c, jt=jt: e.activation(out=jt[:], in_=xt[:], func=AF.Square, accum_out=ss[:, c:c + 1]),
                         [xk], [(ssk, c), jk])
                    xts.append((xt, xk))
                rstd_from(ss[:, 0:CPS], ss[:, 0:CPS], CPS, [(ssk, c) for c in range(CPS)], (ssk, 'rstd'), 1.0 / D)
                return (xts, ss, ssk)

            def norm_part2(st):
                xts, ss, ssk = st
                for c in range(CPS):
                    xt, xk = xts[c]
                    norm_to_T(xt, xk, ss[:, c:c + 1], 0, hT, ('hT', c), c * 128, rkey=(ssk, 'rstd'))

            wl_n = [0]

            def load_w(scr, g, key, nkb=8, rows_rearr="(kb p) c -> p kb c"):
                slot = wl_n[0] % NW
                wl_n[0] += 1
                wt = wbuf[slot]
                assert key in P.wr, key
                P.dma('sp', lambda e: e.dma_start(out=wt[:, 0:nkb, :], in_=scr[:, g * 512:(g + 1) * 512].rearrange(rows_rearr, p=128)),
                      reads=[key], writes=[('wb', slot)])
                return wt, ('wb', slot)

            def proj(c, wt, wk, srcT, skey):
                bt, bk = getbank()

                def mm(e):
                    for kb in range(8):
                        ins = e.matmul(bt[:], lhsT=srcT[:, kb, c * 128:(c + 1) * 128], rhs=wt[:, kb, :],
                                       start=(kb == 0), stop=(kb == 7))
                    return ins
                P.op('pe', mm, [wk] + (skey if isinstance(skey, list) else [skey]), [bk])
                return bt, bk

            def rotary(bt, bk, cst, cskeys, c, dst, dkey):
                b4 = bt[:].rearrange("p (h t j) -> p h t j", h=4, t=2)
                d4 = dst[:].rearrange("p (h t j) -> p h t j", h=4, t=2)
                cosb = cst[:, 0, c, :].unsqueeze(1).to_broadcast([128, 4, 64])
                sinb = cst[:, 1, c, :].unsqueeze(1).to_broadcast([128, 4, 64])
                ra, rak = nxt("ra")
                rb_, rbk = nxt("rb")
                ra3 = ra[:].rearrange("p (h j) -> p h j", h=4)
                rb3 = rb_[:].rearrange("p (h j) -> p h j", h=4)
                P.op('dve', lambda e: e.tensor_tensor(out=ra3, in0=b4[:, :, 0, :], in1=cosb, op=ALU.mult), [bk] + cskeys, [rak])
                P.op('dve', lambda e: e.tensor_tensor(out=rb3, in0=b4[:, :, 1, :], in1=sinb, op=ALU.mult), [bk] + cskeys, [rbk])
                P.op('dve', lambda e: e.tensor_tensor(out=d4[:, :, 0, :], in0=ra3, in1=rb3, op=ALU.subtract), [rak, rbk], [(dkey, 0)])
                P.op('dve', lambda e: e.tensor_tensor(out=ra3, in0=b4[:, :, 0, :], in1=sinb, op=ALU.mult), [bk] + cskeys, [rak])
                P.op('dve', lambda e: e.tensor_tensor(out=rb3, in0=b4[:, :, 1, :], in1=cosb, op=ALU.mult), [bk] + cskeys, [rbk])
                P.op('dve', lambda e: e.tensor_tensor(out=d4[:, :, 1, :], in0=ra3, in1=rb3, op=ALU.add), [rak, rbk], [(dkey, 1)])
                return [(dkey, 0), (dkey, 1)]

            def k_stage(c, wt, wk, cst, cskeys, zcol, kz_eng='pool'):
                bt, bk = proj(c, wt, wk, hT, ('hT', c))
                kr, krk = nxt("krot")
                keys = rotary(bt, bk, cst, cskeys, c, kr, krk)
                P.op(kz_eng, lambda e: e.tensor_tensor(
                    out=kz[:, c, :].rearrange("p (h d) -> p h d", h=4),
                    in0=kr[:].rearrange("p (h d) -> p h d", h=4),
                    in1=zcol.unsqueeze(2).to_broadcast([128, 4, 128]), op=ALU.mult), keys + ['zf', 'zb'], [('kz', c)])
                return kr, keys

            def v_stage(c, n, wt, wk):
                bt, bk = proj(c, wt, wk, hT, ('hT', c))
                P.op('act', lambda e: e.activation(out=v8[:, c, n * 512:(n + 1) * 512], in_=bt[:], func=AF.Identity), [bk], [('v8', c, n)])

            def kv_update(c, dcol0, rbf_dst, rbf_key):
                for hp in range(2):
                    bt, bk = getbank()

                    def mm(e, hp=hp, bt=bt):
                        for hh in range(2):
                            h = hp * 2 + hh
                            ins = e.matmul(bt[:, hh * 256:(hh + 1) * 256], lhsT=kz[:, c, h * 128:(h + 1) * 128],
                                           rhs=v8[:, c, h * 256:(h + 1) * 256], start=True, stop=True)
                        return ins
                    P.op('pe', mm, [('kz', c), ('v8', c, hp)], [bk])
                    for hh in range(2):
                        h = hp * 2 + hh
                        P.op('dve', lambda e, h=h, hh=hh, bt=bt: e.scalar_tensor_tensor(
                            out=R32[:, h, :], in0=R32[:, h, :], scalar=dc[:, dcol0 + h:dcol0 + h + 1],
                            in1=bt[:, hh * 256:(hh + 1) * 256], op0=ALU.mult, op1=ALU.add), [bk, ('R32', h), 'dc'], [('R32', h)])
                    P.op('act', lambda e, hp=hp: e.activation(out=rbf_dst[:, 2 * hp:2 * hp + 2, :], in_=R32[:, 2 * hp:2 * hp + 2, :], func=AF.Identity),
                         [('R32', 2 * hp), ('R32', 2 * hp + 1)], [(rbf_key, hp)])

            P.op('dve', lambda e: e.memset(R32[:], 0.0), [], [('R32', h) for h in range(4)])
            P.op('pool', lambda e: e.memset(Rbf[0][:], 0.0), [], [(('Rbf', 0), 0), (('Rbf', 0), 1)])
            rpar = 0
            if STOP != 'setup':
                def load_w_direct(g):
                    slot = wl_n[0] % NW
                    wl_n[0] += 1
                    wt = wbuf[slot]
                    P.dma('pool', lambda e: e.dma_start(out=wt[:], in_=w_in[:, g * 512:(g + 1) * 512].rearrange("(kb p) c -> p kb c", p=128)),
                          writes=[('wb', slot)])
                    return wt, ('wb', slot)
                wkt, wkk = load_w_direct(1)
                wv0, wv0k = load_w_direct(2)
                wv1, wv1k = load_w_direct(3)
                cs_of = {}
                cs_of[NST - 1] = load_cs(NST - 1)
                norm_part2(norm_part1(NST - 1))
                xts_next = None
                pend = None

                def state_step(i):
                    nonlocal rpar
                    cur = Rbf[rpar]
                    P.dma('sp', lambda e, cur=cur, i=i: e.dma_start(out=rb_scr[i], in_=cur[:].rearrange("p h e -> p (h e)")),
                          reads=[(('Rbf', rpar), 0), (('Rbf', rpar), 1)], writes=[('rb_scr', i)])
                    if i > 0:
                        rpar ^= 1
                        kv_update(i % CPS, 4, Rbf[rpar], ('Rbf', rpar))

                for i in range(NCHUNK - 1, -1, -1):
                    s_, c = divmod(i, CPS)
                    if c == CPS - 1:
                        emit_casts(1)
                        if s_ > 0:
                            cs_of[s_ - 1] = load_cs(s_ - 1)
                            xts_next = norm_part1(s_ - 1)
                    cst, cskeys = cs_of[s_]
                    kr, kkeys = k_stage(c, wkt, wkk, cst, cskeys, zb[:, :], kz_eng='dve')
                    kt, ktk = nxt("tm512")
                    P.op('act', lambda e, kt=kt, kr=kr: e.activation(out=kt[:], in_=kr[:], func=AF.Identity), kkeys, [ktk])
                    P.dma('sp', lambda e, kt=kt, i=i: e.dma_start(out=k_scr[i], in_=kt[:]), reads=[ktk], writes=[('k_scr', i)])
                    v_stage(c, 0, wv0, wv0k)
                    v_stage(c, 1, wv1, wv1k)
                    P.dma('sp', lambda e, c=c, i=i: e.dma_start(out=v_scr[i], in_=v8[:, c, :]),
                          reads=[('v8', c, 0), ('v8', c, 1)], writes=[('v_scr', i)])
                    if c == 0 and s_ > 0:
                        norm_part2(xts_next)
                    if pend is not None:
                        state_step(pend)
                    pend = i
                state_step(pend)

            P.op('dve', lambda e: e.memset(R32[:], 0.0), [('R32', h) for h in range(4)], [('R32', h) for h in range(4)])
            P.op('pool', lambda e: e.memset(Rbf[0][:], 0.0), [], [(('Rbf', 0), 0), (('Rbf', 0), 1)])
            fpar = 0
            RUN1 = STOP not in ('setup', 'p0')
            gl1 = [(w_in_b, 4, 'w_in_b'), (w_in_b, 5, 'w_in_b'),
                   (w_in_b, 0, 'w_in_b'),
                   (w_in_b, 6, 'w_in_b'), (w_in_b, 7, 'w_in_b'),
                   (w_in_b, 8, 'w_in_b'), (w_in_b, 9, 'w_in_b'),
                   (w_in_b, 10, 'w_in_b'), (w_in_b, 11, 'w_in_b'),
                   (w_in_b, 12, 'w_in_b'), (w_in_b, 13, 'w_in_b'),
                   (wro_b, 0, 'wro_b'), (wso_b, 0, 'wso_b'),
                   (wro_b, 1, 'wro_b'), (wso_b, 1, 'wso_b'),
                   (wout_b, 0, 'wout_b'), (wout_b, 1, 'wout_b')]
            NG = len(gl1)
            glist = gl1 * NST
            loaded = {}

            def ensure_cast(key):
                while key not in P.wr and pending:
                    emit_casts(1)

            def Wg(idx, la=3, cap=None):
                hi = idx + la + 1
                if cap is not None:
                    hi = min(hi, cap + 1)
                for j in range(min(hi, len(glist))):
                    if j not in loaded:
                        for jj in range(j, min(j + 5, len(glist))):
                            ensure_cast((glist[jj][2], glist[jj][1]))
                        scr, g, nm = glist[j]
                        loaded[j] = load_w(scr, g, (nm, g))
                return loaded[idx]

            def pipe(n, stages, lag=1, filler=None):
                st = {}
                for t in range(n + (len(stages) - 1) * lag):
                    for k, f in enumerate(stages):
                        idx = t - k * lag
                        if 0 <= idx < n:
                            st[idx] = f(idx, st.get(idx))
                    if filler is not None:
                        filler(t)

            if RUN1:
                cs_cur = load_cs(0)
                norm_part2(norm_part1(0))
            for s in (range(NST) if RUN1 else []):
                W = lambda idx, cap=None, s=s: Wg(s * NG + idx, cap=(None if cap is None else s * NG + cap))
                cst, cskeys = cs_cur

                for n in range(2):
                    wt, wk = W(n)
                    for c in range(CPS):
                        bt, bk = proj(c, wt, wk, hT, ('hT', c))
                        tg, tgk = nxt("tg")
                        P.op('act', lambda e, tg=tg, bt=bt: e.activation(out=tg[:], in_=bt[:], func=AF.Tanh, scale=0.5), [bk], [tgk])
                        P.op('dve', lambda e, tg=tg, bt=bt, c=c, n=n: e.scalar_tensor_tensor(
                            out=gate8[:, c, n * 512:(n + 1) * 512], in0=tg[:], scalar=1.0, in1=bt[:],
                            op0=ALU.add, op1=ALU.mult), [bk, tgk], [('gate8', c, n)])

                wq, wqk = W(2)
                P.dma('sp', lambda e, s=s: e.dma_start(out=v8[:], in_=v_scr[s * CPS:(s + 1) * CPS].rearrange("c p e -> p c e")),
                      reads=[('v_scr', s * CPS + c) for c in range(CPS)], writes=[('v8', c, n) for c in range(CPS) for n in range(2)])

                def qk_A(idx, _):
                    which, c = divmod(idx, CPS)
                    if which == 0:
                        bt, bk = proj(c, wq, wqk, hT, ('hT', c))
                        kr, krk = nxt("krot")
                        keys = rotary(bt, bk, cst, cskeys, c, kr, krk)
                        qt, qtk = nxt("tm512")
                        P.op('act', lambda e, qt=qt, kr=kr: e.activation(out=qt[:], in_=kr[:], func=AF.Identity), keys, [qtk])
                    else:
                        i = s * CPS + c
                        qt, qtk = nxt("tm512")
                        P.dma('sp', lambda e, qt=qt, i=i: e.dma_start(out=qt[:], in_=k_scr[i]), reads=[('k_scr', i)], writes=[qtk])
                        P.op('pool', lambda e, qt=qt, c=c: e.tensor_tensor(
                            out=kz[:, c, :].rearrange("p (h d) -> p h d", h=4),
                            in0=qt[:].rearrange("p (h d) -> p h d", h=4),
                            in1=zf[:, :].unsqueeze(2).to_broadcast([128, 4, 128]), op=ALU.mult), [qtk, 'zf'], [('kz', c)])
                    return (qt, qtk)

                def qk_B(idx, stt):
                    which, c = divmod(idx, CPS)
                    qt, qtk = stt
                    if which == 0:
                        dsts = [(qkT[:, 0, c, :, :], ('qT', c), None), (qx[:, 0, c, :, :], ('qxf', c), 0), (qx[:, 1, c, :, :], ('qxb', c), 1)]
                        for k, (dst, dkey, v) in enumerate(dsts):
                            bt, bk = getbank()

                            def mmq(e, bt=bt, v=v):
                                for h in range(4):
                                    rhs = identb[:] if v is None else DX[:, v, h, :]
                                    ins = e.matmul(bt[:, h * 128:(h + 1) * 128], lhsT=qt[:, h * 128:(h + 1) * 128], rhs=rhs, start=True, stop=True)
                                return ins
                            P.op('pe', mmq, [qtk, 'identb', 'DX'], [bk])
                            eng = 'dve' if k == 2 else 'act'
                            if eng == 'act':
                                P.op('act', lambda e, bt=bt, dst=dst: e.activation(out=dst, in_=bt[:].rearrange("p (h n) -> p h n", h=4), func=AF.Identity),
                                     [bk], [dkey])
                            else:
                                P.op('dve', lambda e, bt=bt, dst=dst: e.tensor_copy(out=dst, in_=bt[:].rearrange("p (h n) -> p h n", h=4)), [bk], [dkey])
                    else:
                        trt, trk = gettr()
                        P.op('pe', lambda e, trt=trt, qt=qt: [e.transpose(trt[:, h * 128:(h + 1) * 128], qt[:, h * 128:(h + 1) * 128], identb[:])
                                                              for h in range(4)][-1], [qtk, 'identb'], [trk])
                        tr3 = trt[:, 0:512].rearrange("p (h n) -> p h n", h=4)
                        P.op('act', lambda e, tr3=tr3, c=c: e.activation(out=qkT[:, 1, c, :, :], in_=tr3, func=AF.Identity), [trk], [('kT', c)])
                    return stt
                pipe(2 * CPS, [qk_A, qk_B])
                W(3)
                emit_casts(2)

                def load_rb(c):
                    i = s * CPS + c
                    rt, rk = nxt("rbc")
                    P.dma('sp', lambda e, rt=rt, i=i: e.dma_start(out=rt[:], in_=rb_scr[i]), reads=[('rb_scr', i)], writes=[rk])
                    return rt, rk
                rbs = {0: load_rb(0), 1: load_rb(1)}

                def ret_A(c, _):
                    nonlocal fpar
                    i = s * CPS + c
                    if c + 2 < CPS:
                        rbs[c + 2] = load_rb(c + 2)
                    rt, rk = rbs[c]
                    rcur = Rbf[fpar]
                    rkeys = [(('Rbf', fpar), 0), (('Rbf', fpar), 1)]
                    if i < NCHUNK - 1:
                        fpar ^= 1
                        kv_update(c, 0, Rbf[fpar], ('Rbf', fpar))
                    stb, stk = getbank()
                    P.op('pe', lambda e, stb=stb, c=c: [e.matmul(stb[:, h * 128:(h + 1) * 128], lhsT=qkT[:, 1, c, h, :], rhs=qkT[:, 0, c, h, :],
                                                                 start=True, stop=True) for h in range(4)][-1],
                         [('qT', c), ('kT', c)], [stk])
                    pt, ptk = nxt("PT")
                    P.op('dve', lambda e, pt=pt, stb=stb: e.tensor_tensor(out=pt[:], in0=stb[:], in1=DT[:].rearrange("p h n -> p (h n)"), op=ALU.mult),
                         [stk, 'DT'], [ptk])
                    obanks = []
                    for hp in range(2):
                        ob, obk = getbank(pin=True)

                        def omm(e, hp=hp, ob=ob, pt=pt, rcur=rcur, c=c, rt=rt):
                            for hh in range(2):
                                h = hp * 2 + hh
                                o_ = ob[:, hh * 256:(hh + 1) * 256]
                                e.matmul(o_, lhsT=pt[:, h * 128:(h + 1) * 128], rhs=v8[:, c, h * 256:(h + 1) * 256], start=True, stop=False)
                                e.matmul(o_, lhsT=qx[:, 0, c, h, :], rhs=rcur[:, h, :], start=False, stop=False)
                                ins = e.matmul(o_, lhsT=qx[:, 1, c, h, :], rhs=rt[:, h * 256:(h + 1) * 256], start=False, stop=True)
                            return ins
                        P.op('pe', omm, [ptk, ('v8', c, hp), ('qxf', c), ('qxb', c), rk] + rkeys, [obk])
                        obanks.append((ob, obk))
                    return obanks

                def ret_B1(c, obanks):
                    sm, smk = nxt("sm")
                    mv, mvk = nxt("mv")
                    st6, s6k = nxt("st6")
                    for h in range(4):
                        ob, obk = obanks[h // 2]
                        osl = ob[:, (h % 2) * 256:(h % 2 + 1) * 256]
                        P.op('dve', lambda e, h=h, osl=osl: e.bn_stats(out=st6[:, h, :], in_=osl), [obk], [(s6k, h)])
                        P.op('dve', lambda e, h=h: e.bn_aggr(out=mv[:, h, :], in_=st6[:, h, :]), [(s6k, h)], [(mvk, h)])
                    rstd_a(mv[:, :, 1], sm[:, 0:4], [(mvk, h) for h in range(4)], (smk, 'rs4'), 1.0)
                    rstd_b(sm[:, 0:4], (smk, 'rs4'))
                    return (obanks, sm, smk, mv, mvk)

                def ret_B2(c, stt):
                    obanks, sm, smk, mv, mvk = stt
                    rstd_c(sm[:, 0:4], (smk, 'rs4'))
                    P.op('dve', lambda e: e.scalar_tensor_tensor(out=sm[:, 4:8], in0=mv[:, :, 0], scalar=-1.0, in1=sm[:, 0:4],
                                                                 op0=ALU.mult, op1=ALU.mult), [(smk, 'rs4')] + [(mvk, h) for h in range(4)], [(smk, 'nmr')])
                    on, onk = nxt("on")
                    for h in range(4):
                        ob, obk = obanks[h // 2]
                        osl = ob[:, (h % 2) * 256:(h % 2 + 1) * 256]
                        P.op('act', lambda e, h=h, osl=osl, on=on: e.activation(out=on[:, h * 256:(h + 1) * 256], in_=osl, func=AF.Identity,
                                                                                bias=sm[:, 4 + h:5 + h], scale=sm[:, h:h + 1]),
                             [obk, (smk, 'rs4'), (smk, 'nmr')], [(onk, h)])
                    yt, ytk = nxt("tm1024")
                    P.op('dve', lambda e, yt=yt, on=on, c=c: e.tensor_tensor(out=yt[:], in0=on[:], in1=gate8[:, c, :], op=ALU.mult),
                         [(onk, h) for h in range(4)] + [('gate8', c, 0), ('gate8', c, 1)], [ytk])
                    for ob, obk in obanks:
                        pinned.discard(obk[1])
                    return (yt, ytk)

                def ret_C(c, stt):
                    yt, ytk = stt
                    trt, trk = gettr()
                    P.op('pe', lambda e, trt=trt, yt=yt: [e.transpose(trt[:, kb * 128:(kb + 1) * 128], yt[:, kb * 128:(kb + 1) * 128], identb[:])
                                                          for kb in range(8)][-1], [ytk, 'identb'], [trk])
                    P.op('act', lambda e, trt=trt, c=c: e.activation(out=yrT[:, :, c * 128:(c + 1) * 128],
                                                                     in_=trt[:].rearrange("p (k n) -> p k n", k=8), func=AF.Identity, scale=0.5),
                         [trk], [('yrT', c)])
                    return stt
                def u_item(c, n):
                    wt, wk = W(3 + n, cap=5)
                    bt, bk = proj(c, wt, wk, hT, ('hT', c))
                    P.op('act', lambda e, bt=bt, c=c, n=n: e.activation(out=gate8[:, c, n * 512:(n + 1) * 512], in_=bt[:], func=AF.Gelu_apprx_tanh),
                         [bk], [('gate8', c, n)])

                def u_fill(t):
                    c = t - 1
                    if 0 <= c < CPS:
                        u_item(c, 0)
                        u_item(c, 1)
                rst = {}
                for t in range(CPS + 2):
                    if 0 <= t - 1 < CPS:
                        rst[t - 1] = ret_B1(t - 1, rst[t - 1])
                    if t < CPS:
                        rst[t] = ret_A(t, None)
                    if 0 <= t - 1 < CPS:
                        rst[t - 1] = ret_B2(t - 1, rst[t - 1])
                    if 0 <= t - 2 < CPS:
                        ret_C(t - 2, rst[t - 2])
                    u_fill(t)

                w0, w0k = W(5)
                w1, w1k = W(6)
                qkkeys = [('qT', c) for c in range(CPS)] + [('kT', c) for c in range(CPS)]

                def sv_A1(c):
                    sm, smk = nxt("sm")
                    mv2, mv2k = nxt("mv2")
                    st12, s12k = nxt("st12")
                    gs, gsk = nxt("gsv")
                    for n, (wt, wk) in enumerate(((w0, w0k), (w1, w1k))):
                        bt, bk = proj(c, wt, wk, hT, ('hT', c))
                        P.op('act', lambda e, bt=bt, gs=gs, n=n: e.activation(out=gs[:, n * 512:(n + 1) * 512], in_=bt[:], func=AF.Gelu_apprx_tanh),
                             [bk], [(gsk, n)])
                        P.op('dve', lambda e, gs=gs, n=n: e.bn_stats(out=st12[:, n, :], in_=gs[:, n * 512:(n + 1) * 512]), [(gsk, n)], [(s12k, n)])
                    P.op('dve', lambda e: e.bn_aggr(out=mv2[:], in_=st12[:].rearrange("p a b -> p (a b)")), [(s12k, 0), (s12k, 1)], [mv2k])
                    rstd_a(mv2[:, 1:2], sm[:, 8:9], [mv2k], (smk, 'rs1'), 1.0)
                    return (sm, smk, mv2, mv2k, gs, gsk)

                def sv_A2(st):
                    sm, smk, mv2, mv2k, gs, gsk = st
                    rstd_b(sm[:, 8:9], (smk, 'rs1'))

                def sv_A3(st):
                    sm, smk, mv2, mv2k, gs, gsk = st
                    rstd_c(sm[:, 8:9], (smk, 'rs1'))
                    P.op('dve', lambda e, gs=gs: e.tensor_scalar(out=gs[:], in0=gs[:], scalar1=mv2[:, 0:1], scalar2=sm[:, 8:9],
                                                                 op0=ALU.subtract, op1=ALU.mult), [(gsk, 0), (gsk, 1), mv2k, (smk, 'rs1')], [(gsk, 0), (gsk, 1)])
                    P.op('dve', lambda e, gs=gs: e.tensor_tensor(out=gs[:], in0=gs[:], in1=lng[:], op=ALU.mult), [(gsk, 0), (gsk, 1), 'lng'], [(gsk, 0), (gsk, 1)])
                    sv, svk = nxt("tm1024")
                    P.op('dve', lambda e, gs=gs, sv=sv: e.tensor_tensor(out=sv[:], in0=gs[:], in1=lnb[:], op=ALU.add), [(gsk, 0), (gsk, 1), 'lnb'], [svk])
                    return (sv, svk)

                def sv_B(c, stt):
                    sv, svk = stt
                    ys, ysk = nxt("tm1024")
                    for gp in range(2):
                        bt, bk = getbank()
                        P.op('pe', lambda e, bt=bt, sv=sv, gp=gp: [e.matmul(bt[:, gg * 256:(gg + 1) * 256], lhsT=wspT[:, gp * 2 + gg, :],
                                                                          rhs=sv[:, (gp * 2 + gg) * 256:(gp * 2 + gg + 1) * 256], start=True, stop=True)
                                                                 for gg in range(2)][-1], [svk, 'wspT'], [bk])
                        for gg in range(2):
                            g = gp * 2 + gg
                            P.op('dve', lambda e, bt=bt, ys=ys, g=g, gg=gg, c=c: e.scalar_tensor_tensor(
                                out=ys[:, g * 256:(g + 1) * 256], in0=bt[:, gg * 256:(gg + 1) * 256], scalar=bsp[:, g:g + 1],
                                in1=gate8[:, c, g * 256:(g + 1) * 256], op0=ALU.add, op1=ALU.mult),
                                [bk, 'bsp', ('gate8', c, g // 2)], [(ysk, g)])
                    return (ys, ysk)

                def sv_C(c, stt):
                    ys, ysk = stt
                    trt, trk = gettr()
                    P.op('pe', lambda e, trt=trt, ys=ys: [e.transpose(trt[:, kb * 128:(kb + 1) * 128], ys[:, kb * 128:(kb + 1) * 128], identb[:])
                                                          for kb in range(8)][-1], [(ysk, g) for g in range(4)] + ['identb'], [trk])
                    P.op('act', lambda e, trt=trt, c=c: e.activation(out=ysT[:, :, c * 128:(c + 1) * 128],
                                                                     in_=trt[:].rearrange("p (k n) -> p k n", k=8), func=AF.Identity),
                         [trk] + qkkeys, [('ysT', c)] + qkkeys)
                    return stt
                qxkeys = [('qxf', c) for c in range(CPS)] + [('qxb', c) for c in range(CPS)]

                def gate_item(g, c):
                    wt, wk = W(7 + g, cap=9 if g < 3 else None)
                    bt, bk = proj(c, wt, wk, hT, ('hT', c))
                    if g < 2:
                        P.op('act', lambda e, bt=bt: e.activation(out=v8[:, c, g * 512:(g + 1) * 512], in_=bt[:], func=AF.Tanh, scale=0.5),
                             [bk], [('v8', c, g)])
                    else:
                        n = g - 2
                        P.op('act', lambda e, bt=bt: e.activation(out=tas[:, c, n * 512:(n + 1) * 512], in_=bt[:], func=AF.Tanh, scale=0.5),
                             [bk] + qxkeys, [('tas', c, n)] + qxkeys)
                g_items = [(g, c) for g in range(3) for c in range(CPS)]

                def g_fill(t):
                    for _ in range(2):
                        if g_items:
                            gate_item(*g_items.pop(0))
                sst = {}
                for t in range(CPS + 2):
                    a1 = sv_A1(t) if t < CPS else None
                    if 0 <= t - 2 < CPS:
                        sv_C(t - 2, sst[t - 2])
                    g_fill(t)
                    if a1 is not None:
                        sv_A2(a1)
                    if 0 <= t - 1 < CPS:
                        sst[t - 1] = sv_B(t - 1, sst[t - 1])
                    if a1 is not None:
                        sst[t] = sv_A3(a1)
                while g_items:
                    gate_item(*g_items.pop(0))
                for c in range(CPS):
                    gate_item(3, c)

                xts_next = None
                if s + 1 < NST:
                    cs_cur = load_cs(s + 1)
                    xts_next = norm_part1(s + 1)

                for n in range(2):
                    wr, wrk = W(11 + 2 * n)
                    ws, wsk = W(12 + 2 * n)
                    for c in range(CPS):
                        br, brk = proj(c, wr, wrk, yrT, ('yrT', c))
                        bs_, bsk = proj(c, ws, wsk, ysT, [('ysT', c)] + qkkeys)
                        m1, m1k = nxt("m1")
                        m2, m2k = nxt("m2")
                        P.op('dve', lambda e, m1=m1, br=br, c=c, n=n: e.scalar_tensor_tensor(
                            out=m1[:], in0=v8[:, c, n * 512:(n + 1) * 512], scalar=1.0, in1=br[:], op0=ALU.add, op1=ALU.mult),
                            [brk, ('v8', c, n)], [m1k])
                        P.op('dve', lambda e, m2=m2, bs_=bs_, c=c, n=n: e.scalar_tensor_tensor(
                            out=m2[:], in0=tas[:, c, n * 512:(n + 1) * 512], scalar=1.0, in1=bs_[:], op0=ALU.add, op1=ALU.mult),
                            [bsk, ('tas', c, n)] + qxkeys, [m2k])
                        P.op('pool', lambda e, m1=m1, m2=m2, c=c, n=n: e.tensor_tensor(out=gate8[:, c, n * 512:(n + 1) * 512], in0=m1[:], in1=m2[:], op=ALU.add),
                             [m1k, m2k], [('gate8', c, n)])
                for c in range(CPS):
                    trt, trk = gettr()
                    P.op('pe', lambda e, trt=trt, c=c: [e.transpose(trt[:, kb * 128:(kb + 1) * 128], gate8[:, c, kb * 128:(kb + 1) * 128], identb[:])
                                                        for kb in range(8)][-1], [('gate8', c, 0), ('gate8', c, 1), 'identb'], [trk])
                    P.op('act', lambda e, trt=trt, c=c: e.activation(out=mT[:, :, c * 128:(c + 1) * 128],
                                                                     in_=trt[:].rearrange("p (k n) -> p k n", k=8), func=AF.Identity, scale=0.5),
                         [trk], [('mT', c)])

                if xts_next is not None:
                    norm_part2(xts_next)

                wo0, wo0k = W(15)
                wo1, wo1k = W(16)
                def wo_X1(c):
                    r0 = s * T + c * 128
                    xt, xk = nxt("on")
                    sm, smk = nxt("sm")
                    P.dma('sp', lambda e, xt=xt, r0=r0: e.dma_start(out=xt[:], in_=x[r0:r0 + 128, :]), writes=[(xk, h) for h in range(4)])
                    pbs = []
                    for n, (wt, wk) in enumerate(((wo0, wo0k), (wo1, wo1k))):
                        bt, bk = proj(c, wt, wk, mT, ('mT', c))
                        jt, jk = nxt("junk")
                        P.op('act', lambda e, bt=bt, n=n, jt=jt, sm=sm: e.activation(out=jt[:, 0:512], in_=bt[:], func=AF.Square, accum_out=sm[:, 10 + n:11 + n]),
                             [bk], [(smk, 'ssq', n), jk])
                        pbs.append((bt, bk))
                    P.op('dve', lambda e, sm=sm: e.tensor_tensor(out=sm[:, 12:13], in0=sm[:, 10:11], in1=sm[:, 11:12], op=ALU.add), [(smk, 'ssq', 0), (smk, 'ssq', 1)], [(smk, 'ssq2')])
                    rstd_a(sm[:, 12:13], sm[:, 13:14], [(smk, 'ssq2')], (smk, 'rsq'), 1.0 / D)
                    return (r0, xt, xk, sm, smk, pbs)

                def wo_X2(st):
                    r0, xt, xk, sm, smk, pbs = st
                    rstd_b(sm[:, 13:14], (smk, 'rsq'))

                def wo_X3(st):
                    r0, xt, xk, sm, smk, pbs = st
                    rstd_c(sm[:, 13:14], (smk, 'rsq'))
                    xkeys = [(xk, h) for h in range(4)]
                    for n in range(2):
                        bt, bk = pbs[n]
                        t1, t1k = nxt("x1t")
                        P.op('dve', lambda e, bt=bt, t1=t1, n=n, sm=sm: e.scalar_tensor_tensor(
                            out=t1[:], in0=bt[:], scalar=sm[:, 13:14], in1=gpost[:, n * 512:(n + 1) * 512], op0=ALU.mult, op1=ALU.mult),
                            [bk, (smk, 'rsq'), 'gpost'], [t1k])
                        P.op('pool', lambda e, t1=t1, xt=xt, n=n: e.tensor_tensor(out=xt[:, n * 512:(n + 1) * 512], in0=xt[:, n * 512:(n + 1) * 512], in1=t1[:], op=ALU.add),
                             [t1k] + xkeys, xkeys)
                    P.dma('pool', lambda e, xt=xt, r0=r0: e.dma_start(out=x1_scr[r0:r0 + 128, :], in_=xt[:]), reads=xkeys, writes=[('x1s', r0 // 128)])

                wst = {0: wo_X1(0)}
                for c in range(CPS):
                    if c + 1 < CPS:
                        wst[c + 1] = wo_X1(c + 1)
                    wo_X2(wst[c])
                    wo_X3(wst[c])

            emit_casts_until(0)
            P.barrier()
            _run_block(nc, P.take())

        with ExitStack() as p2:
            def sb2(name, shape, dt=F32):
                return sb(name, shape, dt, p2)

            WUP = sb2("WUP", [128, 8, 2 * DFF], BF16)
            WDN = sb2("WDN", [128, NJ, D], BF16)
            X1 = [sb2("X1_%d" % i, [128, 2, D]) for i in range(3)]
            HB = [sb2("HB_%d" % i, [128, 8, 258], BF16) for i in range(2)]
            NSP = 4
            actT = sb2("actT", [128, NJ + NSP, 256], BF16)

            def aslot(j, ja):
                return (j * NJ + ja) % (NJ + NSP)
            rtile("s3", [128, 16], F32, 4, p2)
            gpostffn = sb2("gpostffn", [128, D])
            cw = sb2("cw", [128, 2 * NJ, 3])
            cb = sb2("cb", [128, 2 * NJ])
            P.dma('sp', lambda e: e.dma_start(out=gpostffn[:], in_=rows[1:2, :].to_broadcast([128, D])), writes=['gpostffn'])
            P.dma('sp', lambda e: e.dma_start(out=cw[:], in_=cwl[:, :, :]), writes=['cw'])
            P.dma('sp', lambda e: e.dma_start(out=cb[:], in_=cbl[:, :]), writes=['cb'])
            rtile("xn", [128, D], BF16, 2, p2, "q_")
            rtile("acc", [128, 256], F32, 8, p2)
            rtile("ga", [128, 256], F32, 3, p2)
            rtile("yt", [128, 512], F32, 2, p2)

            NT2 = S // 256 if STOP == 'all' else 0

            def hbkeys(b):
                return [(('HB', b), 0), (('HB', b), 1), (('HB', b), 'L'), (('HB', b), 'R')]

            def front(j):
                r0 = j * 256
                x1t = X1[j % 3]
                x1k = ('X1', j % 3)
                hb = HB[j % 2]
                s3, s3k = nxt("s3")
                P.dma('sp', lambda e: e.dma_start(out=x1t[:], in_=x1_scr[r0:r0 + 256, :].rearrange("(c p) d -> p c d", p=128)),
                      reads=[('x1s', r0 // 128), ('x1s', r0 // 128 + 1)], writes=[(x1k, 0), (x1k, 1)])
                for c in range(2):
                    jt, jk = nxt("junk")
                    P.op('act', lambda e, c=c, jt=jt: e.activation(out=jt[:], in_=x1t[:, c, :], func=AF.Square, accum_out=s3[:, c:c + 1]),
                         [(x1k, c)], [(s3k, c), jk])
                rstd_from(s3[:, 0:2], s3[:, 4:6], 2, [(s3k, 0), (s3k, 1)], (s3k, 'rstd'), 1.0 / D)
                for c in range(2):
                    norm_to_T(x1t[:, c, :], (x1k, c), s3[:, 4 + c:5 + c], 8, hb, (('HB', j % 2), c), 1 + c * 128, rkey=(s3k, 'rstd'))

            def halo_copy(dst_b, dst_col, src_b, src_col, dkey, skey):
                P.op('dve', lambda e: e.tensor_copy(out=HB[dst_b][:, :, dst_col:dst_col + 1], in_=HB[src_b][:, :, src_col:src_col + 1]),
                     [skey], [dkey])

            def halo_zero(b, col, dkey):
                P.op('dve', lambda e: e.memset(HB[b][:, :, col:col + 1], 0.0), [], [dkey])

            def up(j, ja_list):
                hb = HB[j % 2]
                hkeys = hbkeys(j % 2)
                for ja in ja_list:
                    accs = []
                    bts = []
                    for half in range(2):
                        ch = half * NJ + ja
                        bt, bk = getbank()

                        def mm(e, bt=bt, ch=ch):
                            for kb in range(8):
                                ins = e.matmul(bt[:, 0:258], lhsT=WUP[:, kb, ch * 128:(ch + 1) * 128], rhs=hb[:, kb, :],
                                               start=(kb == 0), stop=(kb == 7))
                            return ins
                        P.op('pe', mm, hkeys + [('WUP', ch // 4)], [bk])
                        ac, ack = nxt("acc")
                        P.op('act', lambda e, ac=ac, bt=bt, ch=ch: e.activation(out=ac[:], in_=bt[:, 0:256], func=AF.Identity,
                                                                               bias=cb[:, ch:ch + 1], scale=cw[:, ch, 0:1]), [bk, 'cw', 'cb'], [ack])
                        accs.append((ac, ack))
                        bts.append((bt, bk, ch))
                    if pend_gate[0] is not None:
                        pend_gate[0]()
                        pend_gate[0] = None
                    for tap in (1, 2):
                        for half in range(2):
                            ac, ack = accs[half]
                            bt, bk, ch = bts[half]
                            P.op('dve', lambda e, ac=ac, bt=bt, ch=ch, tap=tap: e.scalar_tensor_tensor(
                                out=ac[:], in0=bt[:, tap:tap + 256], scalar=cw[:, ch, tap:tap + 1], in1=ac[:],
                                op0=ALU.mult, op1=ALU.add), [bk, ack, 'cw'], [ack])
                    pend_gate[0] = (lambda accs=accs, sl=aslot(j, ja): gate(accs, sl))

            pend_gate = [None]

            def gate(accs, sl):
                ga, gak = nxt("ga")
                P.op('act', lambda e, ga=ga, a=accs[0][0]: e.activation(out=ga[:], in_=a[:], func=AF.Gelu_apprx_tanh), [accs[0][1]], [gak])
                P.op('pool', lambda e, ga=ga, b=accs[1][0], sl=sl: e.tensor_tensor(out=actT[:, sl, :], in0=ga[:], in1=b[:], op=ALU.mult),
                     [gak, accs[1][1]], [('actT', sl)])

            def flush_gate():
                if pend_gate[0] is not None:
                    pend_gate[0]()
                    pend_gate[0] = None

            def down(j):
                r0 = j * 256
                x1t = X1[j % 3]
                x1k = ('X1', j % 3)
                for c in range(2):
                    pbs = []
                    s3, s3k = nxt("s3")
                    for n in range(2):
                        bt, bk = getbank()

                        def mmd(e, bt=bt, c=c, n=n):
                            for jj in range(NJ):
                                ins = e.matmul(bt[:], lhsT=actT[:, aslot(j, jj), c * 128:(c + 1) * 128], rhs=WDN[:, jj, n * 512:(n + 1) * 512],
                                               start=(jj == 0), stop=(jj == NJ - 1))
                            return ins
                        P.op('pe', mmd, [('actT', aslot(j, jj)) for jj in range(NJ)] + [('WDN', n)], [bk])
                        jt, jk = nxt("junk")
                        P.op('act', lambda e, bt=bt, n=n, jt=jt, s3=s3: e.activation(out=jt[:, 0:512], in_=bt[:], func=AF.Square, accum_out=s3[:, 8 + n:9 + n]),
                             [bk], [(s3k, 'q', n), jk])
                        pbs.append((bt, bk))
                    P.op('dve', lambda e, s3=s3: e.tensor_tensor(out=s3[:, 10:11], in0=s3[:, 8:9], in1=s3[:, 9:10], op=ALU.add), [(s3k, 'q', 0), (s3k, 'q', 1)], [(s3k, 'q2')])
                    rstd_from(s3[:, 10:11], s3[:, 11:12], 1, [(s3k, 'q2')], (s3k, 'rsq'), 1.0 / D)
                    for n in range(2):
                        bt, bk = pbs[n]
                        yt, ytk = nxt("yt")
                        P.op('dve', lambda e, bt=bt, yt=yt, n=n, s3=s3: e.scalar_tensor_tensor(
                            out=yt[:], in0=bt[:], scalar=s3[:, 11:12], in1=gpostffn[:, n * 512:(n + 1) * 512], op0=ALU.mult, op1=ALU.mult),
                            [bk, (s3k, 'rsq'), 'gpostffn'], [ytk])
                        P.op('pool', lambda e, yt=yt, c=c, n=n: e.tensor_tensor(out=x1t[:, c, n * 512:(n + 1) * 512],
                                                                                 in0=x1t[:, c, n * 512:(n + 1) * 512], in1=yt[:], op=ALU.add),
                             [ytk, (x1k, c)], [(x1k, c)])
                    rr = r0 + c * 128
                    P.dma('pool', lambda e, c=c, rr=rr: e.dma_start(out=out[rr:rr + 128, :], in_=x1t[:, c, :]),
                          reads=[(x1k, c)], writes=[('out', rr // 128)])

            def load_wup(g):
                P.dma('sp', lambda e: e.dma_start(out=WUP[:, :, g * 512:(g + 1) * 512],
                                                  in_=wup_b[:, g * 512:(g + 1) * 512].rearrange("(kb p) c -> p kb c", p=128)),
                      reads=[('wup_b', g)], writes=[('WUP', g)])

            def load_wdn(g):
                P.dma('sp', lambda e: e.dma_start(out=WDN[:, :, g * 512:(g + 1) * 512],
                                                  in_=wdn_b[:, g * 512:(g + 1) * 512].rearrange("(j p) c -> p j c", p=128)),
                      reads=[('wdn_b', g)], writes=[('WDN', g)])

            load_wup(0)
            load_wup(5)
            if NT2:
                front(0)
                halo_zero(0, 0, (('HB', 0), 'L'))
                front(1)
                halo_copy(0, 257, 1, 1, (('HB', 0), 'R'), (('HB', 1), 0))
            for g in [1, 6, 2, 7, 3, 8, 4, 9, 10]:
                load_wup(g)
            load_wdn(0)
            load_wdn(1)
            for j in range(NT2):
                up(j, range(NSP if j > 0 else 0, NJ))
                b, nb = j % 2, (j + 1) % 2
                if j + 1 < NT2:
                    halo_copy(nb, 0, b, 256, (('HB', nb), 'L'), (('HB', b), 1))
                if j + 2 < NT2:
                    front(j + 2)
                    halo_copy(nb, 257, b, 1, (('HB', nb), 'R'), (('HB', b), 0))
                elif j + 1 < NT2:
                    halo_zero(nb, 257, (('HB', nb), 'R'))
                if j + 1 < NT2:
                    up(j + 1, range(0, NSP))
                else:
                    flush_gate()
                down(j)
            P.barrier()
            _run_block(nc, P.take())
    return nc


_NC_CACHE = {}


def _consts():
    half = 64
    inv_freq = (1.0 / (np.float32(10000.0) ** (np.arange(half, dtype=np.float32) / np.float32(half)))).astype(np.float32)
    ang = (np.arange(S, dtype=np.float32)[:, None] * inv_freq[None, :]).astype(np.float32)
    cos_t = np.cos(ang).astype(np.float32)
    sin_t = np.sin(ang).astype(np.float32)
    idx = np.arange(128, dtype=np.float32)
    m = idx[:, None]
    n = idx[None, :]
    Ef = np.maximum(n - m, 0.0)
    Mf = (n >= m).astype(np.float32) * np.float32(SC)
    Eb = np.maximum(m - n, 0.0)
    Mb = (m > n).astype(np.float32) * np.float32(SC)
    N1 = np.broadcast_to(n + 1.0, (128, 128))
    N128 = np.broadcast_to(128.0 - n, (128, 128))
    dconst = np.ascontiguousarray(np.stack([Ef, Mf, Eb, Mb, N1, N128], axis=1).astype(np.float32))
    cvec = np.ascontiguousarray(np.stack([127.0 - idx, idx], axis=1).astype(np.float32))
    ident = np.eye(128, dtype=np.float32)
    return cos_t, sin_t, dconst, cvec, ident


def kernel(x, g_pre_mix, w_in, ret_decay_logit, sgu_ln_g, sgu_ln_b, w_spatial, b_spatial,
           w_ret_o, w_sgu_o, w_out, g_post_mix, g_pre_ffn, w_up, conv_w, conv_b, w_down, g_post_ffn):
    f = lambda a: np.ascontiguousarray(np.asarray(a, dtype=np.float32))
    x = f(x)
    cos_t, sin_t, dconst, cvec, ident = _consts()
    gcols = np.ascontiguousarray(np.concatenate([f(g_pre_mix).reshape(8, 128).T, f(g_pre_ffn).reshape(8, 128).T], axis=1))
    rows = np.ascontiguousarray(np.stack([f(g_post_mix), f(g_post_ffn), f(sgu_ln_g), f(sgu_ln_b)], axis=0))
    logits = f(ret_decay_logit).reshape(1, 8)
    bspT = np.ascontiguousarray(f(b_spatial).T)
    cwl = np.ascontiguousarray(f(conv_w).reshape(3, 2 * NJ, 128).transpose(2, 1, 0))
    cbl = np.ascontiguousarray(f(conv_b).reshape(2 * NJ, 128).T)
    if 'nc' not in _NC_CACHE:
        _NC_CACHE['nc'] = build_nc()
    nc = _NC_CACHE['nc']
    shared = dict(w_in=f(w_in), w_ret_o=f(w_ret_o), w_sgu_o=f(w_sgu_o), w_out=f(w_out), w_up=f(w_up), w_down=f(w_down),
                  gcols=gcols, rows=rows, logits=logits, wsp=f(w_spatial), bspT=bspT, cwl=cwl, cbl=cbl, ident=ident,
                  cos_t=cos_t, sin_t=sin_t, dconst=dconst, cvec=cvec)
    in_maps = [dict(shared, x=x[b]) for b in range(8)]
    res = run_bass_kernel_spmd(nc, in_maps, core_ids=list(range(8)))
    return np.stack([np.asarray(r["out"], dtype=np.float32) for r in res.results], axis=0)
```

```python
import os
import numpy as np
from contextlib import ExitStack
import concourse.bass as bass
import concourse.mybir as mybir
from concourse.bass_utils import run_bass_kernel_spmd

F32 = mybir.dt.float32
BF16 = mybir.dt.bfloat16
AF = mybir.ActivationFunctionType
ALU = mybir.AluOpType

S = 4096
D = 1024
NCHUNK = 32
T = 512
NST = S // T
CPS = T // 128
INW = 7168
DFF = 2816
NJ = DFF // 128
EPS = 1e-6
RING = 8
NW = 6
SC = 128.0 ** -0.5
STOP = os.environ.get('KSTOP', 'all')


class Prog:
    def __init__(self, semh):
        self.semh = semh
        self.cnt = {'pe': 0, 'act': 0, 'dve': 0, 'pool': 0}
        self.ring_n = {'sp': 0, 'pool': 0}
        self.ring_use = {'sp': [0] * RING, 'pool': [0] * RING}
        self.streams = ['pe', 'act', 'dve', 'pool', 'sp']
        self.seen = {s: {} for s in self.streams}
        self.ops = {s: [] for s in self.streams}
        self.wr = {}
        self.rd = {}
        self.clock = {}
        self.order = {}
        self.nissued = 0

    def _deps(self, stream, reads, writes, extra=()):
        toks = {}

        def add(tok):
            if tok is None:
                return
            k, v = tok
            if v > toks.get(k, 0):
                toks[k] = v
        for r in reads:
            add(self.wr.get(r))
        for w in writes:
            add(self.wr.get(w))
            for k, v in self.rd.get(w, {}).items():
                add((k, v))
        for t in extra:
            add(t)
        waits = []
        seen = self.seen[stream]
        for k, v in sorted(toks.items(), key=lambda kv: -self.order.get(kv, 0)):
            if k == ('e', 'pe') and stream == 'pe':
                continue
            if seen.get(k, 0) >= v:
                continue
            seen[k] = v
            waits.append((self.semh[k], v))
            for k2, v2 in self.clock.get((k, v), {}).items():
                if v2 > seen.get(k2, 0):
                    seen[k2] = v2
        return waits

    def _commit(self, tok, reads, writes, stream=None):
        k, v = tok
        if stream is not None:
            snap = dict(self.seen[stream])
            snap[k] = max(snap.get(k, 0), v)
            self.clock[tok] = snap
            self.nissued += 1
            self.order[tok] = self.nissued
        for r in reads:
            d = self.rd.setdefault(r, {})
            if v > d.get(k, 0):
                d[k] = v
        for w in writes:
            self.wr[w] = tok
            self.rd[w] = {}

    def op(self, eng, fn, reads=(), writes=()):
        waits = self._deps(eng, reads, writes)
        self.cnt[eng] += 1
        tok = (('e', eng), self.cnt[eng])
        self._commit(tok, reads, writes, eng)
        self.ops[eng].append((waits, fn, (self.semh[('e', eng)], 1)))

    def dma(self, st, fn, reads=(), writes=()):
        n = self.ring_n[st]
        self.ring_n[st] += 1
        slot = n % RING
        prev = self.ring_use[st][slot]
        key = ('r', st, slot)
        extra = [(key, 16 * prev)] if prev > 0 else []
        waits = self._deps(st, reads, writes, extra)
        self.ring_use[st][slot] = prev + 1
        tok = (key, 16 * (prev + 1))
        self._commit(tok, reads, writes, st)
        self.ops[st].append((waits, fn, (self.semh[key], 16)))

    def barrier(self):
        toks = []
        for e, c in self.cnt.items():
            if c > 0:
                toks.append((('e', e), c))
        for st in ('sp', 'pool'):
            for slot in range(RING):
                u = self.ring_use[st][slot]
                if u > 0:
                    toks.append((('r', st, slot), 16 * u))
        for s in self.streams:
            waits = []
            for k, v in toks:
                if self.seen[s].get(k, 0) >= v:
                    continue
                self.seen[s][k] = v
                waits.append((self.semh[k], v))
            self.ops[s].append((waits, None, None))

    def take(self):
        o = self.ops
        self.ops = {s: [] for s in self.streams}
        return o


def _replay(eng, ops, embed=False):
    for waits, fn, inc in ops:
        if fn is None or not embed or not waits:
            for semh, val in waits:
                eng.wait_ge(semh, val)
            if fn is not None:
                ins = fn(eng)
                ins.then_inc(inc[0], inc[1])
        else:
            for semh, val in waits[:-1]:
                eng.wait_ge(semh, val)
            ins = fn(eng)
            ins._wait_ge(waits[-1][0], waits[-1][1])
            ins.then_inc(inc[0], inc[1])


class _PEFirst:
    def __init__(self, eng, wait):
        self.eng = eng
        self.wait = wait

    def _wrap(self, ins):
        if self.wait is not None:
            ins._wait_ge(self.wait[0], self.wait[1])
            self.wait = None
        return ins

    def matmul(self, *a, **k):
        return self._wrap(self.eng.matmul(*a, **k))

    def transpose(self, *a, **k):
        return self._wrap(self.eng.transpose(*a, **k))


def _replay_pe(eng, ops):
    for waits, fn, inc in ops:
        if fn is None or not waits:
            for semh, val in waits:
                eng.wait_ge(semh, val)
            if fn is not None:
                fn(eng).then_inc(inc[0], inc[1])
        else:
            for semh, val in waits[:-1]:
                eng.wait_ge(semh, val)
            prox = _PEFirst(eng, waits[-1])
            ins = fn(prox)
            assert prox.wait is None
            ins.then_inc(inc[0], inc[1])


def _run_block(nc, ops):
    with nc.Block() as block:
        @block.tensor
        def _(e):
            _replay_pe(e, ops['pe'])

        @block.scalar
        def _(e):
            _replay(e, ops['act'], embed=True)

        @block.vector
        def _(e):
            _replay(e, ops['dve'], embed=True)

        @block.gpsimd
        def _(e):
            _replay(e, ops['pool'], embed=True)

        @block.sync
        def _(e):
            _replay(e, ops['sp'], embed=True)


def build_nc():
    nc = bass.Bass("TRN2", target_bir_lowering=False)

    def din(name, shape, dt=F32):
        return nc.dram_tensor(name, list(shape), dt, kind="ExternalInput").ap()

    x = din("x", [S, D])
    w_in = din("w_in", [D, INW])
    w_ret_o = din("w_ret_o", [D, D])
    w_sgu_o = din("w_sgu_o", [D, D])
    w_out = din("w_out", [D, D])
    w_up = din("w_up", [D, 2 * DFF])
    w_down = din("w_down", [DFF, D])
    gcols = din("gcols", [128, 16])
    rows = din("rows", [4, D])
    logits = din("logits", [1, 8])
    wsp = din("wsp", [4, 128, 128])
    bspT = din("bspT", [128, 4])
    cwl = din("cwl", [128, 2 * NJ, 3])
    cbl = din("cbl", [128, 2 * NJ])
    ident = din("ident", [128, 128])
    cos_t = din("cos_t", [S, 64])
    sin_t = din("sin_t", [S, 64])
    dconst = din("dconst", [128, 6, 128])
    cvec = din("cvec", [128, 2])
    out = nc.dram_tensor("out", [S, D], F32, kind="ExternalOutput").ap()

    def dscr(name, shape, dt=BF16):
        return nc.dram_tensor(name, list(shape), dt, kind="Internal").ap()

    w_in_b = dscr("w_in_b", [D, INW])
    wro_b = dscr("wro_b", [D, D])
    wso_b = dscr("wso_b", [D, D])
    wout_b = dscr("wout_b", [D, D])
    wup_b = dscr("wup_b", [D, 2 * DFF])
    wdn_b = dscr("wdn_b", [DFF, D])
    rb_scr = dscr("rb_scr", [NCHUNK, 128, D])
    x1_scr = dscr("x1_scr", [S, D], F32)
    v_scr = dscr("v_scr", [NCHUNK, 128, D])
    k_scr = dscr("k_scr", [NCHUNK, 128, 512])

    with ExitStack() as cm:
        def sb(name, shape, dt=F32, stack=cm):
            return stack.enter_context(nc.sbuf_tensor(name, list(shape), dt))

        semh = {}
        for e in ('pe', 'act', 'dve', 'pool'):
            semh[('e', e)] = cm.enter_context(nc.semaphore("s_" + e))
        for st in ('sp', 'pool'):
            for i in range(RING):
                semh[('r', st, i)] = cm.enter_context(nc.semaphore("r_%s%d" % (st, i)))
        P = Prog(semh)

        NBANK = 8
        banks = [cm.enter_context(nc.psum_tensor("pb%d" % i, [128, 512], F32)) for i in range(NBANK)]
        bank_i = [0]

        pinned = set()

        def getbank(pin=False):
            for _ in range(2 * NBANK):
                i = bank_i[0] % NBANK
                bank_i[0] += 1
                if i not in pinned:
                    break
            else:
                raise RuntimeError("no free PSUM bank")
            if pin:
                pinned.add(i)
            return banks[i], ('ps', i)

        def gettr():
            bt, bk = getbank()
            return bt[:].bitcast(BF16), bk

        identb = sb("identb", [128, 128], BF16)
        gcol = sb("gcol", [128, 16])
        rot = {}

        def rtile(name, shape, dt, n, stack, pfx=""):
            rot[name] = ([sb("%s%s%d" % (pfx, name, i), shape, dt, stack) for i in range(n)], [0])

        def nxt(name):
            tiles, ctr = rot[name]
            i = ctr[0] % len(tiles)
            ctr[0] += 1
            return tiles[i], (name, i)

        rtile("junk", [128, D], BF16, 2, cm)
        P.dma('pool', lambda e: e.dma_start(out=identb[:], in_=ident[:, :]), writes=['identb'])
        P.dma('sp', lambda e: e.dma_start(out=gcol[:], in_=gcols[:, :]), writes=['gcol'])

        def cast(dst, src, c0, c1, key):
            P.dma('pool', lambda e: e.dma_start(out=dst[:, c0:c1], in_=src[:, c0:c1]), writes=[key])

        pending = []
        for g in [4, 5, 0] + list(range(6, 14)):
            pending.append((w_in_b, w_in, g, 'w_in_b'))
        for g in range(2):
            pending.append((wro_b, w_ret_o, g, 'wro_b'))
            pending.append((wso_b, w_sgu_o, g, 'wso_b'))
        for g in range(2):
            pending.append((wout_b, w_out, g, 'wout_b'))
        for g in range(11):
            pending.append((wup_b, w_up, g, 'wup_b'))
        for g in range(2):
            pending.append((wdn_b, w_down, g, 'wdn_b'))

        def emit_casts(k):
            for _ in range(k):
                if pending:
                    dst, src, g, nm = pending.pop(0)
                    cast(dst, src, g * 512, (g + 1) * 512, (nm, g))
        def emit_casts_until(remaining):
            while len(pending) > remaining:
                emit_casts(1)

        with ExitStack() as p1:
            def sb1(name, shape, dt=F32):
                return sb(name, shape, dt, p1)

            lgt = sb1("lgt", [128, 8])
            lg = sb1("lg", [128, 8])
            dc = sb1("dc", [128, 8])
            zf = sb1("zf", [128, 4])
            zb = sb1("zb", [128, 4])
            cv = sb1("cv", [128, 2])
            dcs = sb1("dcs", [128, 6, 128])
            DT = sb1("DT", [128, 4, 128])
            lng = sb1("lng", [128, D])
            lnb = sb1("lnb", [128, D])
            gpost = sb1("gpost", [128, D])
            wspT = sb1("wspT", [128, 4, 128], BF16)
            bsp = sb1("bsp", [128, 4])
            hT = sb1("hT", [128, 8, T], BF16)
            mT = sb1("mT", [128, 8, T], BF16)
            yrT = sb1("yrT", [128, 8, T], BF16)
            gate8 = sb1("gate8", [128, CPS, D], BF16)
            qkT = sb1("qkT", [128, 2, CPS, 4, 128], BF16)
            qx = sb1("qx", [128, 2, CPS, 4, 128], BF16)
            ysT = qkT[:].rearrange("p a c h n -> p (a c h n)").rearrange("p (k t) -> p k t", k=8)
            tas = qx[:].rearrange("p a c h n -> p (a c h n)").rearrange("p (c d) -> p c d", c=CPS)
            kz = sb1("kz", [128, CPS, 512], BF16)
            v8 = sb1("v8", [128, CPS, D], BF16)
            R32 = sb1("R32", [128, 4, 256])
            Rbf = [sb1("Rbf%d" % i, [128, 4, 256], BF16) for i in range(2)]
            wbuf = [sb1("wbuf%d" % i, [128, 8, 512], BF16) for i in range(NW)]
            rtile("ss", [128, 8], F32, 2, p1)
            rtile("st6", [128, 4, 6], F32, 3, p1)
            rtile("mv", [128, 4, 2], F32, 3, p1)
            rtile("st12", [128, 2, 6], F32, 3, p1)
            rtile("mv2", [128, 2], F32, 3, p1)
            rtile("sm", [128, 16], F32, 4, p1)
            rtile("xst", [128, D], F32, 4, p1)
            rtile("xn", [128, D], BF16, 2, p1)
            rtile("tg", [128, 512], F32, 2, p1)
            rtile("krot", [128, 512], F32, 1, p1)
            rtile("ra", [128, 256], F32, 1, p1)
            rtile("rb", [128, 256], F32, 1, p1)
            rtile("tm512", [128, 512], BF16, 2, p1)
            rtile("PT", [128, 512], BF16, 2, p1)
            rtile("on", [128, D], F32, 2, p1)
            rtile("rbc", [128, D], BF16, 3, p1)
            rtile("tm1024", [128, D], BF16, 4, p1)
            rtile("gsv", [128, D], F32, 1, p1)
            rtile("m1", [128, 512], F32, 1, p1)
            rtile("m2", [128, 512], F32, 1, p1)
            rtile("x1t", [128, 512], F32, 1, p1)
            rtile("cs", [128, 2, CPS, 64], F32, 2, p1)

            on0 = rot["on"][0][0]
            gsv0 = rot["gsv"][0][0]
            XIF = on0[:, 0:512].rearrange("p (h n) -> p h n", h=4)
            XIB = on0[:, 512:1024].rearrange("p (h n) -> p h n", h=4)
            wspf = gsv0[:, 0:512].rearrange("p (h n) -> p h n", h=4)
            wspb = gsv0[:, 512:768].bitcast(BF16).rearrange("p (h n) -> p h n", h=4)
            P.dma('sp', lambda e: e.dma_start(out=lgt[:], in_=logits[0:1, :].to_broadcast([128, 8])), writes=['lgt'])
            P.dma('sp', lambda e: e.dma_start(out=cv[:], in_=cvec[:, :]), writes=['cv'])
            P.dma('sp', lambda e: e.dma_start(out=dcs[:], in_=dconst[:, :, :]), writes=['dcs'])
            P.dma('sp', lambda e: e.dma_start(out=lng[:], in_=rows[2:3, :].to_broadcast([128, D])), writes=['lng'])
            P.dma('sp', lambda e: e.dma_start(out=lnb[:], in_=rows[3:4, :].to_broadcast([128, D])), writes=['lnb'])
            P.dma('sp', lambda e: e.dma_start(out=gpost[:], in_=rows[0:1, :].to_broadcast([128, D])), writes=['gpost'])
            P.dma('sp', lambda e: e.dma_start(out=bsp[:], in_=bspT[:, :]), writes=['bsp'])
            P.dma('sp', lambda e: e.dma_start(out=wspf[:], in_=wsp.rearrange("g p q -> p g q")), writes=['wspf'])

            P.op('act', lambda e: e.activation(out=lg[:], in_=lgt[:], func=AF.Exp, scale=-1.0), ['lgt'], ['lg'])
            P.op('dve', lambda e: e.tensor_scalar_add(out=lg[:], in0=lg[:], scalar1=1.0), ['lg'], ['lg'])
            P.op('act', lambda e: e.activation(out=lg[:], in_=lg[:], func=AF.Ln), ['lg'], ['lg'])
            P.op('dve', lambda e: e.tensor_scalar_mul(out=lg[:], in0=lg[:], scalar1=-1.0), ['lg'], ['lg'])
            P.op('act', lambda e: e.activation(out=dc[:], in_=lg[:], func=AF.Exp, scale=128.0), ['lg'], ['dc'])
            for h in range(4):
                P.op('act', lambda e, h=h: e.activation(out=zf[:, h:h + 1], in_=cv[:, 0:1], func=AF.Exp, scale=lg[:, h:h + 1]), ['lg', 'cv'], ['zf'])
                P.op('act', lambda e, h=h: e.activation(out=zb[:, h:h + 1], in_=cv[:, 1:2], func=AF.Exp, scale=lg[:, 4 + h:5 + h]), ['lg', 'cv'], ['zb'])
                P.op('act', lambda e, h=h: e.activation(out=DT[:, h, :], in_=dcs[:, 0, :], func=AF.Exp, scale=lg[:, h:h + 1]), ['lg', 'dcs'], ['DT'])
                P.op('dve', lambda e, h=h: e.tensor_tensor(out=DT[:, h, :], in0=DT[:, h, :], in1=dcs[:, 1, :], op=ALU.mult), ['DT', 'dcs'], ['DT'])
                P.op('act', lambda e, h=h: e.activation(out=XIF[:, h, :], in_=dcs[:, 2, :], func=AF.Exp, scale=lg[:, 4 + h:5 + h]), ['lg', 'dcs', 'DT'], ['XIF'])
                P.op('dve', lambda e, h=h: e.tensor_tensor(out=XIF[:, h, :], in0=XIF[:, h, :], in1=dcs[:, 3, :], op=ALU.mult), ['XIF', 'dcs'], ['XIF'])
                P.op('dve', lambda e, h=h: e.tensor_tensor(out=DT[:, h, :], in0=DT[:, h, :], in1=XIF[:, h, :], op=ALU.add), ['XIF', 'DT'], ['DT'])
            for h in range(4):
                P.op('act', lambda e, h=h: e.activation(out=XIF[:, h, :], in_=dcs[:, 4, :], func=AF.Exp, scale=lg[:, h:h + 1]), ['lg', 'dcs', 'DT'], ['XIF'])
                P.op('act', lambda e, h=h: e.activation(out=XIB[:, h, :], in_=dcs[:, 5, :], func=AF.Exp, scale=lg[:, 4 + h:5 + h]), ['lg', 'dcs'], ['XIB'])
            P.op('dve', lambda e: e.tensor_scalar_mul(out=XIF[:], in0=XIF[:], scalar1=SC), ['XIF'], ['XIF'])
            P.op('dve', lambda e: e.tensor_scalar_mul(out=XIB[:], in0=XIB[:], scalar1=SC), ['XIB'], ['XIB'])
            DX = dcs[:].rearrange("p a n -> p (a n)").bitcast(BF16)[:, 0:1024].rearrange("p (v h n) -> p v h n", v=2, h=4)
            P.op('dve', lambda e: e.tensor_tensor(out=DX[:, 0, :, :], in0=XIF[:], in1=identb[:].unsqueeze(1).to_broadcast([128, 4, 128]), op=ALU.mult),
                 ['XIF', 'identb', 'dcs', 'DT', 'XIB'], ['dcs', 'DX'])
            P.op('dve', lambda e: e.tensor_tensor(out=DX[:, 1, :, :], in0=XIB[:], in1=identb[:].unsqueeze(1).to_broadcast([128, 4, 128]), op=ALU.mult),
                 ['XIB', 'identb', 'dcs', 'DX'], ['dcs', 'DX'])
            P.op('dve', lambda e: e.tensor_copy(out=wspb[:], in_=wspf[:]), ['wspf'], ['wspb'])
            trt, trk = gettr()
            P.op('pe', lambda e: [e.transpose(trt[:, g * 128:(g + 1) * 128], wspb[:, g, :], identb[:]) for g in range(4)][-1],
                 ['wspb', 'identb'], [trk])
            P.op('dve', lambda e: e.tensor_copy(out=wspT[:].rearrange("p g q -> p (g q)"), in_=trt[:, 0:512]), [trk], ['wspT'])

            P.op('dve', lambda e: e.memset(on0[:, 0:1], 0.0), [], ['XIF', 'XIB'] + [(('on', 0), h) for h in range(4)])
            P.op('dve', lambda e: e.memset(gsv0[:, 0:1], 0.0), [], ['wspf', 'wspb'] + [(('gsv', 0), n) for n in range(2)])
            def rstd_a(src_ap, dst_ap, rkeys, wkey, scale):
                P.op('dve', lambda e: e.tensor_scalar(out=dst_ap, in0=src_ap, scalar1=scale, scalar2=EPS,
                                                      op0=ALU.mult, op1=ALU.add), rkeys, [wkey])

            def rstd_b(dst_ap, wkey):
                P.op('act', lambda e: e.activation(out=dst_ap, in_=dst_ap, func=AF.Sqrt), [wkey], [wkey])

            def rstd_c(dst_ap, wkey):
                P.op('dve', lambda e: e.reciprocal(out=dst_ap, in_=dst_ap), [wkey], [wkey])

            def rstd_from(src_ap, dst_ap, n, rkeys, wkey, scale):
                rstd_a(src_ap, dst_ap, rkeys, wkey, scale)
                rstd_b(dst_ap, wkey)
                rstd_c(dst_ap, wkey)

            def load_cs(s):
                cst, csk = nxt("cs")
                P.dma('sp', lambda e: e.dma_start(out=cst[:, 0, :, :], in_=cos_t[s * T:(s + 1) * T, :].rearrange("(c p) j -> p c j", p=128)), writes=[csk])
                P.dma('sp', lambda e: e.dma_start(out=cst[:, 1, :, :], in_=sin_t[s * T:(s + 1) * T, :].rearrange("(c p) j -> p c j", p=128)), reads=[], writes=[(csk, 's')])
                return cst, [csk, (csk, 's')]

            def norm_to_T(xt, xk, ss_col, gofs, dstT, dkey, col0, ncols=128, npart=128, rkey='rstd'):
                xnt, xnk = nxt("xn")
                P.op('act', lambda e: e.activation(out=xnt[0:npart, :], in_=xt[0:npart, :], func=AF.Identity, scale=ss_col),
                     [xk, rkey], [xnk])
                trt, trk = gettr()

                def tr(e):
                    for kb in range(8):
                        ins = e.transpose(trt[:, kb * ncols:(kb + 1) * ncols], xnt[0:npart, kb * 128:(kb + 1) * 128],
                                          identb[0:npart, 0:npart])
                    return ins
                P.op('pe', tr, [xnk, 'identb'], [trk])
                P.op('dve', lambda e: e.tensor_tensor(
                    out=dstT[:, :, col0:col0 + ncols],
                    in0=trt[:, 0:8 * ncols].rearrange("p (k n) -> p k n", k=8),
                    in1=gcol[:, gofs:gofs + 8].unsqueeze(2).to_broadcast([128, 8, ncols]), op=ALU.mult),
                    [trk, 'gcol'], [dkey])

            def norm_part1(s):
                xts = []
                ss, ssk = nxt("ss")
                for c in range(CPS):
                    xt, xk = nxt("xst")
                    r0 = s * T + c * 128
                    P.dma('sp', lambda e, xt=xt, r0=r0: e.dma_start(out=xt[:], in_=x[r0:r0 + 128, :]), writes=[xk])
                    jt, jk = nxt("junk")
                    P.op('act', lambda e, xt=xt, c=c, jt=jt: e.activation(out=jt[:], in_=xt[:], func=AF.Square, accum_out=ss[:, c:c + 1]),
                         [xk], [(ssk, c), jk])
                    xts.append((xt, xk))
                rstd_from(ss[:, 0:CPS], ss[:, 0:CPS], CPS, [(ssk, c) for c in range(CPS)], (ssk, 'rstd'), 1.0 / D)
                return (xts, ss, ssk)

            def norm_part2(st):
                xts, ss, ssk = st
                for c in range(CPS):
                    xt, xk = xts[c]
                    norm_to_T(xt, xk, ss[:, c:c + 1], 0, hT, ('hT', c), c * 128, rkey=(ssk, 'rstd'))

            wl_n = [0]

            def load_w(scr, g, key, nkb=8, rows_rearr="(kb p) c -> p kb c"):
                slot = wl_n[0] % NW
                wl_n[0] += 1
                wt = wbuf[slot]
                assert key in P.wr, key
                P.dma('sp', lambda e: e.dma_start(out=wt[:, 0:nkb, :], in_=scr[:, g * 512:(g + 1) * 512].rearrange(rows_rearr, p=128)),
                      reads=[key], writes=[('wb', slot)])
                return wt, ('wb', slot)

            def proj(c, wt, wk, srcT, skey):
                bt, bk = getbank()

                def mm(e):
                    for kb in range(8):
                        ins = e.matmul(bt[:], lhsT=srcT[:, kb, c * 128:(c + 1) * 128], rhs=wt[:, kb, :],
                                       start=(kb == 0), stop=(kb == 7))
                    return ins
                P.op('pe', mm, [wk] + (skey if isinstance(skey, list) else [skey]), [bk])
                return bt, bk

            def rotary(bt, bk, cst, cskeys, c, dst, dkey):
                b4 = bt[:].rearrange("p (h t j) -> p h t j", h=4, t=2)
                d4 = dst[:].rearrange("p (h t j) -> p h t j", h=4, t=2)
                cosb = cst[:, 0, c, :].unsqueeze(1).to_broadcast([128, 4, 64])
                sinb = cst[:, 1, c, :].unsqueeze(1).to_broadcast([128, 4, 64])
                ra, rak = nxt("ra")
                rb_, rbk = nxt("rb")
                ra3 = ra[:].rearrange("p (h j) -> p h j", h=4)
                rb3 = rb_[:].rearrange("p (h j) -> p h j", h=4)
                P.op('dve', lambda e: e.tensor_tensor(out=ra3, in0=b4[:, :, 0, :], in1=cosb, op=ALU.mult), [bk] + cskeys, [rak])
                P.op('dve', lambda e: e.tensor_tensor(out=rb3, in0=b4[:, :, 1, :], in1=sinb, op=ALU.mult), [bk] + cskeys, [rbk])
                P.op('dve', lambda e: e.tensor_tensor(out=d4[:, :, 0, :], in0=ra3, in1=rb3, op=ALU.subtract), [rak, rbk], [(dkey, 0)])
                P.op('dve', lambda e: e.tensor_tensor(out=ra3, in0=b4[:, :, 0, :], in1=sinb, op=ALU.mult), [bk] + cskeys, [rak])
                P.op('dve', lambda e: e.tensor_tensor(out=rb3, in0=b4[:, :, 1, :], in1=cosb, op=ALU.mult), [bk] + cskeys, [rbk])
                P.op('dve', lambda e: e.tensor_tensor(out=d4[:, :, 1, :], in0=ra3, in1=rb3, op=ALU.add), [rak, rbk], [(dkey, 1)])
                return [(dkey, 0), (dkey, 1)]

            def k_stage(c, wt, wk, cst, cskeys, zcol, kz_eng='pool'):
                bt, bk = proj(c, wt, wk, hT, ('hT', c))
                kr, krk = nxt("krot")
                keys = rotary(bt, bk, cst, cskeys, c, kr, krk)
                P.op(kz_eng, lambda e: e.tensor_tensor(
                    out=kz[:, c, :].rearrange("p (h d) -> p h d", h=4),
                    in0=kr[:].rearrange("p (h d) -> p h d", h=4),
                    in1=zcol.unsqueeze(2).to_broadcast([128, 4, 128]), op=ALU.mult), keys + ['zf', 'zb'], [('kz', c)])
                return kr, keys

            def v_stage(c, n, wt, wk):
                bt, bk = proj(c, wt, wk, hT, ('hT', c))
                P.op('act', lambda e: e.activation(out=v8[:, c, n * 512:(n + 1) * 512], in_=bt[:], func=AF.Identity), [bk], [('v8', c, n)])

            def kv_update(c, dcol0, rbf_dst, rbf_key):
                for hp in range(2):
                    bt, bk = getbank()

                    def mm(e, hp=hp, bt=bt):
                        for hh in range(2):
                            h = hp * 2 + hh
                            ins = e.matmul(bt[:, hh * 256:(hh + 1) * 256], lhsT=kz[:, c, h * 128:(h + 1) * 128],
                                           rhs=v8[:, c, h * 256:(h + 1) * 256], start=True, stop=True)
                        return ins
                    P.op('pe', mm, [('kz', c), ('v8', c, hp)], [bk])
                    for hh in range(2):
                        h = hp * 2 + hh
                        P.op('dve', lambda e, h=h, hh=hh, bt=bt: e.scalar_tensor_tensor(
                            out=R32[:, h, :], in0=R32[:, h, :], scalar=dc[:, dcol0 + h:dcol0 + h + 1],
                            in1=bt[:, hh * 256:(hh + 1) * 256], op0=ALU.mult, op1=ALU.add), [bk, ('R32', h), 'dc'], [('R32', h)])
                    P.op('act', lambda e, hp=hp: e.activation(out=rbf_dst[:, 2 * hp:2 * hp + 2, :], in_=R32[:, 2 * hp:2 * hp + 2, :], func=AF.Identity),
                         [('R32', 2 * hp), ('R32', 2 * hp + 1)], [(rbf_key, hp)])

            P.op('dve', lambda e: e.memset(R32[:], 0.0), [], [('R32', h) for h in range(4)])
            P.op('pool', lambda e: e.memset(Rbf[0][:], 0.0), [], [(('Rbf', 0), 0), (('Rbf', 0), 1)])
            rpar = 0
            if STOP != 'setup':
                def load_w_direct(g):
                    slot = wl_n[0] % NW
                    wl_n[0] += 1
                    wt = wbuf[slot]
                    P.dma('pool', lambda e: e.dma_start(out=wt[:], in_=w_in[:, g * 512:(g + 1) * 512].rearrange("(kb p) c -> p kb c", p=128)),
                          writes=[('wb', slot)])
                    return wt, ('wb', slot)
                wkt, wkk = load_w_direct(1)
                wv0, wv0k = load_w_direct(2)
                wv1, wv1k = load_w_direct(3)
                cs_of = {}
                cs_of[NST - 1] = load_cs(NST - 1)
                norm_part2(norm_part1(NST - 1))
                xts_next = None
                pend = None

                def state_step(i):
                    nonlocal rpar
                    cur = Rbf[rpar]
                    P.dma('sp', lambda e, cur=cur, i=i: e.dma_start(out=rb_scr[i], in_=cur[:].rearrange("p h e -> p (h e)")),
                          reads=[(('Rbf', rpar), 0), (('Rbf', rpar), 1)], writes=[('rb_scr', i)])
                    if i > 0:
                        rpar ^= 1
                        kv_update(i % CPS, 4, Rbf[rpar], ('Rbf', rpar))

                for i in range(NCHUNK - 1, -1, -1):
                    s_, c = divmod(i, CPS)
                    if c == CPS - 1:
                        emit_casts(1)
                        if s_ > 0:
                            cs_of[s_ - 1] = load_cs(s_ - 1)
                            xts_next = norm_part1(s_ - 1)
                    cst, cskeys = cs_of[s_]
                    kr, kkeys = k_stage(c, wkt, wkk, cst, cskeys, zb[:, :], kz_eng='dve')
                    kt, ktk = nxt("tm512")
                    P.op('act', lambda e, kt=kt, kr=kr: e.activation(out=kt[:], in_=kr[:], func=AF.Identity), kkeys, [ktk])
                    P.dma('sp', lambda e, kt=kt, i=i: e.dma_start(out=k_scr[i], in_=kt[:]), reads=[ktk], writes=[('k_scr', i)])
                    v_stage(c, 0, wv0, wv0k)
                    v_stage(c, 1, wv1, wv1k)
                    P.dma('sp', lambda e, c=c, i=i: e.dma_start(out=v_scr[i], in_=v8[:, c, :]),
                          reads=[('v8', c, 0), ('v8', c, 1)], writes=[('v_scr', i)])
                    if c == 0 and s_ > 0:
                        norm_part2(xts_next)
                    if pend is not None:
                        state_step(pend)
                    pend = i
                state_step(pend)

            P.op('dve', lambda e: e.memset(R32[:], 0.0), [('R32', h) for h in range(4)], [('R32', h) for h in range(4)])
            P.op('pool', lambda e: e.memset(Rbf[0][:], 0.0), [], [(('Rbf', 0), 0), (('Rbf', 0), 1)])
            fpar = 0
            RUN1 = STOP not in ('setup', 'p0')
            gl1 = [(w_in_b, 4, 'w_in_b'), (w_in_b, 5, 'w_in_b'),
                   (w_in_b, 0, 'w_in_b'),
                   (w_in_b, 6, 'w_in_b'), (w_in_b, 7, 'w_in_b'),
                   (w_in_b, 8, 'w_in_b'), (w_in_b, 9, 'w_in_b'),
                   (w_in_b, 10, 'w_in_b'), (w_in_b, 11, 'w_in_b'),
                   (w_in_b, 12, 'w_in_b'), (w_in_b, 13, 'w_in_b'),
                   (wro_b, 0, 'wro_b'), (wso_b, 0, 'wso_b'),
                   (wro_b, 1, 'wro_b'), (wso_b, 1, 'wso_b'),
                   (wout_b, 0, 'wout_b'), (wout_b, 1, 'wout_b')]
            NG = len(gl1)
            glist = gl1 * NST
            loaded = {}

            def ensure_cast(key):
                while key not in P.wr and pending:
                    emit_casts(1)

            def Wg(idx, la=3, cap=None):
                hi = idx + la + 1
                if cap is not None:
                    hi = min(hi, cap + 1)
                for j in range(min(hi, len(glist))):
                    if j not in loaded:
                        for jj in range(j, min(j + 5, len(glist))):
                            ensure_cast((glist[jj][2], glist[jj][1]))
                        scr, g, nm = glist[j]
                        loaded[j] = load_w(scr, g, (nm, g))
                return loaded[idx]

            def pipe(n, stages, lag=1, filler=None):
                st = {}
                for t in range(n + (len(stages) - 1) * lag):
                    for k, f in enumerate(stages):
                        idx = t - k * lag
                        if 0 <= idx < n:
                            st[idx] = f(idx, st.get(idx))
                    if filler is not None:
                        filler(t)

            if RUN1:
                cs_cur = load_cs(0)
                norm_part2(norm_part1(0))
            for s in (range(NST) if RUN1 else []):
                W = lambda idx, cap=None, s=s: Wg(s * NG + idx, cap=(None if cap is None else s * NG + cap))
                cst, cskeys = cs_cur

                for n in range(2):
                    wt, wk = W(n)
                    for c in range(CPS):
                        bt, bk = proj(c, wt, wk, hT, ('hT', c))
                        tg, tgk = nxt("tg")
                        P.op('act', lambda e, tg=tg, bt=bt: e.activation(out=tg[:], in_=bt[:], func=AF.Tanh, scale=0.5), [bk], [tgk])
                        P.op('dve', lambda e, tg=tg, bt=bt, c=c, n=n: e.scalar_tensor_tensor(
                            out=gate8[:, c, n * 512:(n + 1) * 512], in0=tg[:], scalar=1.0, in1=bt[:],
                            op0=ALU.add, op1=ALU.mult), [bk, tgk], [('gate8', c, n)])

                wq, wqk = W(2)
                P.dma('sp', lambda e, s=s: e.dma_start(out=v8[:], in_=v_scr[s * CPS:(s + 1) * CPS].rearrange("c p e -> p c e")),
                      reads=[('v_scr', s * CPS + c) for c in range(CPS)], writes=[('v8', c, n) for c in range(CPS) for n in range(2)])

                def qk_A(idx, _):
                    which, c = divmod(idx, CPS)
                    if which == 0:
                        bt, bk = proj(c, wq, wqk, hT, ('hT', c))
                        kr, krk = nxt("krot")
                        keys = rotary(bt, bk, cst, cskeys, c, kr, krk)
                        qt, qtk = nxt("tm512")
                        P.op('act', lambda e, qt=qt, kr=kr: e.activation(out=qt[:], in_=kr[:], func=AF.Identity), keys, [qtk])
                    else:
                        i = s * CPS + c
                        qt, qtk = nxt("tm512")
                        P.dma('sp', lambda e, qt=qt, i=i: e.dma_start(out=qt[:], in_=k_scr[i]), reads=[('k_scr', i)], writes=[qtk])
                        P.op('pool', lambda e, qt=qt, c=c: e.tensor_tensor(
                            out=kz[:, c, :].rearrange("p (h d) -> p h d", h=4),
                            in0=qt[:].rearrange("p (h d) -> p h d", h=4),
                            in1=zf[:, :].unsqueeze(2).to_broadcast([128, 4, 128]), op=ALU.mult), [qtk, 'zf'], [('kz', c)])
                    return (qt, qtk)

                def qk_B(idx, stt):
                    which, c = divmod(idx, CPS)
                    qt, qtk = stt
                    if which == 0:
                        dsts = [(qkT[:, 0, c, :, :], ('qT', c), None), (qx[:, 0, c, :, :], ('qxf', c), 0), (qx[:, 1, c, :, :], ('qxb', c), 1)]
                        for k, (dst, dkey, v) in enumerate(dsts):
                            bt, bk = getbank()

                            def mmq(e, bt=bt, v=v):
                                for h in range(4):
                                    rhs = identb[:] if v is None else DX[:, v, h, :]
                                    ins = e.matmul(bt[:, h * 128:(h + 1) * 128], lhsT=qt[:, h * 128:(h + 1) * 128], rhs=rhs, start=True, stop=True)
                                return ins
                            P.op('pe', mmq, [qtk, 'identb', 'DX'], [bk])
                            eng = 'dve' if k == 2 else 'act'
                            if eng == 'act':
                                P.op('act', lambda e, bt=bt, dst=dst: e.activation(out=dst, in_=bt[:].rearrange("p (h n) -> p h n", h=4), func=AF.Identity),
                                     [bk], [dkey])
                            else:
                                P.op('dve', lambda e, bt=bt, dst=dst: e.tensor_copy(out=dst, in_=bt[:].rearrange("p (h n) -> p h n", h=4)), [bk], [dkey])
                    else:
                        trt, trk = gettr()
                        P.op('pe', lambda e, trt=trt, qt=qt: [e.transpose(trt[:, h * 128:(h + 1) * 128], qt[:, h * 128:(h + 1) * 128], identb[:])
                                                              for h in range(4)][-1], [qtk, 'identb'], [trk])
                        tr3 = trt[:, 0:512].rearrange("p (h n) -> p h n", h=4)
                        P.op('act', lambda e, tr3=tr3, c=c: e.activation(out=qkT[:, 1, c, :, :], in_=tr3, func=AF.Identity), [trk], [('kT', c)])
                    return stt
                pipe(2 * CPS, [qk_A, qk_B])
                W(3)
                emit_casts(2)

                def load_rb(c):
                    i = s * CPS + c
                    rt, rk = nxt("rbc")
                    P.dma('sp', lambda e, rt=rt, i=i: e.dma_start(out=rt[:], in_=rb_scr[i]), reads=[('rb_scr', i)], writes=[rk])
                    return rt, rk
                rbs = {0: load_rb(0), 1: load_rb(1)}

                def ret_A(c, _):
                    nonlocal fpar
                    i = s * CPS + c
                    if c + 2 < CPS:
                        rbs[c + 2] = load_rb(c + 2)
                    rt, rk = rbs[c]
                    rcur = Rbf[fpar]
                    rkeys = [(('Rbf', fpar), 0), (('Rbf', fpar), 1)]
                    if i < NCHUNK - 1:
                        fpar ^= 1
                        kv_update(c, 0, Rbf[fpar], ('Rbf', fpar))
                    stb, stk = getbank()
                    P.op('pe', lambda e, stb=stb, c=c: [e.matmul(stb[:, h * 128:(h + 1) * 128], lhsT=qkT[:, 1, c, h, :], rhs=qkT[:, 0, c, h, :],
                                                                 start=True, stop=True) for h in range(4)][-1],
                         [('qT', c), ('kT', c)], [stk])
                    pt, ptk = nxt("PT")
                    P.op('dve', lambda e, pt=pt, stb=stb: e.tensor_tensor(out=pt[:], in0=stb[:], in1=DT[:].rearrange("p h n -> p (h n)"), op=ALU.mult),
                         [stk, 'DT'], [ptk])
                    obanks = []
                    for hp in range(2):
                        ob, obk = getbank(pin=True)

                        def omm(e, hp=hp, ob=ob, pt=pt, rcur=rcur, c=c, rt=rt):
                            for hh in range(2):
                                h = hp * 2 + hh
                                o_ = ob[:, hh * 256:(hh + 1) * 256]
                                e.matmul(o_, lhsT=pt[:, h * 128:(h + 1) * 128], rhs=v8[:, c, h * 256:(h + 1) * 256], start=True, stop=False)
                                e.matmul(o_, lhsT=qx[:, 0, c, h, :], rhs=rcur[:, h, :], start=False, stop=False)
                                ins = e.matmul(o_, lhsT=qx[:, 1, c, h, :], rhs=rt[:, h * 256:(h + 1) * 256], start=False, stop=True)
                            return ins
                        P.op('pe', omm, [ptk, ('v8', c, hp), ('qxf', c), ('qxb', c), rk] + rkeys, [obk])
                        obanks.append((ob, obk))
                    return obanks

                def ret_B1(c, obanks):
                    sm, smk = nxt("sm")
                    mv, mvk = nxt("mv")
                    st6, s6k = nxt("st6")
                    for h in range(4):
                        ob, obk = obanks[h // 2]
                        osl = ob[:, (h % 2) * 256:(h % 2 + 1) * 256]
                        P.op('dve', lambda e, h=h, osl=osl: e.bn_stats(out=st6[:, h, :], in_=osl), [obk], [(s6k, h)])
                        P.op('dve', lambda e, h=h: e.bn_aggr(out=mv[:, h, :], in_=st6[:, h, :]), [(s6k, h)], [(mvk, h)])
                    rstd_a(mv[:, :, 1], sm[:, 0:4], [(mvk, h) for h in range(4)], (smk, 'rs4'), 1.0)
                    rstd_b(sm[:, 0:4], (smk, 'rs4'))
                    return (obanks, sm, smk, mv, mvk)

                def ret_B2(c, stt):
                    obanks, sm, smk, mv, mvk = stt
                    rstd_c(sm[:, 0:4], (smk, 'rs4'))
                    P.op('dve', lambda e: e.scalar_tensor_tensor(out=sm[:, 4:8], in0=mv[:, :, 0], scalar=-1.0, in1=sm[:, 0:4],
                                                                 op0=ALU.mult, op1=ALU.mult), [(smk, 'rs4')] + [(mvk, h) for h in range(4)], [(smk, 'nmr')])
                    on, onk = nxt("on")
                    for h in range(4):
                        ob, obk = obanks[h // 2]
                        osl = ob[:, (h % 2) * 256:(h % 2 + 1) * 256]
                        P.op('act', lambda e, h=h, osl=osl, on=on: e.activation(out=on[:, h * 256:(h + 1) * 256], in_=osl, func=AF.Identity,
                                                                                bias=sm[:, 4 + h:5 + h], scale=sm[:, h:h + 1]),
                             [obk, (smk, 'rs4'), (smk, 'nmr')], [(onk, h)])
                    yt, ytk = nxt("tm1024")
                    P.op('dve', lambda e, yt=yt, on=on, c=c: e.tensor_tensor(out=yt[:], in0=on[:], in1=gate8[:, c, :], op=ALU.mult),
                         [(onk, h) for h in range(4)] + [('gate8', c, 0), ('gate8', c, 1)], [ytk])
                    for ob, obk in obanks:
                        pinned.discard(obk[1])
                    return (yt, ytk)

                def ret_C(c, stt):
                    yt, ytk = stt
                    trt, trk = gettr()
                    P.op('pe', lambda e, trt=trt, yt=yt: [e.transpose(trt[:, kb * 128:(kb + 1) * 128], yt[:, kb * 128:(kb + 1) * 128], identb[:])
                                                          for kb in range(8)][-1], [ytk, 'identb'], [trk])
                    P.op('act', lambda e, trt=trt, c=c: e.activation(out=yrT[:, :, c * 128:(c + 1) * 128],
                                                                     in_=trt[:].rearrange("p (k n) -> p k n", k=8), func=AF.Identity, scale=0.5),
                         [trk], [('yrT', c)])
                    return stt
                def u_item(c, n):
                    wt, wk = W(3 + n, cap=5)
                    bt, bk = proj(c, wt, wk, hT, ('hT', c))
                    P.op('act', lambda e, bt=bt, c=c, n=n: e.activation(out=gate8[:, c, n * 512:(n + 1) * 512], in_=bt[:], func=AF.Gelu_apprx_tanh),
                         [bk], [('gate8', c, n)])

                def u_fill(t):
                    c = t - 1
                    if 0 <= c < CPS:
                        u_item(c, 0)
                        u_item(c, 1)
                rst = {}
                for t in range(CPS + 2):
                    if 0 <= t - 1 < CPS:
                        rst[t - 1] = ret_B1(t - 1, rst[t - 1])
                    if t < CPS:
                        rst[t] = ret_A(t, None)
                    if 0 <= t - 1 < CPS:
                        rst[t - 1] = ret_B2(t - 1, rst[t - 1])
                    if 0 <= t - 2 < CPS:
                        ret_C(t - 2, rst[t - 2])
                    u_fill(t)

                w0, w0k = W(5)
                w1, w1k = W(6)
                qkkeys = [('qT', c) for c in range(CPS)] + [('kT', c) for c in range(CPS)]

                def sv_A1(c):
                    sm, smk = nxt("sm")
                    mv2, mv2k = nxt("mv2")
                    st12, s12k = nxt("st12")
                    gs, gsk = nxt("gsv")
                    for n, (wt, wk) in enumerate(((w0, w0k), (w1, w1k))):
                        bt, bk = proj(c, wt, wk, hT, ('hT', c))
                        P.op('act', lambda e, bt=bt, gs=gs, n=n: e.activation(out=gs[:, n * 512:(n + 1) * 512], in_=bt[:], func=AF.Gelu_apprx_tanh),
                             [bk], [(gsk, n)])
                        P.op('dve', lambda e, gs=gs, n=n: e.bn_stats(out=st12[:, n, :], in_=gs[:, n * 512:(n + 1) * 512]), [(gsk, n)], [(s12k, n)])
                    P.op('dve', lambda e: e.bn_aggr(out=mv2[:], in_=st12[:].rearrange("p a b -> p (a b)")), [(s12k, 0), (s12k, 1)], [mv2k])
                    rstd_a(mv2[:, 1:2], sm[:, 8:9], [mv2k], (smk, 'rs1'), 1.0)
                    P.op('dve', lambda e, gs=gs: e.scalar_tensor_tensor(out=gs[:], in0=gs[:], scalar=mv2[:, 0:1], in1=lng[:],
                                                                        op0=ALU.subtract, op1=ALU.mult), [(gsk, 0), (gsk, 1), mv2k, 'lng'], [(gsk, 0), (gsk, 1)])
                    return (sm, smk, mv2, mv2k, gs, gsk)

                def sv_A2(st):
                    sm, smk, mv2, mv2k, gs, gsk = st
                    rstd_b(sm[:, 8:9], (smk, 'rs1'))

                def sv_A3(st):
                    sm, smk, mv2, mv2k, gs, gsk = st
                    rstd_c(sm[:, 8:9], (smk, 'rs1'))
                    sv, svk = nxt("tm1024")
                    P.op('dve', lambda e, gs=gs, sv=sv: e.scalar_tensor_tensor(out=sv[:], in0=gs[:], scalar=sm[:, 8:9], in1=lnb[:],
                                                                               op0=ALU.mult, op1=ALU.add), [(gsk, 0), (gsk, 1), (smk, 'rs1'), 'lnb'], [svk])
                    return (sv, svk)

                def sv_B(c, stt):
                    sv, svk = stt
                    ys, ysk = nxt("tm1024")
                    for gp in range(2):
                        bt, bk = getbank()
                        P.op('pe', lambda e, bt=bt, sv=sv, gp=gp: [e.matmul(bt[:, gg * 256:(gg + 1) * 256], lhsT=wspT[:, gp * 2 + gg, :],
                                                                          rhs=sv[:, (gp * 2 + gg) * 256:(gp * 2 + gg + 1) * 256], start=True, stop=True)
                                                                 for gg in range(2)][-1], [svk, 'wspT'], [bk])
                        for gg in range(2):
                            g = gp * 2 + gg
                            P.op('dve', lambda e, bt=bt, ys=ys, g=g, gg=gg, c=c: e.scalar_tensor_tensor(
                                out=ys[:, g * 256:(g + 1) * 256], in0=bt[:, gg * 256:(gg + 1) * 256], scalar=bsp[:, g:g + 1],
                                in1=gate8[:, c, g * 256:(g + 1) * 256], op0=ALU.add, op1=ALU.mult),
                                [bk, 'bsp', ('gate8', c, g // 2)], [(ysk, g)])
                    return (ys, ysk)

                def sv_C(c, stt):
                    ys, ysk = stt
                    trt, trk = gettr()
                    P.op('pe', lambda e, trt=trt, ys=ys: [e.transpose(trt[:, kb * 128:(kb + 1) * 128], ys[:, kb * 128:(kb + 1) * 128], identb[:])
                                                          for kb in range(8)][-1], [(ysk, g) for g in range(4)] + ['identb'], [trk])
                    P.op('act', lambda e, trt=trt, c=c: e.activation(out=ysT[:, :, c * 128:(c + 1) * 128],
                                                                     in_=trt[:].rearrange("p (k n) -> p k n", k=8), func=AF.Identity),
                         [trk] + qkkeys, [('ysT', c)] + qkkeys)
                    return stt
                qxkeys = [('qxf', c) for c in range(CPS)] + [('qxb', c) for c in range(CPS)]

                def gate_item(g, c):
                    wt, wk = W(7 + g, cap=9 if g < 3 else None)
                    bt, bk = proj(c, wt, wk, hT, ('hT', c))
                    if g < 2:
                        P.op('act', lambda e, bt=bt: e.activation(out=v8[:, c, g * 512:(g + 1) * 512], in_=bt[:], func=AF.Tanh, scale=0.5),
                             [bk], [('v8', c, g)])
                    else:
                        n = g - 2
                        P.op('act', lambda e, bt=bt: e.activation(out=tas[:, c, n * 512:(n + 1) * 512], in_=bt[:], func=AF.Tanh, scale=0.5),
                             [bk] + qxkeys, [('tas', c, n)] + qxkeys)
                g_items = [(g, c) for g in range(3) for c in range(CPS)]

                def g_fill(t):
                    for _ in range(2):
                        if g_items:
                            gate_item(*g_items.pop(0))
                sst = {}
                for t in range(CPS + 2):
                    a1 = sv_A1(t) if t < CPS else None
                    if 0 <= t - 2 < CPS:
                        sv_C(t - 2, sst[t - 2])
                    g_fill(t)
                    if a1 is not None:
                        sv_A2(a1)
                    if 0 <= t - 1 < CPS:
                        sst[t - 1] = sv_B(t - 1, sst[t - 1])
                    if a1 is not None:
                        sst[t] = sv_A3(a1)
                while g_items:
                    gate_item(*g_items.pop(0))
                for c in range(CPS):
                    gate_item(3, c)

                xts_next = None
                if s + 1 < NST:
                    cs_cur = load_cs(s + 1)
                    xts_next = norm_part1(s + 1)

                for n in range(2):
                    wr, wrk = W(11 + 2 * n)
                    ws, wsk = W(12 + 2 * n)
                    for c in range(CPS):
                        br, brk = proj(c, wr, wrk, yrT, ('yrT', c))
                        bs_, bsk = proj(c, ws, wsk, ysT, [('ysT', c)] + qkkeys)
                        m1, m1k = nxt("m1")
                        m2, m2k = nxt("m2")
                        P.op('dve', lambda e, m1=m1, br=br, c=c, n=n: e.scalar_tensor_tensor(
                            out=m1[:], in0=v8[:, c, n * 512:(n + 1) * 512], scalar=1.0, in1=br[:], op0=ALU.add, op1=ALU.mult),
                            [brk, ('v8', c, n)], [m1k])
                        P.op('dve', lambda e, m2=m2, bs_=bs_, c=c, n=n: e.scalar_tensor_tensor(
                            out=m2[:], in0=tas[:, c, n * 512:(n + 1) * 512], scalar=1.0, in1=bs_[:], op0=ALU.add, op1=ALU.mult),
                            [bsk, ('tas', c, n)] + qxkeys, [m2k])
                        P.op('pool', lambda e, m1=m1, m2=m2, c=c, n=n: e.tensor_tensor(out=gate8[:, c, n * 512:(n + 1) * 512], in0=m1[:], in1=m2[:], op=ALU.add),
                             [m1k, m2k], [('gate8', c, n)])
                for c in range(CPS):
                    trt, trk = gettr()
                    P.op('pe', lambda e, trt=trt, c=c: [e.transpose(trt[:, kb * 128:(kb + 1) * 128], gate8[:, c, kb * 128:(kb + 1) * 128], identb[:])
                                                        for kb in range(8)][-1], [('gate8', c, 0), ('gate8', c, 1), 'identb'], [trk])
                    P.op('act', lambda e, trt=trt, c=c: e.activation(out=mT[:, :, c * 128:(c + 1) * 128],
                                                                     in_=trt[:].rearrange("p (k n) -> p k n", k=8), func=AF.Identity, scale=0.5),
                         [trk], [('mT', c)])

                if xts_next is not None:
                    norm_part2(xts_next)

                wo0, wo0k = W(15)
                wo1, wo1k = W(16)
                def wo_X1(c):
                    r0 = s * T + c * 128
                    xt, xk = nxt("on")
                    sm, smk = nxt("sm")
                    P.dma('sp', lambda e, xt=xt, r0=r0: e.dma_start(out=xt[:], in_=x[r0:r0 + 128, :]), writes=[(xk, h) for h in range(4)])
                    pbs = []
                    for n, (wt, wk) in enumerate(((wo0, wo0k), (wo1, wo1k))):
                        bt, bk = proj(c, wt, wk, mT, ('mT', c))
                        jt, jk = nxt("junk")
                        P.op('act', lambda e, bt=bt, n=n, jt=jt, sm=sm: e.activation(out=jt[:, 0:512], in_=bt[:], func=AF.Square, accum_out=sm[:, 10 + n:11 + n]),
                             [bk], [(smk, 'ssq', n), jk])
                        pbs.append((bt, bk))
                    P.op('dve', lambda e, sm=sm: e.tensor_tensor(out=sm[:, 12:13], in0=sm[:, 10:11], in1=sm[:, 11:12], op=ALU.add), [(smk, 'ssq', 0), (smk, 'ssq', 1)], [(smk, 'ssq2')])
                    rstd_a(sm[:, 12:13], sm[:, 13:14], [(smk, 'ssq2')], (smk, 'rsq'), 1.0 / D)
                    return (r0, xt, xk, sm, smk, pbs)

                def wo_X2(st):
                    r0, xt, xk, sm, smk, pbs = st
                    rstd_b(sm[:, 13:14], (smk, 'rsq'))

                def wo_X3(st):
                    r0, xt, xk, sm, smk, pbs = st
                    rstd_c(sm[:, 13:14], (smk, 'rsq'))
                    xkeys = [(xk, h) for h in range(4)]
                    for n in range(2):
                        bt, bk = pbs[n]
                        t1, t1k = nxt("x1t")
                        P.op('dve', lambda e, bt=bt, t1=t1, n=n, sm=sm: e.scalar_tensor_tensor(
                            out=t1[:], in0=bt[:], scalar=sm[:, 13:14], in1=gpost[:, n * 512:(n + 1) * 512], op0=ALU.mult, op1=ALU.mult),
                            [bk, (smk, 'rsq'), 'gpost'], [t1k])
                        P.op('pool', lambda e, t1=t1, xt=xt, n=n: e.tensor_tensor(out=xt[:, n * 512:(n + 1) * 512], in0=xt[:, n * 512:(n + 1) * 512], in1=t1[:], op=ALU.add),
                             [t1k] + xkeys, xkeys)
                    P.dma('sp', lambda e, xt=xt, r0=r0: e.dma_start(out=x1_scr[r0:r0 + 128, :], in_=xt[:]), reads=xkeys, writes=[('x1s', r0 // 128)])

                wst = {0: wo_X1(0)}
                for c in range(CPS):
                    if c + 1 < CPS:
                        wst[c + 1] = wo_X1(c + 1)
                    wo_X2(wst[c])
                    wo_X3(wst[c])

            emit_casts_until(0)
            P.barrier()
            _run_block(nc, P.take())

        with ExitStack() as p2:
            def sb2(name, shape, dt=F32):
                return sb(name, shape, dt, p2)

            WUP = sb2("WUP", [128, 8, 2 * DFF], BF16)
            WDN = sb2("WDN", [128, NJ, D], BF16)
            X1 = [sb2("X1_%d" % i, [128, 2, D]) for i in range(3)]
            HB = [sb2("HB_%d" % i, [128, 8, 258], BF16) for i in range(2)]
            NSP = 4
            actT = sb2("actT", [128, NJ + NSP, 256], BF16)

            def aslot(j, ja):
                return (j * NJ + ja) % (NJ + NSP)
            rtile("s3", [128, 16], F32, 4, p2)
            gpostffn = sb2("gpostffn", [128, D])
            cw = sb2("cw", [128, 2 * NJ, 3])
            cb = sb2("cb", [128, 2 * NJ])
            P.dma('sp', lambda e: e.dma_start(out=gpostffn[:], in_=rows[1:2, :].to_broadcast([128, D])), writes=['gpostffn'])
            P.dma('sp', lambda e: e.dma_start(out=cw[:], in_=cwl[:, :, :]), writes=['cw'])
            P.dma('sp', lambda e: e.dma_start(out=cb[:], in_=cbl[:, :]), writes=['cb'])
            rtile("xn", [128, D], BF16, 2, p2, "q_")
            rtile("acc", [128, 256], F32, 8, p2)
            rtile("ga", [128, 256], F32, 3, p2)
            rtile("yt", [128, 512], F32, 2, p2)

            NT2 = S // 256 if STOP == 'all' else 0

            def hbkeys(b):
                return [(('HB', b), 0), (('HB', b), 1), (('HB', b), 'L'), (('HB', b), 'R')]

            def front(j):
                r0 = j * 256
                x1t = X1[j % 3]
                x1k = ('X1', j % 3)
                hb = HB[j % 2]
                s3, s3k = nxt("s3")
                P.dma('sp', lambda e: e.dma_start(out=x1t[:], in_=x1_scr[r0:r0 + 256, :].rearrange("(c p) d -> p c d", p=128)),
                      reads=[('x1s', r0 // 128), ('x1s', r0 // 128 + 1)], writes=[(x1k, 0), (x1k, 1)])
                for c in range(2):
                    jt, jk = nxt("junk")
                    P.op('act', lambda e, c=c, jt=jt: e.activation(out=jt[:], in_=x1t[:, c, :], func=AF.Square, accum_out=s3[:, c:c + 1]),
                         [(x1k, c)], [(s3k, c), jk])
                rstd_from(s3[:, 0:2], s3[:, 4:6], 2, [(s3k, 0), (s3k, 1)], (s3k, 'rstd'), 1.0 / D)
                for c in range(2):
                    norm_to_T(x1t[:, c, :], (x1k, c), s3[:, 4 + c:5 + c], 8, hb, (('HB', j % 2), c), 1 + c * 128, rkey=(s3k, 'rstd'))

            def halo_copy(dst_b, dst_col, src_b, src_col, dkey, skey):
                P.op('dve', lambda e: e.tensor_copy(out=HB[dst_b][:, :, dst_col:dst_col + 1], in_=HB[src_b][:, :, src_col:src_col + 1]),
                     [skey], [dkey])

            def halo_zero(b, col, dkey):
                P.op('dve', lambda e: e.memset(HB[b][:, :, col:col + 1], 0.0), [], [dkey])

            def up(j, ja_list):
                hb = HB[j % 2]
                hkeys = hbkeys(j % 2)
                for ja in ja_list:
                    accs = []
                    bts = []
                    for half in range(2):
                        ch = half * NJ + ja
                        bt, bk = getbank()

                        def mm(e, bt=bt, ch=ch):
                            for kb in range(8):
                                ins = e.matmul(bt[:, 0:258], lhsT=WUP[:, kb, ch * 128:(ch + 1) * 128], rhs=hb[:, kb, :],
                                               start=(kb == 0), stop=(kb == 7))
                            return ins
                        P.op('pe', mm, hkeys + [('WUP', ch // 4)], [bk])
                        ac, ack = nxt("acc")
                        P.op('act', lambda e, ac=ac, bt=bt, ch=ch: e.activation(out=ac[:], in_=bt[:, 0:256], func=AF.Identity,
                                                                               bias=cb[:, ch:ch + 1], scale=cw[:, ch, 0:1]), [bk, 'cw', 'cb'], [ack])
                        accs.append((ac, ack))
                        bts.append((bt, bk, ch))
                    if pend_gate[0] is not None:
                        pend_gate[0]()
                        pend_gate[0] = None
                    for tap in (1, 2):
                        for half in range(2):
                            ac, ack = accs[half]
                            bt, bk, ch = bts[half]
                            P.op('dve', lambda e, ac=ac, bt=bt, ch=ch, tap=tap: e.scalar_tensor_tensor(
                                out=ac[:], in0=bt[:, tap:tap + 256], scalar=cw[:, ch, tap:tap + 1], in1=ac[:],
                                op0=ALU.mult, op1=ALU.add), [bk, ack, 'cw'], [ack])
                    pend_gate[0] = (lambda accs=accs, sl=aslot(j, ja): gate(accs, sl))

            pend_gate = [None]

            def gate(accs, sl):
                ga, gak = nxt("ga")
                P.op('act', lambda e, ga=ga, a=accs[0][0]: e.activation(out=ga[:], in_=a[:], func=AF.Gelu_apprx_tanh), [accs[0][1]], [gak])
                P.op('pool', lambda e, ga=ga, b=accs[1][0], sl=sl: e.tensor_tensor(out=actT[:, sl, :], in0=ga[:], in1=b[:], op=ALU.mult),
                     [gak, accs[1][1]], [('actT', sl)])

            def flush_gate():
                if pend_gate[0] is not None:
                    pend_gate[0]()
                    pend_gate[0] = None

            def down(j):
                r0 = j * 256
                x1t = X1[j % 3]
                x1k = ('X1', j % 3)
                for c in range(2):
                    pbs = []
                    s3, s3k = nxt("s3")
                    for n in range(2):
                        bt, bk = getbank()

                        def mmd(e, bt=bt, c=c, n=n):
                            for jj in range(NJ):
                                ins = e.matmul(bt[:], lhsT=actT[:, aslot(j, jj), c * 128:(c + 1) * 128], rhs=WDN[:, jj, n * 512:(n + 1) * 512],
                                               start=(jj == 0), stop=(jj == NJ - 1))
                            return ins
                        P.op('pe', mmd, [('actT', aslot(j, jj)) for jj in range(NJ)] + [('WDN', n)], [bk])
                        jt, jk = nxt("junk")
                        P.op('act', lambda e, bt=bt, n=n, jt=jt, s3=s3: e.activation(out=jt[:, 0:512], in_=bt[:], func=AF.Square, accum_out=s3[:, 8 + n:9 + n]),
                             [bk], [(s3k, 'q', n), jk])
                        pbs.append((bt, bk))
                    P.op('dve', lambda e, s3=s3: e.tensor_tensor(out=s3[:, 10:11], in0=s3[:, 8:9], in1=s3[:, 9:10], op=ALU.add), [(s3k, 'q', 0), (s3k, 'q', 1)], [(s3k, 'q2')])
                    rstd_from(s3[:, 10:11], s3[:, 11:12], 1, [(s3k, 'q2')], (s3k, 'rsq'), 1.0 / D)
                    for n in range(2):
                        bt, bk = pbs[n]
                        yt, ytk = nxt("yt")
                        P.op('dve', lambda e, bt=bt, yt=yt, n=n, s3=s3: e.scalar_tensor_tensor(
                            out=yt[:], in0=bt[:], scalar=s3[:, 11:12], in1=gpostffn[:, n * 512:(n + 1) * 512], op0=ALU.mult, op1=ALU.mult),
                            [bk, (s3k, 'rsq'), 'gpostffn'], [ytk])
                        P.op('pool', lambda e, yt=yt, c=c, n=n: e.tensor_tensor(out=x1t[:, c, n * 512:(n + 1) * 512],
                                                                                 in0=x1t[:, c, n * 512:(n + 1) * 512], in1=yt[:], op=ALU.add),
                             [ytk, (x1k, c)], [(x1k, c)])
                    rr = r0 + c * 128
                    P.dma('sp', lambda e, c=c, rr=rr: e.dma_start(out=out[rr:rr + 128, :], in_=x1t[:, c, :]),
                          reads=[(x1k, c)], writes=[('out', rr // 128)])

            def load_wup(g):
                P.dma('sp', lambda e: e.dma_start(out=WUP[:, :, g * 512:(g + 1) * 512],
                                                  in_=wup_b[:, g * 512:(g + 1) * 512].rearrange("(kb p) c -> p kb c", p=128)),
                      reads=[('wup_b', g)], writes=[('WUP', g)])

            def load_wdn(g):
                P.dma('sp', lambda e: e.dma_start(out=WDN[:, :, g * 512:(g + 1) * 512],
                                                  in_=wdn_b[:, g * 512:(g + 1) * 512].rearrange("(j p) c -> p j c", p=128)),
                      reads=[('wdn_b', g)], writes=[('WDN', g)])

            load_wup(0)
            load_wup(5)
            if NT2:
                front(0)
                halo_zero(0, 0, (('HB', 0), 'L'))
                front(1)
                halo_copy(0, 257, 1, 1, (('HB', 0), 'R'), (('HB', 1), 0))
            for g in [1, 6, 2, 7, 3, 8, 4, 9, 10]:
                load_wup(g)
            load_wdn(0)
            load_wdn(1)
            for j in range(NT2):
                up(j, range(NSP if j > 0 else 0, NJ))
                b, nb = j % 2, (j + 1) % 2
                if j + 1 < NT2:
                    halo_copy(nb, 0, b, 256, (('HB', nb), 'L'), (('HB', b), 1))
                if j + 2 < NT2:
                    front(j + 2)
                    halo_copy(nb, 257, b, 1, (('HB', nb), 'R'), (('HB', b), 0))
                elif j + 1 < NT2:
                    halo_zero(nb, 257, (('HB', nb), 'R'))
                if j + 1 < NT2:
                    up(j + 1, range(0, NSP))
                else:
                    flush_gate()
                down(j)
            P.barrier()
            _run_block(nc, P.take())
    return nc


_NC_CACHE = {}


def _consts():
    half = 64
    inv_freq = (1.0 / (np.float32(10000.0) ** (np.arange(half, dtype=np.float32) / np.float32(half)))).astype(np.float32)
    ang = (np.arange(S, dtype=np.float32)[:, None] * inv_freq[None, :]).astype(np.float32)
    cos_t = np.cos(ang).astype(np.float32)
    sin_t = np.sin(ang).astype(np.float32)
    idx = np.arange(128, dtype=np.float32)
    m = idx[:, None]
    n = idx[None, :]
    Ef = np.maximum(n - m, 0.0)
    Mf = (n >= m).astype(np.float32) * np.float32(SC)
    Eb = np.maximum(m - n, 0.0)
    Mb = (m > n).astype(np.float32) * np.float32(SC)
    N1 = np.broadcast_to(n + 1.0, (128, 128))
    N128 = np.broadcast_to(128.0 - n, (128, 128))
    dconst = np.ascontiguousarray(np.stack([Ef, Mf, Eb, Mb, N1, N128], axis=1).astype(np.float32))
    cvec = np.ascontiguousarray(np.stack([127.0 - idx, idx], axis=1).astype(np.float32))
    ident = np.eye(128, dtype=np.float32)
    return cos_t, sin_t, dconst, cvec, ident


def kernel(x, g_pre_mix, w_in, ret_decay_logit, sgu_ln_g, sgu_ln_b, w_spatial, b_spatial,
           w_ret_o, w_sgu_o, w_out, g_post_mix, g_pre_ffn, w_up, conv_w, conv_b, w_down, g_post_ffn):
    f = lambda a: np.ascontiguousarray(np.asarray(a, dtype=np.float32))
    x = f(x)
    cos_t, sin_t, dconst, cvec, ident = _consts()
    gcols = np.ascontiguousarray(np.concatenate([f(g_pre_mix).reshape(8, 128).T, f(g_pre_ffn).reshape(8, 128).T], axis=1))
    rows = np.ascontiguousarray(np.stack([f(g_post_mix), f(g_post_ffn), f(sgu_ln_g), f(sgu_ln_b)], axis=0))
    logits = f(ret_decay_logit).reshape(1, 8)
    bspT = np.ascontiguousarray(f(b_spatial).T)
    cwl = np.ascontiguousarray(f(conv_w).reshape(3, 2 * NJ, 128).transpose(2, 1, 0))
    cbl = np.ascontiguousarray(f(conv_b).reshape(2 * NJ, 128).T)
    if 'nc' not in _NC_CACHE:
        _NC_CACHE['nc'] = build_nc()
    nc = _NC_CACHE['nc']
    shared = dict(w_in=f(w_in), w_ret_o=f(w_ret_o), w_sgu_o=f(w_sgu_o), w_out=f(w_out), w_up=f(w_up), w_down=f(w_down),
                  gcols=gcols, rows=rows, logits=logits, wsp=f(w_spatial), bspT=bspT, cwl=cwl, cbl=cbl, ident=ident,
                  cos_t=cos_t, sin_t=sin_t, dconst=dconst, cvec=cvec)
    in_maps = [dict(shared, x=x[b]) for b in range(8)]
    res = run_bass_kernel_spmd(nc, in_maps, core_ids=list(range(8)))
    return np.stack([np.asarray(r["out"], dtype=np.float32) for r in res.results], axis=0)
```

```python
import os
import numpy as np
from contextlib import ExitStack
import concourse.bass as bass
import concourse.mybir as mybir
from concourse.bass_utils import run_bass_kernel_spmd

F32 = mybir.dt.float32
BF16 = mybir.dt.bfloat16
AF = mybir.ActivationFunctionType
ALU = mybir.AluOpType

S = 4096
D = 1024
NCHUNK = 32
T = 512
NST = S // T
CPS = T // 128
INW = 7168
DFF = 2816
NJ = DFF // 128
EPS = 1e-6
RING = 8
NW = 6
SC = 128.0 ** -0.5
STOP = os.environ.get('KSTOP', 'all')


class Prog:
    def __init__(self, semh):
        self.semh = semh
        self.cnt = {'pe': 0, 'act': 0, 'dve': 0, 'pool': 0}
        self.ring_n = {'sp': 0, 'pool': 0}
        self.ring_use = {'sp': [0] * RING, 'pool': [0] * RING}
        self.streams = ['pe', 'act', 'dve', 'pool', 'sp']
        self.seen = {s: {} for s in self.streams}
        self.ops = {s: [] for s in self.streams}
        self.wr = {}
        self.rd = {}
        self.clock = {}
        self.order = {}
        self.nissued = 0

    def _deps(self, stream, reads, writes, extra=()):
        toks = {}

        def add(tok):
            if tok is None:
                return
            k, v = tok
            if v > toks.get(k, 0):
                toks[k] = v
        for r in reads:
            add(self.wr.get(r))
        for w in writes:
            add(self.wr.get(w))
            for k, v in self.rd.get(w, {}).items():
                add((k, v))
        for t in extra:
            add(t)
        waits = []
        seen = self.seen[stream]
        for k, v in sorted(toks.items(), key=lambda kv: -self.order.get(kv, 0)):
            if k == ('e', 'pe') and stream == 'pe':
                continue
            if seen.get(k, 0) >= v:
                continue
            seen[k] = v
            waits.append((self.semh[k], v))
            for k2, v2 in self.clock.get((k, v), {}).items():
                if v2 > seen.get(k2, 0):
                    seen[k2] = v2
        return waits

    def _commit(self, tok, reads, writes, stream=None):
        k, v = tok
        if stream is not None:
            snap = dict(self.seen[stream])
            snap[k] = max(snap.get(k, 0), v)
            self.clock[tok] = snap
            self.nissued += 1
            self.order[tok] = self.nissued
        for r in reads:
            d = self.rd.setdefault(r, {})
            if v > d.get(k, 0):
                d[k] = v
        for w in writes:
            self.wr[w] = tok
            self.rd[w] = {}

    def op(self, eng, fn, reads=(), writes=()):
        waits = self._deps(eng, reads, writes)
        self.cnt[eng] += 1
        tok = (('e', eng), self.cnt[eng])
        self._commit(tok, reads, writes, eng)
        self.ops[eng].append((waits, fn, (self.semh[('e', eng)], 1)))

    def dma(self, st, fn, reads=(), writes=()):
        n = self.ring_n[st]
        self.ring_n[st] += 1
        slot = n % RING
        prev = self.ring_use[st][slot]
        key = ('r', st, slot)
        extra = [(key, 16 * prev)] if prev > 0 else []
        waits = self._deps(st, reads, writes, extra)
        self.ring_use[st][slot] = prev + 1
        tok = (key, 16 * (prev + 1))
        self._commit(tok, reads, writes, st)
        self.ops[st].append((waits, fn, (self.semh[key], 16)))

    def barrier(self):
        toks = []
        for e, c in self.cnt.items():
            if c > 0:
                toks.append((('e', e), c))
        for st in ('sp', 'pool'):
            for slot in range(RING):
                u = self.ring_use[st][slot]
                if u > 0:
                    toks.append((('r', st, slot), 16 * u))
        for s in self.streams:
            waits = []
            for k, v in toks:
                if self.seen[s].get(k, 0) >= v:
                    continue
                self.seen[s][k] = v
                waits.append((self.semh[k], v))
            self.ops[s].append((waits, None, None))

    def take(self):
        o = self.ops
        self.ops = {s: [] for s in self.streams}
        return o


def _replay(eng, ops, embed=False):
    for waits, fn, inc in ops:
        if fn is None or not embed or not waits:
            for semh, val in waits:
                eng.wait_ge(semh, val)
            if fn is not None:
                ins = fn(eng)
                ins.then_inc(inc[0], inc[1])
        else:
            for semh, val in waits[:-1]:
                eng.wait_ge(semh, val)
            ins = fn(eng)
            ins._wait_ge(waits[-1][0], waits[-1][1])
            ins.then_inc(inc[0], inc[1])


class _PEFirst:
    def __init__(self, eng, wait):
        self.eng = eng
        self.wait = wait

    def _wrap(self, ins):
        if self.wait is not None:
            ins._wait_ge(self.wait[0], self.wait[1])
            self.wait = None
        return ins

    def matmul(self, *a, **k):
        return self._wrap(self.eng.matmul(*a, **k))

    def transpose(self, *a, **k):
        return self._wrap(self.eng.transpose(*a, **k))


def _replay_pe(eng, ops):
    for waits, fn, inc in ops:
        if fn is None or not waits:
            for semh, val in waits:
                eng.wait_ge(semh, val)
            if fn is not None:
                fn(eng).then_inc(inc[0], inc[1])
        else:
            for semh, val in waits[:-1]:
                eng.wait_ge(semh, val)
            prox = _PEFirst(eng, waits[-1])
            ins = fn(prox)
            assert prox.wait is None
            ins.then_inc(inc[0], inc[1])


def _run_block(nc, ops):
    with nc.Block() as block:
        @block.tensor
        def _(e):
            _replay_pe(e, ops['pe'])

        @block.scalar
        def _(e):
            _replay(e, ops['act'], embed=True)

        @block.vector
        def _(e):
            _replay(e, ops['dve'], embed=True)

        @block.gpsimd
        def _(e):
            _replay(e, ops['pool'], embed=True)

        @block.sync
        def _(e):
            _replay(e, ops['sp'], embed=True)


def build_nc():
    nc = bass.Bass("TRN2", target_bir_lowering=False)

    def din(name, shape, dt=F32):
        return nc.dram_tensor(name, list(shape), dt, kind="ExternalInput").ap()

    x = din("x", [S, D])
    w_in = din("w_in", [D, INW])
    w_ret_o = din("w_ret_o", [D, D])
    w_sgu_o = din("w_sgu_o", [D, D])
    w_out = din("w_out", [D, D])
    w_up = din("w_up", [D, 2 * DFF])
    w_down = din("w_down", [DFF, D])
    gcols = din("gcols", [128, 16])
    rows = din("rows", [4, D])
    logits = din("logits", [1, 8])
    wsp = din("wsp", [4, 128, 128])
    bspT = din("bspT", [128, 4])
    cwl = din("cwl", [128, 2 * NJ, 3])
    cbl = din("cbl", [128, 2 * NJ])
    ident = din("ident", [128, 128])
    cos_t = din("cos_t", [S, 64])
    sin_t = din("sin_t", [S, 64])
    dconst = din("dconst", [128, 6, 128])
    cvec = din("cvec", [128, 2])
    out = nc.dram_tensor("out", [S, D], F32, kind="ExternalOutput").ap()

    def dscr(name, shape, dt=BF16):
        return nc.dram_tensor(name, list(shape), dt, kind="Internal").ap()

    w_in_b = dscr("w_in_b", [D, INW])
    wro_b = dscr("wro_b", [D, D])
    wso_b = dscr("wso_b", [D, D])
    wout_b = dscr("wout_b", [D, D])
    wup_b = dscr("wup_b", [D, 2 * DFF])
    wdn_b = dscr("wdn_b", [DFF, D])
    rb_scr = dscr("rb_scr", [NCHUNK, 128, D])
    x1_scr = dscr("x1_scr", [S, D], F32)
    v_scr = dscr("v_scr", [NCHUNK, 128, D])
    k_scr = dscr("k_scr", [NCHUNK, 128, 512])

    with ExitStack() as cm:
        def sb(name, shape, dt=F32, stack=cm):
            return stack.enter_context(nc.sbuf_tensor(name, list(shape), dt))

        semh = {}
        for e in ('pe', 'act', 'dve', 'pool'):
            semh[('e', e)] = cm.enter_context(nc.semaphore("s_" + e))
        for st in ('sp', 'pool'):
            for i in range(RING):
                semh[('r', st, i)] = cm.enter_context(nc.semaphore("r_%s%d" % (st, i)))
        P = Prog(semh)

        NBANK = 8
        banks = [cm.enter_context(nc.psum_tensor("pb%d" % i, [128, 512], F32)) for i in range(NBANK)]
        bank_i = [0]

        pinned = set()

        def getbank(pin=False):
            for _ in range(2 * NBANK):
                i = bank_i[0] % NBANK
                bank_i[0] += 1
                if i not in pinned:
                    break
            else:
                raise RuntimeError("no free PSUM bank")
            if pin:
                pinned.add(i)
            return banks[i], ('ps', i)

        def gettr():
            bt, bk = getbank()
            return bt[:].bitcast(BF16), bk

        identb = sb("identb", [128, 128], BF16)
        gcol = sb("gcol", [128, 16])
        rot = {}

        def rtile(name, shape, dt, n, stack, pfx=""):
            rot[name] = ([sb("%s%s%d" % (pfx, name, i), shape, dt, stack) for i in range(n)], [0])

        def nxt(name):
            tiles, ctr = rot[name]
            i = ctr[0] % len(tiles)
            ctr[0] += 1
            return tiles[i], (name, i)

        rtile("junk", [128, D], BF16, 2, cm)
        P.dma('pool', lambda e: e.dma_start(out=identb[:], in_=ident[:, :]), writes=['identb'])
        P.dma('sp', lambda e: e.dma_start(out=gcol[:], in_=gcols[:, :]), writes=['gcol'])

        def cast(dst, src, c0, c1, key):
            P.dma('pool', lambda e: e.dma_start(out=dst[:, c0:c1], in_=src[:, c0:c1]), writes=[key])

        pending = []
        for g in [4, 5, 0] + list(range(6, 14)):
            pending.append((w_in_b, w_in, g, 'w_in_b'))
        for g in range(2):
            pending.append((wro_b, w_ret_o, g, 'wro_b'))
            pending.append((wso_b, w_sgu_o, g, 'wso_b'))
        for g in range(2):
            pending.append((wout_b, w_out, g, 'wout_b'))
        for g in range(11):
            pending.append((wup_b, w_up, g, 'wup_b'))
        for g in range(2):
            pending.append((wdn_b, w_down, g, 'wdn_b'))

        def emit_casts(k):
            for _ in range(k):
                if pending:
                    dst, src, g, nm = pending.pop(0)
                    cast(dst, src, g * 512, (g + 1) * 512, (nm, g))
        def emit_casts_until(remaining):
            while len(pending) > remaining:
                emit_casts(1)

        with ExitStack() as p1:
            def sb1(name, shape, dt=F32):
                return sb(name, shape, dt, p1)

            lgt = sb1("lgt", [128, 8])
            lg = sb1("lg", [128, 8])
            dc = sb1("dc", [128, 8])
            zf = sb1("zf", [128, 4])
            zb = sb1("zb", [128, 4])
            cv = sb1("cv", [128, 2])
            dcs = sb1("dcs", [128, 6, 128])
            DT = sb1("DT", [128, 4, 128])
            lng = sb1("lng", [128, D])
            lnb = sb1("lnb", [128, D])
            gpost = sb1("gpost", [128, D])
            wspT = sb1("wspT", [128, 4, 128], BF16)
            bsp = sb1("bsp", [128, 4])
            hT = sb1("hT", [128, 8, T], BF16)
            mT = sb1("mT", [128, 8, T], BF16)
            yrT = sb1("yrT", [128, 8, T], BF16)
            gate8 = sb1("gate8", [128, CPS, D], BF16)
            qkT = sb1("qkT", [128, 2, CPS, 4, 128], BF16)
            qx = sb1("qx", [128, 2, CPS, 4, 128], BF16)
            ysT = qkT[:].rearrange("p a c h n -> p (a c h n)").rearrange("p (k t) -> p k t", k=8)
            tas = qx[:].rearrange("p a c h n -> p (a c h n)").rearrange("p (c d) -> p c d", c=CPS)
            kz = sb1("kz", [128, CPS, 512], BF16)
            v8 = sb1("v8", [128, CPS, D], BF16)
            R32 = sb1("R32", [128, 4, 256])
            Rbf = [sb1("Rbf%d" % i, [128, 4, 256], BF16) for i in range(2)]
            wbuf = [sb1("wbuf%d" % i, [128, 8, 512], BF16) for i in range(NW)]
            rtile("ss", [128, 8], F32, 2, p1)
            rtile("st6", [128, 4, 6], F32, 3, p1)
            rtile("mv", [128, 4, 2], F32, 3, p1)
            rtile("st12", [128, 2, 6], F32, 3, p1)
            rtile("mv2", [128, 2], F32, 3, p1)
            rtile("sm", [128, 16], F32, 4, p1)
            rtile("xst", [128, D], F32, 4, p1)
            rtile("xn", [128, D], BF16, 2, p1)
            rtile("tg", [128, 512], F32, 2, p1)
            rtile("krot", [128, 512], F32, 1, p1)
            rtile("ra", [128, 256], F32, 1, p1)
            rtile("rb", [128, 256], F32, 1, p1)
            rtile("tm512", [128, 512], BF16, 2, p1)
            rtile("PT", [128, 512], BF16, 2, p1)
            rtile("on", [128, D], F32, 2, p1)
            rtile("rbc", [128, D], BF16, 3, p1)
            rtile("tm1024", [128, D], BF16, 4, p1)
            rtile("gsv", [128, D], F32, 1, p1)
            rtile("m1", [128, 512], F32, 1, p1)
            rtile("m2", [128, 512], F32, 1, p1)
            rtile("x1t", [128, 512], F32, 1, p1)
            rtile("cs", [128, 2, CPS, 64], F32, 2, p1)

            on0 = rot["on"][0][0]
            gsv0 = rot["gsv"][0][0]
            XIF = on0[:, 0:512].rearrange("p (h n) -> p h n", h=4)
            XIB = on0[:, 512:1024].rearrange("p (h n) -> p h n", h=4)
            wspf = gsv0[:, 0:512].rearrange("p (h n) -> p h n", h=4)
            wspb = gsv0[:, 512:768].bitcast(BF16).rearrange("p (h n) -> p h n", h=4)
            P.dma('sp', lambda e: e.dma_start(out=lgt[:], in_=logits[0:1, :].to_broadcast([128, 8])), writes=['lgt'])
            P.dma('sp', lambda e: e.dma_start(out=cv[:], in_=cvec[:, :]), writes=['cv'])
            P.dma('sp', lambda e: e.dma_start(out=dcs[:], in_=dconst[:, :, :]), writes=['dcs'])
            P.dma('sp', lambda e: e.dma_start(out=lng[:], in_=rows[2:3, :].to_broadcast([128, D])), writes=['lng'])
            P.dma('sp', lambda e: e.dma_start(out=lnb[:], in_=rows[3:4, :].to_broadcast([128, D])), writes=['lnb'])
            P.dma('sp', lambda e: e.dma_start(out=gpost[:], in_=rows[0:1, :].to_broadcast([128, D])), writes=['gpost'])
            P.dma('sp', lambda e: e.dma_start(out=bsp[:], in_=bspT[:, :]), writes=['bsp'])
            P.dma('sp', lambda e: e.dma_start(out=wspf[:], in_=wsp.rearrange("g p q -> p g q")), writes=['wspf'])

            P.op('act', lambda e: e.activation(out=lg[:], in_=lgt[:], func=AF.Exp, scale=-1.0), ['lgt'], ['lg'])
            P.op('dve', lambda e: e.tensor_scalar_add(out=lg[:], in0=lg[:], scalar1=1.0), ['lg'], ['lg'])
            P.op('act', lambda e: e.activation(out=lg[:], in_=lg[:], func=AF.Ln), ['lg'], ['lg'])
            P.op('dve', lambda e: e.tensor_scalar_mul(out=lg[:], in0=lg[:], scalar1=-1.0), ['lg'], ['lg'])
            P.op('act', lambda e: e.activation(out=dc[:], in_=lg[:], func=AF.Exp, scale=128.0), ['lg'], ['dc'])
            for h in range(4):
                P.op('act', lambda e, h=h: e.activation(out=zf[:, h:h + 1], in_=cv[:, 0:1], func=AF.Exp, scale=lg[:, h:h + 1]), ['lg', 'cv'], ['zf'])
                P.op('act', lambda e, h=h: e.activation(out=zb[:, h:h + 1], in_=cv[:, 1:2], func=AF.Exp, scale=lg[:, 4 + h:5 + h]), ['lg', 'cv'], ['zb'])
                P.op('act', lambda e, h=h: e.activation(out=DT[:, h, :], in_=dcs[:, 0, :], func=AF.Exp, scale=lg[:, h:h + 1]), ['lg', 'dcs'], ['DT'])
                P.op('dve', lambda e, h=h: e.tensor_tensor(out=DT[:, h, :], in0=DT[:, h, :], in1=dcs[:, 1, :], op=ALU.mult), ['DT', 'dcs'], ['DT'])
                P.op('act', lambda e, h=h: e.activation(out=XIF[:, h, :], in_=dcs[:, 2, :], func=AF.Exp, scale=lg[:, 4 + h:5 + h]), ['lg', 'dcs', 'DT'], ['XIF'])
                P.op('dve', lambda e, h=h: e.tensor_tensor(out=XIF[:, h, :], in0=XIF[:, h, :], in1=dcs[:, 3, :], op=ALU.mult), ['XIF', 'dcs'], ['XIF'])
                P.op('dve', lambda e, h=h: e.tensor_tensor(out=DT[:, h, :], in0=DT[:, h, :], in1=XIF[:, h, :], op=ALU.add), ['XIF', 'DT'], ['DT'])
            for h in range(4):
                P.op('act', lambda e, h=h: e.activation(out=XIF[:, h, :], in_=dcs[:, 4, :], func=AF.Exp, scale=lg[:, h:h + 1]), ['lg', 'dcs', 'DT'], ['XIF'])
                P.op('act', lambda e, h=h: e.activation(out=XIB[:, h, :], in_=dcs[:, 5, :], func=AF.Exp, scale=lg[:, 4 + h:5 + h]), ['lg', 'dcs'], ['XIB'])
            P.op('dve', lambda e: e.tensor_scalar_mul(out=XIF[:], in0=XIF[:], scalar1=SC), ['XIF'], ['XIF'])
            P.op('dve', lambda e: e.tensor_scalar_mul(out=XIB[:], in0=XIB[:], scalar1=SC), ['XIB'], ['XIB'])
            DX = dcs[:].rearrange("p a n -> p (a n)").bitcast(BF16)[:, 0:1024].rearrange("p (v h n) -> p v h n", v=2, h=4)
            P.op('dve', lambda e: e.tensor_tensor(out=DX[:, 0, :, :], in0=XIF[:], in1=identb[:].unsqueeze(1).to_broadcast([128, 4, 128]), op=ALU.mult),
                 ['XIF', 'identb', 'dcs', 'DT', 'XIB'], ['dcs', 'DX'])
            P.op('dve', lambda e: e.tensor_tensor(out=DX[:, 1, :, :], in0=XIB[:], in1=identb[:].unsqueeze(1).to_broadcast([128, 4, 128]), op=ALU.mult),
                 ['XIB', 'identb', 'dcs', 'DX'], ['dcs', 'DX'])
            P.op('dve', lambda e: e.tensor_copy(out=wspb[:], in_=wspf[:]), ['wspf'], ['wspb'])
            trt, trk = gettr()
            P.op('pe', lambda e: [e.transpose(trt[:, g * 128:(g + 1) * 128], wspb[:, g, :], identb[:]) for g in range(4)][-1],
                 ['wspb', 'identb'], [trk])
            P.op('dve', lambda e: e.tensor_copy(out=wspT[:].rearrange("p g q -> p (g q)"), in_=trt[:, 0:512]), [trk], ['wspT'])

            P.op('dve', lambda e: e.memset(on0[:, 0:1], 0.0), [], ['XIF', 'XIB'] + [(('on', 0), h) for h in range(4)])
            P.op('dve', lambda e: e.memset(gsv0[:, 0:1], 0.0), [], ['wspf', 'wspb'] + [(('gsv', 0), n) for n in range(2)])
            def rstd_a(src_ap, dst_ap, rkeys, wkey, scale):
                P.op('dve', lambda e: e.tensor_scalar(out=dst_ap, in0=src_ap, scalar1=scale, scalar2=EPS,
                                                      op0=ALU.mult, op1=ALU.add), rkeys, [wkey])

            def rstd_b(dst_ap, wkey):
                P.op('act', lambda e: e.activation(out=dst_ap, in_=dst_ap, func=AF.Sqrt), [wkey], [wkey])

            def rstd_c(dst_ap, wkey):
                P.op('dve', lambda e: e.reciprocal(out=dst_ap, in_=dst_ap), [wkey], [wkey])

            def rstd_from(src_ap, dst_ap, n, rkeys, wkey, scale):
                rstd_a(src_ap, dst_ap, rkeys, wkey, scale)
                rstd_b(dst_ap, wkey)
                rstd_c(dst_ap, wkey)

            def load_cs(s):
                cst, csk = nxt("cs")
                P.dma('sp', lambda e: e.dma_start(out=cst[:, 0, :, :], in_=cos_t[s * T:(s + 1) * T, :].rearrange("(c p) j -> p c j", p=128)), writes=[csk])
                P.dma('sp', lambda e: e.dma_start(out=cst[:, 1, :, :], in_=sin_t[s * T:(s + 1) * T, :].rearrange("(c p) j -> p c j", p=128)), reads=[], writes=[(csk, 's')])
                return cst, [csk, (csk, 's')]

            def norm_to_T(xt, xk, ss_col, gofs, dstT, dkey, col0, ncols=128, npart=128, rkey='rstd'):
                xnt, xnk = nxt("xn")
                P.op('act', lambda e: e.activation(out=xnt[0:npart, :], in_=xt[0:npart, :], func=AF.Identity, scale=ss_col),
                     [xk, rkey], [xnk])
                trt, trk = gettr()

                def tr(e):
                    for kb in range(8):
                        ins = e.transpose(trt[:, kb * ncols:(kb + 1) * ncols], xnt[0:npart, kb * 128:(kb + 1) * 128],
                                          identb[0:npart, 0:npart])
                    return ins
                P.op('pe', tr, [xnk, 'identb'], [trk])
                P.op('dve', lambda e: e.tensor_tensor(
                    out=dstT[:, :, col0:col0 + ncols],
                    in0=trt[:, 0:8 * ncols].rearrange("p (k n) -> p k n", k=8),
                    in1=gcol[:, gofs:gofs + 8].unsqueeze(2).to_broadcast([128, 8, ncols]), op=ALU.mult),
                    [trk, 'gcol'], [dkey])

            def norm_part1(s):
                xts = []
                ss, ssk = nxt("ss")
                for c in range(CPS):
                    xt, xk = nxt("xst")
                    r0 = s * T + c * 128
                    P.dma('sp', lambda e, xt=xt, r0=r0: e.dma_start(out=xt[:], in_=x[r0:r0 + 128, :]), writes=[xk])
                    jt, jk = nxt("junk")
                    P.op('act', lambda e, xt=xt, c=c, jt=jt: e.activation(out=jt[:], in_=xt[:], func=AF.Square, accum_out=ss[:, c:c + 1]),
                         [xk], [(ssk, c), jk])
                    xts.append((xt, xk))
                rstd_from(ss[:, 0:CPS], ss[:, 0:CPS], CPS, [(ssk, c) for c in range(CPS)], (ssk, 'rstd'), 1.0 / D)
                return (xts, ss, ssk)

            def norm_part2(st):
                xts, ss, ssk = st
                for c in range(CPS):
                    xt, xk = xts[c]
                    norm_to_T(xt, xk, ss[:, c:c + 1], 0, hT, ('hT', c), c * 128, rkey=(ssk, 'rstd'))

            wl_n = [0]

            def load_w(scr, g, key, nkb=8, rows_rearr="(kb p) c -> p kb c"):
                slot = wl_n[0] % NW
                wl_n[0] += 1
                wt = wbuf[slot]
                assert key in P.wr, key
                P.dma('sp', lambda e: e.dma_start(out=wt[:, 0:nkb, :], in_=scr[:, g * 512:(g + 1) * 512].rearrange(rows_rearr, p=128)),
                      reads=[key], writes=[('wb', slot)])
                return wt, ('wb', slot)

            def proj(c, wt, wk, srcT, skey):
                bt, bk = getbank()

                def mm(e):
                    for kb in range(8):
                        ins = e.matmul(bt[:], lhsT=srcT[:, kb, c * 128:(c + 1) * 128], rhs=wt[:, kb, :],
                                       start=(kb == 0), stop=(kb == 7))
                    return ins
                P.op('pe', mm, [wk] + (skey if isinstance(skey, list) else [skey]), [bk])
                return bt, bk

            def rotary(bt, bk, cst, cskeys, c, dst, dkey):
                b4 = bt[:].rearrange("p (h t j) -> p h t j", h=4, t=2)
                d4 = dst[:].rearrange("p (h t j) -> p h t j", h=4, t=2)
                cosb = cst[:, 0, c, :].unsqueeze(1).to_broadcast([128, 4, 64])
                sinb = cst[:, 1, c, :].unsqueeze(1).to_broadcast([128, 4, 64])
                ra, rak = nxt("ra")
                rb_, rbk = nxt("rb")
                ra3 = ra[:].rearrange("p (h j) -> p h j", h=4)
                rb3 = rb_[:].rearrange("p (h j) -> p h j", h=4)
                P.op('dve', lambda e: e.tensor_tensor(out=ra3, in0=b4[:, :, 0, :], in1=cosb, op=ALU.mult), [bk] + cskeys, [rak])
                P.op('dve', lambda e: e.tensor_tensor(out=rb3, in0=b4[:, :, 1, :], in1=sinb, op=ALU.mult), [bk] + cskeys, [rbk])
                P.op('dve', lambda e: e.tensor_tensor(out=d4[:, :, 0, :], in0=ra3, in1=rb3, op=ALU.subtract), [rak, rbk], [(dkey, 0)])
                P.op('dve', lambda e: e.tensor_tensor(out=ra3, in0=b4[:, :, 0, :], in1=sinb, op=ALU.mult), [bk] + cskeys, [rak])
                P.op('dve', lambda e: e.tensor_tensor(out=rb3, in0=b4[:, :, 1, :], in1=cosb, op=ALU.mult), [bk] + cskeys, [rbk])
                P.op('dve', lambda e: e.tensor_tensor(out=d4[:, :, 1, :], in0=ra3, in1=rb3, op=ALU.add), [rak, rbk], [(dkey, 1)])
                return [(dkey, 0), (dkey, 1)]

            def k_stage(c, wt, wk, cst, cskeys, zcol, kz_eng='pool'):
                bt, bk = proj(c, wt, wk, hT, ('hT', c))
                kr, krk = nxt("krot")
                keys = rotary(bt, bk, cst, cskeys, c, kr, krk)
                P.op(kz_eng, lambda e: e.tensor_tensor(
                    out=kz[:, c, :].rearrange("p (h d) -> p h d", h=4),
                    in0=kr[:].rearrange("p (h d) -> p h d", h=4),
                    in1=zcol.unsqueeze(2).to_broadcast([128, 4, 128]), op=ALU.mult), keys + ['zf', 'zb'], [('kz', c)])
                return kr, keys

            def v_stage(c, n, wt, wk):
                bt, bk = proj(c, wt, wk, hT, ('hT', c))
                P.op('act', lambda e: e.activation(out=v8[:, c, n * 512:(n + 1) * 512], in_=bt[:], func=AF.Identity), [bk], [('v8', c, n)])

            def kv_update(c, dcol0, rbf_dst, rbf_key):
                for hp in range(2):
                    bt, bk = getbank()

                    def mm(e, hp=hp, bt=bt):
                        for hh in range(2):
                            h = hp * 2 + hh
                            ins = e.matmul(bt[:, hh * 256:(hh + 1) * 256], lhsT=kz[:, c, h * 128:(h + 1) * 128],
                                           rhs=v8[:, c, h * 256:(h + 1) * 256], start=True, stop=True)
                        return ins
                    P.op('pe', mm, [('kz', c), ('v8', c, hp)], [bk])
                    for hh in range(2):
                        h = hp * 2 + hh
                        P.op('dve', lambda e, h=h, hh=hh, bt=bt: e.scalar_tensor_tensor(
                            out=R32[:, h, :], in0=R32[:, h, :], scalar=dc[:, dcol0 + h:dcol0 + h + 1],
                            in1=bt[:, hh * 256:(hh + 1) * 256], op0=ALU.mult, op1=ALU.add), [bk, ('R32', h), 'dc'], [('R32', h)])
                    P.op('act', lambda e, hp=hp: e.activation(out=rbf_dst[:, 2 * hp:2 * hp + 2, :], in_=R32[:, 2 * hp:2 * hp + 2, :], func=AF.Identity),
                         [('R32', 2 * hp), ('R32', 2 * hp + 1)], [(rbf_key, hp)])

            P.op('dve', lambda e: e.memset(R32[:], 0.0), [], [('R32', h) for h in range(4)])
            P.op('pool', lambda e: e.memset(Rbf[0][:], 0.0), [], [(('Rbf', 0), 0), (('Rbf', 0), 1)])
            rpar = 0
            if STOP != 'setup':
                def load_w_direct(g):
                    slot = wl_n[0] % NW
                    wl_n[0] += 1
                    wt = wbuf[slot]
                    P.dma('pool', lambda e: e.dma_start(out=wt[:], in_=w_in[:, g * 512:(g + 1) * 512].rearrange("(kb p) c -> p kb c", p=128)),
                          writes=[('wb', slot)])
                    return wt, ('wb', slot)
                wkt, wkk = load_w_direct(1)
                wv0, wv0k = load_w_direct(2)
                wv1, wv1k = load_w_direct(3)
                cs_of = {}
                cs_of[NST - 1] = load_cs(NST - 1)
                norm_part2(norm_part1(NST - 1))
                xts_next = None
                pend = None

                def state_step(i):
                    nonlocal rpar
                    cur = Rbf[rpar]
                    P.dma('sp', lambda e, cur=cur, i=i: e.dma_start(out=rb_scr[i], in_=cur[:].rearrange("p h e -> p (h e)")),
                          reads=[(('Rbf', rpar), 0), (('Rbf', rpar), 1)], writes=[('rb_scr', i)])
                    if i > 0:
                        rpar ^= 1
                        kv_update(i % CPS, 4, Rbf[rpar], ('Rbf', rpar))

                for i in range(NCHUNK - 1, -1, -1):
                    s_, c = divmod(i, CPS)
                    if c == CPS - 1:
                        emit_casts(1)
                        if s_ > 0:
                            cs_of[s_ - 1] = load_cs(s_ - 1)
                            xts_next = norm_part1(s_ - 1)
                    cst, cskeys = cs_of[s_]
                    kr, kkeys = k_stage(c, wkt, wkk, cst, cskeys, zb[:, :], kz_eng='dve')
                    kt, ktk = nxt("tm512")
                    P.op('act', lambda e, kt=kt, kr=kr: e.activation(out=kt[:], in_=kr[:], func=AF.Identity), kkeys, [ktk])
                    P.dma('sp', lambda e, kt=kt, i=i: e.dma_start(out=k_scr[i], in_=kt[:]), reads=[ktk], writes=[('k_scr', i)])
                    v_stage(c, 0, wv0, wv0k)
                    v_stage(c, 1, wv1, wv1k)
                    P.dma('sp', lambda e, c=c, i=i: e.dma_start(out=v_scr[i], in_=v8[:, c, :]),
                          reads=[('v8', c, 0), ('v8', c, 1)], writes=[('v_scr', i)])
                    if c == 0 and s_ > 0:
                        norm_part2(xts_next)
                    if pend is not None:
                        state_step(pend)
                    pend = i
                state_step(pend)

            P.op('dve', lambda e: e.memset(R32[:], 0.0), [('R32', h) for h in range(4)], [('R32', h) for h in range(4)])
            P.op('pool', lambda e: e.memset(Rbf[0][:], 0.0), [], [(('Rbf', 0), 0), (('Rbf', 0), 1)])
            fpar = 0
            RUN1 = STOP not in ('setup', 'p0')
            gl1 = [(w_in_b, 4, 'w_in_b'), (w_in_b, 5, 'w_in_b'),
                   (w_in_b, 0, 'w_in_b'),
                   (w_in_b, 6, 'w_in_b'), (w_in_b, 7, 'w_in_b'),
                   (w_in_b, 8, 'w_in_b'), (w_in_b, 9, 'w_in_b'),
                   (w_in_b, 10, 'w_in_b'), (w_in_b, 11, 'w_in_b'),
                   (w_in_b, 12, 'w_in_b'), (w_in_b, 13, 'w_in_b'),
                   (wro_b, 0, 'wro_b'), (wso_b, 0, 'wso_b'),
                   (wro_b, 1, 'wro_b'), (wso_b, 1, 'wso_b'),
                   (wout_b, 0, 'wout_b'), (wout_b, 1, 'wout_b')]
            NG = len(gl1)
            glist = gl1 * NST
            loaded = {}

            def ensure_cast(key):
                while key not in P.wr and pending:
                    emit_casts(1)

            def Wg(idx, la=3, cap=None):
                hi = idx + la + 1
                if cap is not None:
                    hi = min(hi, cap + 1)
                for j in range(min(hi, len(glist))):
                    if j not in loaded:
                        for jj in range(j, min(j + 5, len(glist))):
                            ensure_cast((glist[jj][2], glist[jj][1]))
                        scr, g, nm = glist[j]
                        loaded[j] = load_w(scr, g, (nm, g))
                return loaded[idx]

            def pipe(n, stages, lag=1, filler=None):
                st = {}
                for t in range(n + (len(stages) - 1) * lag):
                    for k, f in enumerate(stages):
                        idx = t - k * lag
                        if 0 <= idx < n:
                            st[idx] = f(idx, st.get(idx))
                    if filler is not None:
                        filler(t)

            if RUN1:
                cs_cur = load_cs(0)
                norm_part2(norm_part1(0))
            for s in (range(NST) if RUN1 else []):
                W = lambda idx, cap=None, s=s: Wg(s * NG + idx, cap=(None if cap is None else s * NG + cap))
                cst, cskeys = cs_cur

                for n in range(2):
                    wt, wk = W(n)
                    for c in range(CPS):
                        bt, bk = proj(c, wt, wk, hT, ('hT', c))
                        tg, tgk = nxt("tg")
                        P.op('act', lambda e, tg=tg, bt=bt: e.activation(out=tg[:], in_=bt[:], func=AF.Tanh, scale=0.5), [bk], [tgk])
                        P.op('dve', lambda e, tg=tg, bt=bt, c=c, n=n: e.scalar_tensor_tensor(
                            out=gate8[:, c, n * 512:(n + 1) * 512], in0=tg[:], scalar=1.0, in1=bt[:],
                            op0=ALU.add, op1=ALU.mult), [bk, tgk], [('gate8', c, n)])

                wq, wqk = W(2)
                P.dma('sp', lambda e, s=s: e.dma_start(out=v8[:], in_=v_scr[s * CPS:(s + 1) * CPS].rearrange("c p e -> p c e")),
                      reads=[('v_scr', s * CPS + c) for c in range(CPS)], writes=[('v8', c, n) for c in range(CPS) for n in range(2)])

                def qk_A(idx, _):
                    which, c = divmod(idx, CPS)
                    if which == 0:
                        bt, bk = proj(c, wq, wqk, hT, ('hT', c))
                        kr, krk = nxt("krot")
                        keys = rotary(bt, bk, cst, cskeys, c, kr, krk)
                        qt, qtk = nxt("tm512")
                        P.op('act', lambda e, qt=qt, kr=kr: e.activation(out=qt[:], in_=kr[:], func=AF.Identity), keys, [qtk])
                    else:
                        i = s * CPS + c
                        qt, qtk = nxt("tm512")
                        P.dma('sp', lambda e, qt=qt, i=i: e.dma_start(out=qt[:], in_=k_scr[i]), reads=[('k_scr', i)], writes=[qtk])
                        P.op('pool', lambda e, qt=qt, c=c: e.tensor_tensor(
                            out=kz[:, c, :].rearrange("p (h d) -> p h d", h=4),
                            in0=qt[:].rearrange("p (h d) -> p h d", h=4),
                            in1=zf[:, :].unsqueeze(2).to_broadcast([128, 4, 128]), op=ALU.mult), [qtk, 'zf'], [('kz', c)])
                    return (qt, qtk)

                def qk_B(idx, stt):
                    which, c = divmod(idx, CPS)
                    qt, qtk = stt
                    if which == 0:
                        dsts = [(qkT[:, 0, c, :, :], ('qT', c), None), (qx[:, 0, c, :, :], ('qxf', c), 0), (qx[:, 1, c, :, :], ('qxb', c), 1)]
                        for k, (dst, dkey, v) in enumerate(dsts):
                            bt, bk = getbank()

                            def mmq(e, bt=bt, v=v):
                                for h in range(4):
                                    rhs = identb[:] if v is None else DX[:, v, h, :]
                                    ins = e.matmul(bt[:, h * 128:(h + 1) * 128], lhsT=qt[:, h * 128:(h + 1) * 128], rhs=rhs, start=True, stop=True)
                                return ins
                            P.op('pe', mmq, [qtk, 'identb', 'DX'], [bk])
                            eng = 'dve' if k == 2 else 'act'
                            if eng == 'act':
                                P.op('act', lambda e, bt=bt, dst=dst: e.activation(out=dst, in_=bt[:].rearrange("p (h n) -> p h n", h=4), func=AF.Identity),
                                     [bk], [dkey])
                            else:
                                P.op('dve', lambda e, bt=bt, dst=dst: e.tensor_copy(out=dst, in_=bt[:].rearrange("p (h n) -> p h n", h=4)), [bk], [dkey])
                    else:
                        trt, trk = gettr()
                        P.op('pe', lambda e, trt=trt, qt=qt: [e.transpose(trt[:, h * 128:(h + 1) * 128], qt[:, h * 128:(h + 1) * 128], identb[:])
                                                              for h in range(4)][-1], [qtk, 'identb'], [trk])
                        tr3 = trt[:, 0:512].rearrange("p (h n) -> p h n", h=4)
                        P.op('act', lambda e, tr3=tr3, c=c: e.activation(out=qkT[:, 1, c, :, :], in_=tr3, func=AF.Identity), [trk], [('kT', c)])
                    return stt
                pipe(2 * CPS, [qk_A, qk_B])
                W(3)
                emit_casts(2)

                def load_rb(c):
                    i = s * CPS + c
                    rt, rk = nxt("rbc")
                    P.dma('sp', lambda e, rt=rt, i=i: e.dma_start(out=rt[:], in_=rb_scr[i]), reads=[('rb_scr', i)], writes=[rk])
                    return rt, rk
                rbs = {0: load_rb(0), 1: load_rb(1)}

                def ret_A(c, _):
                    nonlocal fpar
                    i = s * CPS + c
                    if c + 2 < CPS:
                        rbs[c + 2] = load_rb(c + 2)
                    rt, rk = rbs[c]
                    rcur = Rbf[fpar]
                    rkeys = [(('Rbf', fpar), 0), (('Rbf', fpar), 1)]
                    if i < NCHUNK - 1:
                        fpar ^= 1
                        kv_update(c, 0, Rbf[fpar], ('Rbf', fpar))
                    stb, stk = getbank()
                    P.op('pe', lambda e, stb=stb, c=c: [e.matmul(stb[:, h * 128:(h + 1) * 128], lhsT=qkT[:, 1, c, h, :], rhs=qkT[:, 0, c, h, :],
                                                                 start=True, stop=True) for h in range(4)][-1],
                         [('qT', c), ('kT', c)], [stk])
                    pt, ptk = nxt("PT")
                    P.op('dve', lambda e, pt=pt, stb=stb: e.tensor_tensor(out=pt[:], in0=stb[:], in1=DT[:].rearrange("p h n -> p (h n)"), op=ALU.mult),
                         [stk, 'DT'], [ptk])
                    obanks = []
                    for hp in range(2):
                        ob, obk = getbank(pin=True)

                        def omm(e, hp=hp, ob=ob, pt=pt, rcur=rcur, c=c, rt=rt):
                            for hh in range(2):
                                h = hp * 2 + hh
                                o_ = ob[:, hh * 256:(hh + 1) * 256]
                                e.matmul(o_, lhsT=pt[:, h * 128:(h + 1) * 128], rhs=v8[:, c, h * 256:(h + 1) * 256], start=True, stop=False)
                                e.matmul(o_, lhsT=qx[:, 0, c, h, :], rhs=rcur[:, h, :], start=False, stop=False)
                                ins = e.matmul(o_, lhsT=qx[:, 1, c, h, :], rhs=rt[:, h * 256:(h + 1) * 256], start=False, stop=True)
                            return ins
                        P.op('pe', omm, [ptk, ('v8', c, hp), ('qxf', c), ('qxb', c), rk] + rkeys, [obk])
                        obanks.append((ob, obk))
                    return obanks

                def ret_B1(c, obanks):
                    sm, smk = nxt("sm")
                    mv, mvk = nxt("mv")
                    st6, s6k = nxt("st6")
                    on, onk = nxt("on")
                    for h in range(4):
                        ob, obk = obanks[h // 2]
                        osl = ob[:, (h % 2) * 256:(h % 2 + 1) * 256]
                        P.op('dve', lambda e, h=h, osl=osl: e.bn_stats(out=st6[:, h, :], in_=osl), [obk], [(s6k, h)])
                        P.op('dve', lambda e, h=h: e.bn_aggr(out=mv[:, h, :], in_=st6[:, h, :]), [(s6k, h)], [(mvk, h)])
                    rstd_a(mv[:, :, 1], sm[:, 0:4], [(mvk, h) for h in range(4)], (smk, 'rs4'), 1.0)
                    rstd_b(sm[:, 0:4], (smk, 'rs4'))
                    for h in range(4):
                        ob, obk = obanks[h // 2]
                        osl = ob[:, (h % 2) * 256:(h % 2 + 1) * 256]
                        P.op('dve', lambda e, h=h, osl=osl, on=on: e.scalar_tensor_tensor(
                            out=on[:, h * 256:(h + 1) * 256], in0=osl, scalar=mv[:, h, 0:1], in1=gate8[:, c, h * 256:(h + 1) * 256],
                            op0=ALU.subtract, op1=ALU.mult), [obk, (mvk, h), ('gate8', c, h // 2)], [(onk, h)])
                    for ob, obk in obanks:
                        pinned.discard(obk[1])
                    return (on, onk, sm, smk)

                def ret_B2(c, stt):
                    on, onk, sm, smk = stt
                    rstd_c(sm[:, 0:4], (smk, 'rs4'))
                    yt, ytk = nxt("tm1024")
                    for h in range(4):
                        P.op('act', lambda e, h=h, on=on, yt=yt: e.activation(out=yt[:, h * 256:(h + 1) * 256], in_=on[:, h * 256:(h + 1) * 256],
                                                                              func=AF.Identity, scale=sm[:, h:h + 1]),
                             [(onk, h), (smk, 'rs4')], [(ytk, h)] + ([ytk] if h == 0 else []))
                    return (yt, [(ytk, h) for h in range(4)] + [ytk])

                def ret_C(c, stt):
                    yt, ytks = stt
                    trt, trk = gettr()
                    P.op('pe', lambda e, trt=trt, yt=yt: [e.transpose(trt[:, kb * 128:(kb + 1) * 128], yt[:, kb * 128:(kb + 1) * 128], identb[:])
                                                          for kb in range(8)][-1], ytks + ['identb'], [trk])
                    P.op('act', lambda e, trt=trt, c=c: e.activation(out=yrT[:, :, c * 128:(c + 1) * 128],
                                                                     in_=trt[:].rearrange("p (k n) -> p k n", k=8), func=AF.Identity, scale=0.5),
                         [trk], [('yrT', c)])
                    return stt
                def u_item(c, n):
                    wt, wk = W(3 + n, cap=5)
                    bt, bk = proj(c, wt, wk, hT, ('hT', c))
                    P.op('act', lambda e, bt=bt, c=c, n=n: e.activation(out=gate8[:, c, n * 512:(n + 1) * 512], in_=bt[:], func=AF.Gelu_apprx_tanh),
                         [bk], [('gate8', c, n)])

                def u_fill(t):
                    c = t - 1
                    if 0 <= c < CPS:
                        u_item(c, 0)
                        u_item(c, 1)
                rst = {}
                for t in range(CPS + 2):
                    if 0 <= t - 1 < CPS:
                        rst[t - 1] = ret_B1(t - 1, rst[t - 1])
                    if t < CPS:
                        rst[t] = ret_A(t, None)
                    if 0 <= t - 1 < CPS:
                        rst[t - 1] = ret_B2(t - 1, rst[t - 1])
                    if 0 <= t - 2 < CPS:
                        ret_C(t - 2, rst[t - 2])
                    u_fill(t)

                w0, w0k = W(5)
                w1, w1k = W(6)
                qkkeys = [('qT', c) for c in range(CPS)] + [('kT', c) for c in range(CPS)]

                def sv_A1(c):
                    sm, smk = nxt("sm")
                    mv2, mv2k = nxt("mv2")
                    st12, s12k = nxt("st12")
                    gs, gsk = nxt("gsv")
                    for n, (wt, wk) in enumerate(((w0, w0k), (w1, w1k))):
                        bt, bk = proj(c, wt, wk, hT, ('hT', c))
                        P.op('act', lambda e, bt=bt, gs=gs, n=n: e.activation(out=gs[:, n * 512:(n + 1) * 512], in_=bt[:], func=AF.Gelu_apprx_tanh),
                             [bk], [(gsk, n)])
                        P.op('dve', lambda e, gs=gs, n=n: e.bn_stats(out=st12[:, n, :], in_=gs[:, n * 512:(n + 1) * 512]), [(gsk, n)], [(s12k, n)])
                    P.op('dve', lambda e: e.bn_aggr(out=mv2[:], in_=st12[:].rearrange("p a b -> p (a b)")), [(s12k, 0), (s12k, 1)], [mv2k])
                    rstd_a(mv2[:, 1:2], sm[:, 8:9], [mv2k], (smk, 'rs1'), 1.0)
                    P.op('dve', lambda e, gs=gs: e.scalar_tensor_tensor(out=gs[:], in0=gs[:], scalar=mv2[:, 0:1], in1=lng[:],
                                                                        op0=ALU.subtract, op1=ALU.mult), [(gsk, 0), (gsk, 1), mv2k, 'lng'], [(gsk, 0), (gsk, 1)])
                    return (sm, smk, mv2, mv2k, gs, gsk)

                def sv_A2(st):
                    sm, smk, mv2, mv2k, gs, gsk = st
                    rstd_b(sm[:, 8:9], (smk, 'rs1'))

                def sv_A3(st):
                    sm, smk, mv2, mv2k, gs, gsk = st
                    rstd_c(sm[:, 8:9], (smk, 'rs1'))
                    sv, svk = nxt("tm1024")
                    P.op('dve', lambda e, gs=gs, sv=sv: e.scalar_tensor_tensor(out=sv[:], in0=gs[:], scalar=sm[:, 8:9], in1=lnb[:],
                                                                               op0=ALU.mult, op1=ALU.add), [(gsk, 0), (gsk, 1), (smk, 'rs1'), 'lnb'], [svk])
                    return (sv, svk)

                def sv_B(c, stt):
                    sv, svk = stt
                    ys, ysk = nxt("tm1024")
                    for gp in range(2):
                        bt, bk = getbank()
                        P.op('pe', lambda e, bt=bt, sv=sv, gp=gp: [e.matmul(bt[:, gg * 256:(gg + 1) * 256], lhsT=wspT[:, gp * 2 + gg, :],
                                                                          rhs=sv[:, (gp * 2 + gg) * 256:(gp * 2 + gg + 1) * 256], start=True, stop=True)
                                                                 for gg in range(2)][-1], [svk, 'wspT'], [bk])
                        for gg in range(2):
                            g = gp * 2 + gg
                            P.op('dve', lambda e, bt=bt, ys=ys, g=g, gg=gg, c=c: e.scalar_tensor_tensor(
                                out=ys[:, g * 256:(g + 1) * 256], in0=bt[:, gg * 256:(gg + 1) * 256], scalar=bsp[:, g:g + 1],
                                in1=gate8[:, c, g * 256:(g + 1) * 256], op0=ALU.add, op1=ALU.mult),
                                [bk, 'bsp', ('gate8', c, g // 2)], [(ysk, g)] + ([ysk] if g == 0 else []))
                    return (ys, ysk)

                def sv_C(c, stt):
                    ys, ysk = stt
                    trt, trk = gettr()
                    P.op('pe', lambda e, trt=trt, ys=ys: [e.transpose(trt[:, kb * 128:(kb + 1) * 128], ys[:, kb * 128:(kb + 1) * 128], identb[:])
                                                          for kb in range(8)][-1], [(ysk, g) for g in range(4)] + [ysk, 'identb'], [trk])
                    P.op('act', lambda e, trt=trt, c=c: e.activation(out=ysT[:, :, c * 128:(c + 1) * 128],
                                                                     in_=trt[:].rearrange("p (k n) -> p k n", k=8), func=AF.Identity),
                         [trk] + qkkeys, [('ysT', c)] + qkkeys)
                    return stt
                qxkeys = [('qxf', c) for c in range(CPS)] + [('qxb', c) for c in range(CPS)]

                def gate_item(g, c):
                    wt, wk = W(7 + g, cap=9 if g < 3 else None)
                    bt, bk = proj(c, wt, wk, hT, ('hT', c))
                    if g < 2:
                        P.op('act', lambda e, bt=bt: e.activation(out=v8[:, c, g * 512:(g + 1) * 512], in_=bt[:], func=AF.Tanh, scale=0.5),
                             [bk], [('v8', c, g)])
                    else:
                        n = g - 2
                        P.op('act', lambda e, bt=bt: e.activation(out=tas[:, c, n * 512:(n + 1) * 512], in_=bt[:], func=AF.Tanh, scale=0.5),
                             [bk] + qxkeys, [('tas', c, n)] + qxkeys)
                g_items = [(g, c) for g in range(3) for c in range(CPS)]

                def g_fill(t):
                    for _ in range(2):
                        if g_items:
                            gate_item(*g_items.pop(0))
                sst = {}
                for t in range(CPS + 2):
                    a1 = sv_A1(t) if t < CPS else None
                    if 0 <= t - 2 < CPS:
                        sv_C(t - 2, sst[t - 2])
                    g_fill(t)
                    if a1 is not None:
                        sv_A2(a1)
                    if 0 <= t - 1 < CPS:
                        sst[t - 1] = sv_B(t - 1, sst[t - 1])
                    if a1 is not None:
                        sst[t] = sv_A3(a1)
                while g_items:
                    gate_item(*g_items.pop(0))
                for c in range(CPS):
                    gate_item(3, c)

                xts_next = None
                if s + 1 < NST:
                    cs_cur = load_cs(s + 1)
                    xts_next = norm_part1(s + 1)

                for n in range(2):
                    wr, wrk = W(11 + 2 * n)
                    ws, wsk = W(12 + 2 * n)
                    for c in range(CPS):
                        br, brk = proj(c, wr, wrk, yrT, ('yrT', c))
                        bs_, bsk = proj(c, ws, wsk, ysT, [('ysT', c)] + qkkeys)
                        m1, m1k = nxt("m1")
                        m2, m2k = nxt("m2")
                        P.op('dve', lambda e, m1=m1, br=br, c=c, n=n: e.scalar_tensor_tensor(
                            out=m1[:], in0=v8[:, c, n * 512:(n + 1) * 512], scalar=1.0, in1=br[:], op0=ALU.add, op1=ALU.mult),
                            [brk, ('v8', c, n)], [m1k])
                        P.op('dve', lambda e, m2=m2, bs_=bs_, c=c, n=n: e.scalar_tensor_tensor(
                            out=m2[:], in0=tas[:, c, n * 512:(n + 1) * 512], scalar=1.0, in1=bs_[:], op0=ALU.add, op1=ALU.mult),
                            [bsk, ('tas', c, n)] + qxkeys, [m2k])
                        P.op('pool', lambda e, m1=m1, m2=m2, c=c, n=n: e.tensor_tensor(out=gate8[:, c, n * 512:(n + 1) * 512], in0=m1[:], in1=m2[:], op=ALU.add),
                             [m1k, m2k], [('gate8', c, n)])
                for c in range(CPS):
                    trt, trk = gettr()
                    P.op('pe', lambda e, trt=trt, c=c: [e.transpose(trt[:, kb * 128:(kb + 1) * 128], gate8[:, c, kb * 128:(kb + 1) * 128], identb[:])
                                                        for kb in range(8)][-1], [('gate8', c, 0), ('gate8', c, 1), 'identb'], [trk])
                    P.op('act', lambda e, trt=trt, c=c: e.activation(out=mT[:, :, c * 128:(c + 1) * 128],
                                                                     in_=trt[:].rearrange("p (k n) -> p k n", k=8), func=AF.Identity, scale=0.5),
                         [trk], [('mT', c)])

                if xts_next is not None:
                    norm_part2(xts_next)

                wo0, wo0k = W(15)
                wo1, wo1k = W(16)
                def wo_X1(c):
                    r0 = s * T + c * 128
                    xt, xk = nxt("on")
                    sm, smk = nxt("sm")
                    P.dma('sp', lambda e, xt=xt, r0=r0: e.dma_start(out=xt[:], in_=x[r0:r0 + 128, :]), writes=[(xk, h) for h in range(4)])
                    pbs = []
                    for n, (wt, wk) in enumerate(((wo0, wo0k), (wo1, wo1k))):
                        bt, bk = proj(c, wt, wk, mT, ('mT', c))
                        jt, jk = nxt("junk")
                        P.op('act', lambda e, bt=bt, n=n, jt=jt, sm=sm: e.activation(out=jt[:, 0:512], in_=bt[:], func=AF.Square, accum_out=sm[:, 10 + n:11 + n]),
                             [bk], [(smk, 'ssq', n), jk])
                        pbs.append((bt, bk))
                    P.op('dve', lambda e, sm=sm: e.tensor_tensor(out=sm[:, 12:13], in0=sm[:, 10:11], in1=sm[:, 11:12], op=ALU.add), [(smk, 'ssq', 0), (smk, 'ssq', 1)], [(smk, 'ssq2')])
                    rstd_a(sm[:, 12:13], sm[:, 13:14], [(smk, 'ssq2')], (smk, 'rsq'), 1.0 / D)
                    return (r0, xt, xk, sm, smk, pbs)

                def wo_X2(st):
                    r0, xt, xk, sm, smk, pbs = st
                    rstd_b(sm[:, 13:14], (smk, 'rsq'))

                def wo_X3(st):
                    r0, xt, xk, sm, smk, pbs = st
                    rstd_c(sm[:, 13:14], (smk, 'rsq'))
                    xkeys = [(xk, h) for h in range(4)]
                    for n in range(2):
                        bt, bk = pbs[n]
                        t1, t1k = nxt("x1t")
                        P.op('dve', lambda e, bt=bt, t1=t1, n=n, sm=sm: e.scalar_tensor_tensor(
                            out=t1[:], in0=bt[:], scalar=sm[:, 13:14], in1=gpost[:, n * 512:(n + 1) * 512], op0=ALU.mult, op1=ALU.mult),
                            [bk, (smk, 'rsq'), 'gpost'], [t1k])
                        P.op('pool', lambda e, t1=t1, xt=xt, n=n: e.tensor_tensor(out=xt[:, n * 512:(n + 1) * 512], in0=xt[:, n * 512:(n + 1) * 512], in1=t1[:], op=ALU.add),
                             [t1k] + xkeys, xkeys)
                    P.dma('sp', lambda e, xt=xt, r0=r0: e.dma_start(out=x1_scr[r0:r0 + 128, :], in_=xt[:]), reads=xkeys, writes=[('x1s', r0 // 128)])

                wst = {0: wo_X1(0)}
                for c in range(CPS):
                    if c + 1 < CPS:
                        wst[c + 1] = wo_X1(c + 1)
                    wo_X2(wst[c])
                    wo_X3(wst[c])

            emit_casts_until(0)
            P.barrier()
            _run_block(nc, P.take())

        with ExitStack() as p2:
            def sb2(name, shape, dt=F32):
                return sb(name, shape, dt, p2)

            WUP = sb2("WUP", [128, 8, 2 * DFF], BF16)
            WDN = sb2("WDN", [128, NJ, D], BF16)
            X1 = [sb2("X1_%d" % i, [128, 2, D]) for i in range(3)]
            HB = [sb2("HB_%d" % i, [128, 8, 258], BF16) for i in range(2)]
            NSP = 4
            actT = sb2("actT", [128, NJ + NSP, 256], BF16)

            def aslot(j, ja):
                return (j * NJ + ja) % (NJ + NSP)
            rtile("s3", [128, 16], F32, 4, p2)
            gpostffn = sb2("gpostffn", [128, D])
            cw = sb2("cw", [128, 2 * NJ, 3])
            cb = sb2("cb", [128, 2 * NJ])
            P.dma('sp', lambda e: e.dma_start(out=gpostffn[:], in_=rows[1:2, :].to_broadcast([128, D])), writes=['gpostffn'])
            P.dma('sp', lambda e: e.dma_start(out=cw[:], in_=cwl[:, :, :]), writes=['cw'])
            P.dma('sp', lambda e: e.dma_start(out=cb[:], in_=cbl[:, :]), writes=['cb'])
            rtile("xn", [128, D], BF16, 2, p2, "q_")
            rtile("acc", [128, 256], F32, 8, p2)
            rtile("ga", [128, 256], F32, 3, p2)
            rtile("yt", [128, 512], F32, 2, p2)

            NT2 = S // 256 if STOP == 'all' else 0

            def hbkeys(b):
                return [(('HB', b), 0), (('HB', b), 1), (('HB', b), 'L'), (('HB', b), 'R')]

            def front(j):
                r0 = j * 256
                x1t = X1[j % 3]
                x1k = ('X1', j % 3)
                hb = HB[j % 2]
                s3, s3k = nxt("s3")
                P.dma('sp', lambda e: e.dma_start(out=x1t[:], in_=x1_scr[r0:r0 + 256, :].rearrange("(c p) d -> p c d", p=128)),
                      reads=[('x1s', r0 // 128), ('x1s', r0 // 128 + 1)], writes=[(x1k, 0), (x1k, 1)])
                for c in range(2):
                    jt, jk = nxt("junk")
                    P.op('act', lambda e, c=c, jt=jt: e.activation(out=jt[:], in_=x1t[:, c, :], func=AF.Square, accum_out=s3[:, c:c + 1]),
                         [(x1k, c)], [(s3k, c), jk])
                rstd_from(s3[:, 0:2], s3[:, 4:6], 2, [(s3k, 0), (s3k, 1)], (s3k, 'rstd'), 1.0 / D)
                for c in range(2):
                    norm_to_T(x1t[:, c, :], (x1k, c), s3[:, 4 + c:5 + c], 8, hb, (('HB', j % 2), c), 1 + c * 128, rkey=(s3k, 'rstd'))

            def halo_copy(dst_b, dst_col, src_b, src_col, dkey, skey):
                P.op('dve', lambda e: e.tensor_copy(out=HB[dst_b][:, :, dst_col:dst_col + 1], in_=HB[src_b][:, :, src_col:src_col + 1]),
                     [skey], [dkey])

            def halo_zero(b, col, dkey):
                P.op('dve', lambda e: e.memset(HB[b][:, :, col:col + 1], 0.0), [], [dkey])

            def up(j, ja_list):
                hb = HB[j % 2]
                hkeys = hbkeys(j % 2)
                for ja in ja_list:
                    accs = []
                    bts = []
                    for half in range(2):
                        ch = half * NJ + ja
                        bt, bk = getbank()

                        def mm(e, bt=bt, ch=ch):
                            for kb in range(8):
                                ins = e.matmul(bt[:, 0:258], lhsT=WUP[:, kb, ch * 128:(ch + 1) * 128], rhs=hb[:, kb, :],
                                               start=(kb == 0), stop=(kb == 7))
                            return ins
                        P.op('pe', mm, hkeys + [('WUP', ch // 4)], [bk])
                        ac, ack = nxt("acc")
                        P.op('act', lambda e, ac=ac, bt=bt, ch=ch: e.activation(out=ac[:], in_=bt[:, 0:256], func=AF.Identity,
                                                                               bias=cb[:, ch:ch + 1], scale=cw[:, ch, 0:1]), [bk, 'cw', 'cb'], [ack])
                        accs.append((ac, ack))
                        bts.append((bt, bk, ch))
                    if pend_gate[0] is not None:
                        pend_gate[0]()
                        pend_gate[0] = None
                    for tap in (1, 2):
                        for half in range(2):
                            ac, ack = accs[half]
                            bt, bk, ch = bts[half]
                            P.op('dve', lambda e, ac=ac, bt=bt, ch=ch, tap=tap: e.scalar_tensor_tensor(
                                out=ac[:], in0=bt[:, tap:tap + 256], scalar=cw[:, ch, tap:tap + 1], in1=ac[:],
                                op0=ALU.mult, op1=ALU.add), [bk, ack, 'cw'], [ack])
                    pend_gate[0] = (lambda accs=accs, sl=aslot(j, ja): gate(accs, sl))

            pend_gate = [None]

            def gate(accs, sl):
                ga, gak = nxt("ga")
                P.op('act', lambda e, ga=ga, a=accs[0][0]: e.activation(out=ga[:], in_=a[:], func=AF.Gelu_apprx_tanh), [accs[0][1]], [gak])
                P.op('pool', lambda e, ga=ga, b=accs[1][0], sl=sl: e.tensor_tensor(out=actT[:, sl, :], in0=ga[:], in1=b[:], op=ALU.mult),
                     [gak, accs[1][1]], [('actT', sl)])

            def flush_gate():
                if pend_gate[0] is not None:
                    pend_gate[0]()
                    pend_gate[0] = None

            def down(j):
                r0 = j * 256
                x1t = X1[j % 3]
                x1k = ('X1', j % 3)
                for c in range(2):
                    pbs = []
                    s3, s3k = nxt("s3")
                    for n in range(2):
                        bt, bk = getbank()

                        def mmd(e, bt=bt, c=c, n=n):
                            for jj in range(NJ):
                                ins = e.matmul(bt[:], lhsT=actT[:, aslot(j, jj), c * 128:(c + 1) * 128], rhs=WDN[:, jj, n * 512:(n + 1) * 512],
                                               start=(jj == 0), stop=(jj == NJ - 1))
                            return ins
                        P.op('pe', mmd, [('actT', aslot(j, jj)) for jj in range(NJ)] + [('WDN', n)], [bk])
                        jt, jk = nxt("junk")
                        P.op('act', lambda e, bt=bt, n=n, jt=jt, s3=s3: e.activation(out=jt[:, 0:512], in_=bt[:], func=AF.Square, accum_out=s3[:, 8 + n:9 + n]),
                             [bk], [(s3k, 'q', n), jk])
                        pbs.append((bt, bk))
                    P.op('dve', lambda e, s3=s3: e.tensor_tensor(out=s3[:, 10:11], in0=s3[:, 8:9], in1=s3[:, 9:10], op=ALU.add), [(s3k, 'q', 0), (s3k, 'q', 1)], [(s3k, 'q2')])
                    rstd_from(s3[:, 10:11], s3[:, 11:12], 1, [(s3k, 'q2')], (s3k, 'rsq'), 1.0 / D)
                    for n in range(2):
                        bt, bk = pbs[n]
                        yt, ytk = nxt("yt")
                        P.op('dve', lambda e, bt=bt, yt=yt, n=n, s3=s3: e.scalar_tensor_tensor(
                            out=yt[:], in0=bt[:], scalar=s3[:, 11:12], in1=gpostffn[:, n * 512:(n + 1) * 512], op0=ALU.mult, op1=ALU.mult),
                            [bk, (s3k, 'rsq'), 'gpostffn'], [ytk])
                        P.op('pool', lambda e, yt=yt, c=c, n=n: e.tensor_tensor(out=x1t[:, c, n * 512:(n + 1) * 512],
                                                                                 in0=x1t[:, c, n * 512:(n + 1) * 512], in1=yt[:], op=ALU.add),
                             [ytk, (x1k, c)], [(x1k, c)])
                    rr = r0 + c * 128
                    P.dma('sp', lambda e, c=c, rr=rr: e.dma_start(out=out[rr:rr + 128, :], in_=x1t[:, c, :]),
                          reads=[(x1k, c)], writes=[('out', rr // 128)])

            def load_wup(g):
                P.dma('sp', lambda e: e.dma_start(out=WUP[:, :, g * 512:(g + 1) * 512],
                                                  in_=wup_b[:, g * 512:(g + 1) * 512].rearrange("(kb p) c -> p kb c", p=128)),
                      reads=[('wup_b', g)], writes=[('WUP', g)])

            def load_wdn(g):
                P.dma('sp', lambda e: e.dma_start(out=WDN[:, :, g * 512:(g + 1) * 512],
                                                  in_=wdn_b[:, g * 512:(g + 1) * 512].rearrange("(j p) c -> p j c", p=128)),
                      reads=[('wdn_b', g)], writes=[('WDN', g)])

            load_wup(0)
            load_wup(5)
            if NT2:
                front(0)
                halo_zero(0, 0, (('HB', 0), 'L'))
                front(1)
                halo_copy(0, 257, 1, 1, (('HB', 0), 'R'), (('HB', 1), 0))
            for g in [1, 6, 2, 7, 3, 8, 4, 9, 10]:
                load_wup(g)
            load_wdn(0)
            load_wdn(1)
            for j in range(NT2):
                up(j, range(NSP if j > 0 else 0, NJ))
                b, nb = j % 2, (j + 1) % 2
                if j + 1 < NT2:
                    halo_copy(nb, 0, b, 256, (('HB', nb), 'L'), (('HB', b), 1))
                if j + 2 < NT2:
                    front(j + 2)
                    halo_copy(nb, 257, b, 1, (('HB', nb), 'R'), (('HB', b), 0))
                elif j + 1 < NT2:
                    halo_zero(nb, 257, (('HB', nb), 'R'))
                if j + 1 < NT2:
                    up(j + 1, range(0, NSP))
                else:
                    flush_gate()
                down(j)
            P.barrier()
            _run_block(nc, P.take())
    return nc


_NC_CACHE = {}


def _consts():
    half = 64
    inv_freq = (1.0 / (np.float32(10000.0) ** (np.arange(half, dtype=np.float32) / np.float32(half)))).astype(np.float32)
    ang = (np.arange(S, dtype=np.float32)[:, None] * inv_freq[None, :]).astype(np.float32)
    cos_t = np.cos(ang).astype(np.float32)
    sin_t = np.sin(ang).astype(np.float32)
    idx = np.arange(128, dtype=np.float32)
    m = idx[:, None]
    n = idx[None, :]
    Ef = np.maximum(n - m, 0.0)
    Mf = (n >= m).astype(np.float32) * np.float32(SC)
    Eb = np.maximum(m - n, 0.0)
    Mb = (m > n).astype(np.float32) * np.float32(SC)
    N1 = np.broadcast_to(n + 1.0, (128, 128))
    N128 = np.broadcast_to(128.0 - n, (128, 128))
    dconst = np.ascontiguousarray(np.stack([Ef, Mf, Eb, Mb, N1, N128], axis=1).astype(np.float32))
    cvec = np.ascontiguousarray(np.stack([127.0 - idx, idx], axis=1).astype(np.float32))
    ident = np.eye(128, dtype=np.float32)
    return cos_t, sin_t, dconst, cvec, ident


def kernel(x, g_pre_mix, w_in, ret_decay_logit, sgu_ln_g, sgu_ln_b, w_spatial, b_spatial,
           w_ret_o, w_sgu_o, w_out, g_post_mix, g_pre_ffn, w_up, conv_w, conv_b, w_down, g_post_ffn):
    f = lambda a: np.ascontiguousarray(np.asarray(a, dtype=np.float32))
    x = f(x)
    cos_t, sin_t, dconst, cvec, ident = _consts()
    gcols = np.ascontiguousarray(np.concatenate([f(g_pre_mix).reshape(8, 128).T, f(g_pre_ffn).reshape(8, 128).T], axis=1))
    rows = np.ascontiguousarray(np.stack([f(g_post_mix), f(g_post_ffn), f(sgu_ln_g), f(sgu_ln_b)], axis=0))
    logits = f(ret_decay_logit).reshape(1, 8)
    bspT = np.ascontiguousarray(f(b_spatial).T)
    cwl = np.ascontiguousarray(f(conv_w).reshape(3, 2 * NJ, 128).transpose(2, 1, 0))
    cbl = np.ascontiguousarray(f(conv_b).reshape(2 * NJ, 128).T)
    if 'nc' not in _NC_CACHE:
        _NC_CACHE['nc'] = build_nc()
    nc = _NC_CACHE['nc']
    shared = dict(w_in=f(w_in), w_ret_o=f(w_ret_o), w_sgu_o=f(w_sgu_o), w_out=f(w_out), w_up=f(w_up), w_down=f(w_down),
                  gcols=gcols, rows=rows, logits=logits, wsp=f(w_spatial), bspT=bspT, cwl=cwl, cbl=cbl, ident=ident,
                  cos_t=cos_t, sin_t=sin_t, dconst=dconst, cvec=cvec)
    in_maps = [dict(shared, x=x[b]) for b in range(8)]
    res = run_bass_kernel_spmd(nc, in_maps, core_ids=list(range(8)))
    return np.stack([np.asarray(r["out"], dtype=np.float32) for r in res.results], axis=0)
```

```python
import os
import numpy as np
from contextlib import ExitStack
import concourse.bass as bass
import concourse.mybir as mybir
from concourse.bass_utils import run_bass_kernel_spmd

F32 = mybir.dt.float32
BF16 = mybir.dt.bfloat16
AF = mybir.ActivationFunctionType
ALU = mybir.AluOpType

S = 4096
D = 1024
NCHUNK = 32
T = 512
NST = S // T
CPS = T // 128
INW = 7168
DFF = 2816
NJ = DFF // 128
EPS = 1e-6
RING = 4
NW = 6
SC = 128.0 ** -0.5
STOP = os.environ.get('KSTOP', 'all')


class Prog:
    def __init__(self, semh):
        self.semh = semh
        self.cnt = {'pe': 0, 'act': 0, 'dve': 0, 'pool': 0}
        self.ring_n = {'sp': 0, 'pool': 0}
        self.ring_use = {'sp': [0] * RING, 'pool': [0] * RING}
        self.streams = ['pe', 'act', 'dve', 'pool', 'sp']
        self.seen = {s: {} for s in self.streams}
        self.ops = {s: [] for s in self.streams}
        self.wr = {}
        self.rd = {}
        self.clock = {}
        self.order = {}
        self.nissued = 0

    def _deps(self, stream, reads, writes, extra=()):
        toks = {}

        def add(tok):
            if tok is None:
                return
            k, v = tok
            if v > toks.get(k, 0):
                toks[k] = v
        for r in reads:
            add(self.wr.get(r))
        for w in writes:
            add(self.wr.get(w))
            for k, v in self.rd.get(w, {}).items():
                add((k, v))
        for t in extra:
            add(t)
        waits = []
        seen = self.seen[stream]
        for k, v in sorted(toks.items(), key=lambda kv: -self.order.get(kv, 0)):
            if k == ('e', 'pe') and stream == 'pe':
                continue
            if seen.get(k, 0) >= v:
                continue
            seen[k] = v
            waits.append((self.semh[k], v))
            for k2, v2 in self.clock.get((k, v), {}).items():
                if v2 > seen.get(k2, 0):
                    seen[k2] = v2
        return waits

    def _commit(self, tok, reads, writes, stream=None):
        k, v = tok
        if stream is not None:
            snap = dict(self.seen[stream])
            snap[k] = max(snap.get(k, 0), v)
            self.clock[tok] = snap
            self.nissued += 1
            self.order[tok] = self.nissued
        for r in reads:
            d = self.rd.setdefault(r, {})
            if v > d.get(k, 0):
                d[k] = v
        for w in writes:
            self.wr[w] = tok
            self.rd[w] = {}

    def op(self, eng, fn, reads=(), writes=()):
        waits = self._deps(eng, reads, writes)
        self.cnt[eng] += 1
        tok = (('e', eng), self.cnt[eng])
        self._commit(tok, reads, writes, eng)
        self.ops[eng].append((waits, fn, (self.semh[('e', eng)], 1)))

    def dma(self, st, fn, reads=(), writes=()):
        n = self.ring_n[st]
        self.ring_n[st] += 1
        slot = n % RING
        prev = self.ring_use[st][slot]
        key = ('r', st, slot)
        extra = [(key, 16 * prev)] if prev > 0 else []
        waits = self._deps(st, reads, writes, extra)
        self.ring_use[st][slot] = prev + 1
        tok = (key, 16 * (prev + 1))
        self._commit(tok, reads, writes, st)
        self.ops[st].append((waits, fn, (self.semh[key], 16)))

    def barrier(self):
        toks = []
        for e, c in self.cnt.items():
            if c > 0:
                toks.append((('e', e), c))
        for st in ('sp', 'pool'):
            for slot in range(RING):
                u = self.ring_use[st][slot]
                if u > 0:
                    toks.append((('r', st, slot), 16 * u))
        for s in self.streams:
            waits = []
            for k, v in toks:
                if self.seen[s].get(k, 0) >= v:
                    continue
                self.seen[s][k] = v
                waits.append((self.semh[k], v))
            self.ops[s].append((waits, None, None))

    def take(self):
        o = self.ops
        self.ops = {s: [] for s in self.streams}
        return o


def _replay(eng, ops, embed=False):
    for waits, fn, inc in ops:
        if fn is None or not embed or not waits:
            for semh, val in waits:
                eng.wait_ge(semh, val)
            if fn is not None:
                ins = fn(eng)
                ins.then_inc(inc[0], inc[1])
        else:
            for semh, val in waits[:-1]:
                eng.wait_ge(semh, val)
            ins = fn(eng)
            ins._wait_ge(waits[-1][0], waits[-1][1])
            ins.then_inc(inc[0], inc[1])


class _PEFirst:
    def __init__(self, eng, wait):
        self.eng = eng
        self.wait = wait

    def _wrap(self, ins):
        if self.wait is not None:
            ins._wait_ge(self.wait[0], self.wait[1])
            self.wait = None
        return ins

    def matmul(self, *a, **k):
        return self._wrap(self.eng.matmul(*a, **k))

    def transpose(self, *a, **k):
        return self._wrap(self.eng.transpose(*a, **k))


def _replay_pe(eng, ops):
    for waits, fn, inc in ops:
        if fn is None or not waits:
            for semh, val in waits:
                eng.wait_ge(semh, val)
            if fn is not None:
                fn(eng).then_inc(inc[0], inc[1])
        else:
            for semh, val in waits[:-1]:
                eng.wait_ge(semh, val)
            prox = _PEFirst(eng, waits[-1])
            ins = fn(prox)
            assert prox.wait is None
            ins.then_inc(inc[0], inc[1])


def _run_block(nc, ops):
    with nc.Block() as block:
        @block.tensor
        def _(e):
            _replay_pe(e, ops['pe'])

        @block.scalar
        def _(e):
            _replay(e, ops['act'], embed=True)

        @block.vector
        def _(e):
            _replay(e, ops['dve'], embed=True)

        @block.gpsimd
        def _(e):
            _replay(e, ops['pool'], embed=True)

        @block.sync
        def _(e):
            _replay(e, ops['sp'], embed=True)


def build_nc():
    nc = bass.Bass("TRN2", target_bir_lowering=False)

    def din(name, shape, dt=F32):
        return nc.dram_tensor(name, list(shape), dt, kind="ExternalInput").ap()

    x = din("x", [S, D])
    w_in = din("w_in", [D, INW])
    w_ret_o = din("w_ret_o", [D, D])
    w_sgu_o = din("w_sgu_o", [D, D])
    w_out = din("w_out", [D, D])
    w_up = din("w_up", [D, 2 * DFF])
    w_down = din("w_down", [DFF, D])
    gcols = din("gcols", [128, 16])
    rows = din("rows", [4, D])
    logits = din("logits", [1, 8])
    wsp = din("wsp", [4, 128, 128])
    bspT = din("bspT", [128, 4])
    cwl = din("cwl", [128, 2 * NJ, 3])
    cbl = din("cbl", [128, 2 * NJ])
    ident = din("ident", [128, 128])
    cos_t = din("cos_t", [S, 64])
    sin_t = din("sin_t", [S, 64])
    dconst = din("dconst", [128, 6, 128])
    cvec = din("cvec", [128, 2])
    out = nc.dram_tensor("out", [S, D], F32, kind="ExternalOutput").ap()

    def dscr(name, shape, dt=BF16):
        return nc.dram_tensor(name, list(shape), dt, kind="Internal").ap()

    w_in_b = dscr("w_in_b", [D, INW])
    wro_b = dscr("wro_b", [D, D])
    wso_b = dscr("wso_b", [D, D])
    wout_b = dscr("wout_b", [D, D])
    wup_b = dscr("wup_b", [D, 2 * DFF])
    wdn_b = dscr("wdn_b", [DFF, D])
    rb_scr = dscr("rb_scr", [NCHUNK, 128, D])
    x1_scr = dscr("x1_scr", [S, D], F32)
    v_scr = dscr("v_scr", [NCHUNK, 128, D])
    k_scr = dscr("k_scr", [NCHUNK, 128, 512])

    with ExitStack() as cm:
        def sb(name, shape, dt=F32, stack=cm):
            return stack.enter_context(nc.sbuf_tensor(name, list(shape), dt))

        semh = {}
        for e in ('pe', 'act', 'dve', 'pool'):
            semh[('e', e)] = cm.enter_context(nc.semaphore("s_" + e))
        for st in ('sp', 'pool'):
            for i in range(RING):
                semh[('r', st, i)] = cm.enter_context(nc.semaphore("r_%s%d" % (st, i)))
        P = Prog(semh)

        NBANK = 8
        banks = [cm.enter_context(nc.psum_tensor("pb%d" % i, [128, 512], F32)) for i in range(NBANK)]
        bank_i = [0]

        pinned = set()

        def getbank(pin=False):
            for _ in range(2 * NBANK):
                i = bank_i[0] % NBANK
                bank_i[0] += 1
                if i not in pinned:
                    break
            else:
                raise RuntimeError("no free PSUM bank")
            if pin:
                pinned.add(i)
            return banks[i], ('ps', i)

        def gettr():
            bt, bk = getbank()
            return bt[:].bitcast(BF16), bk

        identb = sb("identb", [128, 128], BF16)
        gcol = sb("gcol", [128, 16])
        rot = {}

        def rtile(name, shape, dt, n, stack, pfx=""):
            rot[name] = ([sb("%s%s%d" % (pfx, name, i), shape, dt, stack) for i in range(n)], [0])

        def nxt(name):
            tiles, ctr = rot[name]
            i = ctr[0] % len(tiles)
            ctr[0] += 1
            return tiles[i], (name, i)

        rtile("junk", [128, D], BF16, 2, cm)
        P.dma('pool', lambda e: e.dma_start(out=identb[:], in_=ident[:, :]), writes=['identb'])
        P.dma('sp', lambda e: e.dma_start(out=gcol[:], in_=gcols[:, :]), writes=['gcol'])

        def cast(dst, src, c0, c1, key):
            P.dma('pool', lambda e: e.dma_start(out=dst[:, c0:c1], in_=src[:, c0:c1]), writes=[key])

        pending = []
        for g in [4, 5, 0] + list(range(6, 14)):
            pending.append((w_in_b, w_in, g, 'w_in_b'))
        for g in range(2):
            pending.append((wro_b, w_ret_o, g, 'wro_b'))
            pending.append((wso_b, w_sgu_o, g, 'wso_b'))
        for g in range(2):
            pending.append((wout_b, w_out, g, 'wout_b'))
        for g in range(11):
            pending.append((wup_b, w_up, g, 'wup_b'))
        for g in range(2):
            pending.append((wdn_b, w_down, g, 'wdn_b'))

        def emit_casts(k):
            for _ in range(k):
                if pending:
                    dst, src, g, nm = pending.pop(0)
                    cast(dst, src, g * 512, (g + 1) * 512, (nm, g))
        def emit_casts_until(remaining):
            while len(pending) > remaining:
                emit_casts(1)

        with ExitStack() as p1:
            def sb1(name, shape, dt=F32):
                return sb(name, shape, dt, p1)

            lgt = sb1("lgt", [128, 8])
            lg = sb1("lg", [128, 8])
            dc = sb1("dc", [128, 8])
            zf = sb1("zf", [128, 4])
            zb = sb1("zb", [128, 4])
            cv = sb1("cv", [128, 2])
            dcs = sb1("dcs", [128, 6, 128])
            DT = sb1("DT", [128, 4, 128])
            lng = sb1("lng", [128, D])
            lnb = sb1("lnb", [128, D])
            gpost = sb1("gpost", [128, D])
            wspT = sb1("wspT", [128, 4, 128], BF16)
            bsp = sb1("bsp", [128, 4])
            hT = sb1("hT", [128, 8, T], BF16)
            mT = sb1("mT", [128, 8, T], BF16)
            yrT = sb1("yrT", [128, 8, T], BF16)
            gate8 = sb1("gate8", [128, CPS, D], BF16)
            qkT = sb1("qkT", [128, 2, CPS, 4, 128], BF16)
            qx = sb1("qx", [128, 2, CPS, 4, 128], BF16)
            ysT = qkT[:].rearrange("p a c h n -> p (a c h n)").rearrange("p (k t) -> p k t", k=8)
            tas = qx[:].rearrange("p a c h n -> p (a c h n)").rearrange("p (c d) -> p c d", c=CPS)
            kz = sb1("kz", [128, CPS, 512], BF16)
            v8 = sb1("v8", [128, CPS, D], BF16)
            R32 = sb1("R32", [128, 4, 256])
            Rbf = [sb1("Rbf%d" % i, [128, 4, 256], BF16) for i in range(2)]
            wbuf = [sb1("wbuf%d" % i, [128, 8, 512], BF16) for i in range(NW)]
            rtile("ss", [128, 8], F32, 2, p1)
            rtile("st6", [128, 4, 6], F32, 3, p1)
            rtile("mv", [128, 4, 2], F32, 3, p1)
            rtile("st12", [128, 2, 6], F32, 3, p1)
            rtile("mv2", [128, 2], F32, 3, p1)
            rtile("sm", [128, 16], F32, 4, p1)
            rtile("xst", [128, D], F32, 4, p1)
            rtile("xn", [128, D], BF16, 2, p1)
            rtile("tg", [128, 512], F32, 2, p1)
            rtile("krot", [128, 512], F32, 1, p1)
            rtile("ra", [128, 256], F32, 1, p1)
            rtile("rb", [128, 256], F32, 1, p1)
            rtile("tm512", [128, 512], BF16, 2, p1)
            rtile("PT", [128, 512], BF16, 2, p1)
            rtile("on", [128, D], F32, 2, p1)
            rtile("rbc", [128, D], BF16, 3, p1)
            rtile("tm1024", [128, D], BF16, 4, p1)
            rtile("gsv", [128, D], F32, 1, p1)
            rtile("m1", [128, 512], F32, 1, p1)
            rtile("m2", [128, 512], F32, 1, p1)
            rtile("x1t", [128, 512], F32, 1, p1)
            rtile("cs", [128, 2, CPS, 64], F32, 2, p1)

            on0 = rot["on"][0][0]
            gsv0 = rot["gsv"][0][0]
            XIF = on0[:, 0:512].rearrange("p (h n) -> p h n", h=4)
            XIB = on0[:, 512:1024].rearrange("p (h n) -> p h n", h=4)
            wspf = gsv0[:, 0:512].rearrange("p (h n) -> p h n", h=4)
            wspb = gsv0[:, 512:768].bitcast(BF16).rearrange("p (h n) -> p h n", h=4)
            P.dma('sp', lambda e: e.dma_start(out=lgt[:], in_=logits[0:1, :].to_broadcast([128, 8])), writes=['lgt'])
            P.dma('sp', lambda e: e.dma_start(out=cv[:], in_=cvec[:, :]), writes=['cv'])
            P.dma('sp', lambda e: e.dma_start(out=dcs[:], in_=dconst[:, :, :]), writes=['dcs'])
            P.dma('sp', lambda e: e.dma_start(out=lng[:], in_=rows[2:3, :].to_broadcast([128, D])), writes=['lng'])
            P.dma('sp', lambda e: e.dma_start(out=lnb[:], in_=rows[3:4, :].to_broadcast([128, D])), writes=['lnb'])
            P.dma('sp', lambda e: e.dma_start(out=gpost[:], in_=rows[0:1, :].to_broadcast([128, D])), writes=['gpost'])
            P.dma('sp', lambda e: e.dma_start(out=bsp[:], in_=bspT[:, :]), writes=['bsp'])
            P.dma('sp', lambda e: e.dma_start(out=wspf[:], in_=wsp.rearrange("g p q -> p g q")), writes=['wspf'])

            P.op('act', lambda e: e.activation(out=lg[:], in_=lgt[:], func=AF.Exp, scale=-1.0), ['lgt'], ['lg'])
            P.op('dve', lambda e: e.tensor_scalar_add(out=lg[:], in0=lg[:], scalar1=1.0), ['lg'], ['lg'])
            P.op('act', lambda e: e.activation(out=lg[:], in_=lg[:], func=AF.Ln), ['lg'], ['lg'])
            P.op('dve', lambda e: e.tensor_scalar_mul(out=lg[:], in0=lg[:], scalar1=-1.0), ['lg'], ['lg'])
            P.op('act', lambda e: e.activation(out=dc[:], in_=lg[:], func=AF.Exp, scale=128.0), ['lg'], ['dc'])
            for h in range(4):
                P.op('act', lambda e, h=h: e.activation(out=zf[:, h:h + 1], in_=cv[:, 0:1], func=AF.Exp, scale=lg[:, h:h + 1]), ['lg', 'cv'], ['zf'])
                P.op('act', lambda e, h=h: e.activation(out=zb[:, h:h + 1], in_=cv[:, 1:2], func=AF.Exp, scale=lg[:, 4 + h:5 + h]), ['lg', 'cv'], ['zb'])
                P.op('act', lambda e, h=h: e.activation(out=DT[:, h, :], in_=dcs[:, 0, :], func=AF.Exp, scale=lg[:, h:h + 1]), ['lg', 'dcs'], ['DT'])
                P.op('dve', lambda e, h=h: e.tensor_tensor(out=DT[:, h, :], in0=DT[:, h, :], in1=dcs[:, 1, :], op=ALU.mult), ['DT', 'dcs'], ['DT'])
                P.op('act', lambda e, h=h: e.activation(out=XIF[:, h, :], in_=dcs[:, 2, :], func=AF.Exp, scale=lg[:, 4 + h:5 + h]), ['lg', 'dcs', 'DT'], ['XIF'])
                P.op('dve', lambda e, h=h: e.tensor_tensor(out=XIF[:, h, :], in0=XIF[:, h, :], in1=dcs[:, 3, :], op=ALU.mult), ['XIF', 'dcs'], ['XIF'])
                P.op('dve', lambda e, h=h: e.tensor_tensor(out=DT[:, h, :], in0=DT[:, h, :], in1=XIF[:, h, :], op=ALU.add), ['XIF', 'DT'], ['DT'])
            for h in range(4):
                P.op('act', lambda e, h=h: e.activation(out=XIF[:, h, :], in_=dcs[:, 4, :], func=AF.Exp, scale=lg[:, h:h + 1]), ['lg', 'dcs', 'DT'], ['XIF'])
                P.op('act', lambda e, h=h: e.activation(out=XIB[:, h, :], in_=dcs[:, 5, :], func=AF.Exp, scale=lg[:, 4 + h:5 + h]), ['lg', 'dcs'], ['XIB'])
            P.op('dve', lambda e: e.tensor_scalar_mul(out=XIF[:], in0=XIF[:], scalar1=SC), ['XIF'], ['XIF'])
            P.op('dve', lambda e: e.tensor_scalar_mul(out=XIB[:], in0=XIB[:], scalar1=SC), ['XIB'], ['XIB'])
            DX = dcs[:].rearrange("p a n -> p (a n)").bitcast(BF16)[:, 0:1024].rearrange("p (v h n) -> p v h n", v=2, h=4)
            P.op('dve', lambda e: e.tensor_tensor(out=DX[:, 0, :, :], in0=XIF[:], in1=identb[:].unsqueeze(1).to_broadcast([128, 4, 128]), op=ALU.mult),
                 ['XIF', 'identb', 'dcs', 'DT', 'XIB'], ['dcs', 'DX'])
            P.op('dve', lambda e: e.tensor_tensor(out=DX[:, 1, :, :], in0=XIB[:], in1=identb[:].unsqueeze(1).to_broadcast([128, 4, 128]), op=ALU.mult),
                 ['XIB', 'identb', 'dcs', 'DX'], ['dcs', 'DX'])
            P.op('dve', lambda e: e.tensor_copy(out=wspb[:], in_=wspf[:]), ['wspf'], ['wspb'])
            trt, trk = gettr()
            P.op('pe', lambda e: [e.transpose(trt[:, g * 128:(g + 1) * 128], wspb[:, g, :], identb[:]) for g in range(4)][-1],
                 ['wspb', 'identb'], [trk])
            P.op('dve', lambda e: e.tensor_copy(out=wspT[:].rearrange("p g q -> p (g q)"), in_=trt[:, 0:512]), [trk], ['wspT'])

            P.op('dve', lambda e: e.memset(on0[:, 0:1], 0.0), [], ['XIF', 'XIB'] + [(('on', 0), h) for h in range(4)])
            P.op('dve', lambda e: e.memset(gsv0[:, 0:1], 0.0), [], ['wspf', 'wspb'] + [(('gsv', 0), n) for n in range(2)])
            def rstd_a(src_ap, dst_ap, rkeys, wkey, scale):
                P.op('dve', lambda e: e.tensor_scalar(out=dst_ap, in0=src_ap, scalar1=scale, scalar2=EPS,
                                                      op0=ALU.mult, op1=ALU.add), rkeys, [wkey])

            def rstd_b(dst_ap, wkey):
                P.op('act', lambda e: e.activation(out=dst_ap, in_=dst_ap, func=AF.Sqrt), [wkey], [wkey])

            def rstd_c(dst_ap, wkey):
                P.op('dve', lambda e: e.reciprocal(out=dst_ap, in_=dst_ap), [wkey], [wkey])

            def rstd_from(src_ap, dst_ap, n, rkeys, wkey, scale):
                rstd_a(src_ap, dst_ap, rkeys, wkey, scale)
                rstd_b(dst_ap, wkey)
                rstd_c(dst_ap, wkey)

            def load_cs(s):
                cst, csk = nxt("cs")
                P.dma('sp', lambda e: e.dma_start(out=cst[:, 0, :, :], in_=cos_t[s * T:(s + 1) * T, :].rearrange("(c p) j -> p c j", p=128)), writes=[csk])
                P.dma('sp', lambda e: e.dma_start(out=cst[:, 1, :, :], in_=sin_t[s * T:(s + 1) * T, :].rearrange("(c p) j -> p c j", p=128)), reads=[], writes=[(csk, 's')])
                return cst, [csk, (csk, 's')]

            def norm_to_T(xt, xk, ss_col, gofs, dstT, dkey, col0, ncols=128, npart=128, rkey='rstd'):
                xnt, xnk = nxt("xn")
                P.op('act', lambda e: e.activation(out=xnt[0:npart, :], in_=xt[0:npart, :], func=AF.Identity, scale=ss_col),
                     [xk, rkey], [xnk])
                trt, trk = gettr()

                def tr(e):
                    for kb in range(8):
                        ins = e.transpose(trt[:, kb * ncols:(kb + 1) * ncols], xnt[0:npart, kb * 128:(kb + 1) * 128],
                                          identb[0:npart, 0:npart])
                    return ins
                P.op('pe', tr, [xnk, 'identb'], [trk])
                P.op('dve', lambda e: e.tensor_tensor(
                    out=dstT[:, :, col0:col0 + ncols],
                    in0=trt[:, 0:8 * ncols].rearrange("p (k n) -> p k n", k=8),
                    in1=gcol[:, gofs:gofs + 8].unsqueeze(2).to_broadcast([128, 8, ncols]), op=ALU.mult),
                    [trk, 'gcol'], [dkey])

            def norm_part1(s):
                xts = []
                ss, ssk = nxt("ss")
                for c in range(CPS):
                    xt, xk = nxt("xst")
                    r0 = s * T + c * 128
                    P.dma('sp', lambda e, xt=xt, r0=r0: e.dma_start(out=xt[:], in_=x[r0:r0 + 128, :]), writes=[xk])
                    jt, jk = nxt("junk")
                    P.op('act', lambda e, xt=xt, c=c, jt=jt: e.activation(out=jt[:], in_=xt[:], func=AF.Square, accum_out=ss[:, c:c + 1]),
                         [xk], [(ssk, c), jk])
                    xts.append((xt, xk))
                rstd_from(ss[:, 0:CPS], ss[:, 0:CPS], CPS, [(ssk, c) for c in range(CPS)], (ssk, 'rstd'), 1.0 / D)
                return (xts, ss, ssk)

            def norm_part2(st):
                xts, ss, ssk = st
                for c in range(CPS):
                    xt, xk = xts[c]
                    norm_to_T(xt, xk, ss[:, c:c + 1], 0, hT, ('hT', c), c * 128, rkey=(ssk, 'rstd'))

            wl_n = [0]

            def load_w(scr, g, key, nkb=8, rows_rearr="(kb p) c -> p kb c"):
                slot = wl_n[0] % NW
                wl_n[0] += 1
                wt = wbuf[slot]
                assert key in P.wr, key
                P.dma('sp', lambda e: e.dma_start(out=wt[:, 0:nkb, :], in_=scr[:, g * 512:(g + 1) * 512].rearrange(rows_rearr, p=128)),
                      reads=[key], writes=[('wb', slot)])
                return wt, ('wb', slot)

            def proj(c, wt, wk, srcT, skey):
                bt, bk = getbank()

                def mm(e):
                    for kb in range(8):
                        ins = e.matmul(bt[:], lhsT=srcT[:, kb, c * 128:(c + 1) * 128], rhs=wt[:, kb, :],
                                       start=(kb == 0), stop=(kb == 7))
                    return ins
                P.op('pe', mm, [wk] + (skey if isinstance(skey, list) else [skey]), [bk])
                return bt, bk

            def rotary(bt, bk, cst, cskeys, c, dst, dkey):
                b4 = bt[:].rearrange("p (h t j) -> p h t j", h=4, t=2)
                d4 = dst[:].rearrange("p (h t j) -> p h t j", h=4, t=2)
                cosb = cst[:, 0, c, :].unsqueeze(1).to_broadcast([128, 4, 64])
                sinb = cst[:, 1, c, :].unsqueeze(1).to_broadcast([128, 4, 64])
                ra, rak = nxt("ra")
                rb_, rbk = nxt("rb")
                ra3 = ra[:].rearrange("p (h j) -> p h j", h=4)
                rb3 = rb_[:].rearrange("p (h j) -> p h j", h=4)
                P.op('dve', lambda e: e.tensor_tensor(out=ra3, in0=b4[:, :, 0, :], in1=cosb, op=ALU.mult), [bk] + cskeys, [rak])
                P.op('dve', lambda e: e.tensor_tensor(out=rb3, in0=b4[:, :, 1, :], in1=sinb, op=ALU.mult), [bk] + cskeys, [rbk])
                P.op('dve', lambda e: e.tensor_tensor(out=d4[:, :, 0, :], in0=ra3, in1=rb3, op=ALU.subtract), [rak, rbk], [(dkey, 0)])
                P.op('dve', lambda e: e.tensor_tensor(out=ra3, in0=b4[:, :, 0, :], in1=sinb, op=ALU.mult), [bk] + cskeys, [rak])
                P.op('dve', lambda e: e.tensor_tensor(out=rb3, in0=b4[:, :, 1, :], in1=cosb, op=ALU.mult), [bk] + cskeys, [rbk])
                P.op('dve', lambda e: e.tensor_tensor(out=d4[:, :, 1, :], in0=ra3, in1=rb3, op=ALU.add), [rak, rbk], [(dkey, 1)])
                return [(dkey, 0), (dkey, 1)]

            def k_stage(c, wt, wk, cst, cskeys, zcol, kz_eng='pool'):
                bt, bk = proj(c, wt, wk, hT, ('hT', c))
                kr, krk = nxt("krot")
                keys = rotary(bt, bk, cst, cskeys, c, kr, krk)
                P.op(kz_eng, lambda e: e.tensor_tensor(
                    out=kz[:, c, :].rearrange("p (h d) -> p h d", h=4),
                    in0=kr[:].rearrange("p (h d) -> p h d", h=4),
                    in1=zcol.unsqueeze(2).to_broadcast([128, 4, 128]), op=ALU.mult), keys + ['zf', 'zb'], [('kz', c)])
                return kr, keys

            def v_stage(c, n, wt, wk):
                bt, bk = proj(c, wt, wk, hT, ('hT', c))
                P.op('act', lambda e: e.activation(out=v8[:, c, n * 512:(n + 1) * 512], in_=bt[:], func=AF.Identity), [bk], [('v8', c, n)])

            def kv_update(c, dcol0, rbf_dst, rbf_key):
                for hp in range(2):
                    bt, bk = getbank()

                    def mm(e, hp=hp, bt=bt):
                        for hh in range(2):
                            h = hp * 2 + hh
                            ins = e.matmul(bt[:, hh * 256:(hh + 1) * 256], lhsT=kz[:, c, h * 128:(h + 1) * 128],
                                           rhs=v8[:, c, h * 256:(h + 1) * 256], start=True, stop=True)
                        return ins
                    P.op('pe', mm, [('kz', c), ('v8', c, hp)], [bk])
                    for hh in range(2):
                        h = hp * 2 + hh
                        P.op('dve', lambda e, h=h, hh=hh, bt=bt: e.scalar_tensor_tensor(
                            out=R32[:, h, :], in0=R32[:, h, :], scalar=dc[:, dcol0 + h:dcol0 + h + 1],
                            in1=bt[:, hh * 256:(hh + 1) * 256], op0=ALU.mult, op1=ALU.add), [bk, ('R32', h), 'dc'], [('R32', h)])
                    P.op('act', lambda e, hp=hp: e.activation(out=rbf_dst[:, 2 * hp:2 * hp + 2, :], in_=R32[:, 2 * hp:2 * hp + 2, :], func=AF.Identity),
                         [('R32', 2 * hp), ('R32', 2 * hp + 1)], [(rbf_key, hp)])

            P.op('dve', lambda e: e.memset(R32[:], 0.0), [], [('R32', h) for h in range(4)])
            P.op('pool', lambda e: e.memset(Rbf[0][:], 0.0), [], [(('Rbf', 0), 0), (('Rbf', 0), 1)])
            rpar = 0
            if STOP != 'setup':
                def load_w_direct(g):
                    slot = wl_n[0] % NW
                    wl_n[0] += 1
                    wt = wbuf[slot]
                    P.dma('pool', lambda e: e.dma_start(out=wt[:], in_=w_in[:, g * 512:(g + 1) * 512].rearrange("(kb p) c -> p kb c", p=128)),
                          writes=[('wb', slot)])
                    return wt, ('wb', slot)
                wkt, wkk = load_w_direct(1)
                wv0, wv0k = load_w_direct(2)
                wv1, wv1k = load_w_direct(3)
                cs_of = {}
                cs_of[NST - 1] = load_cs(NST - 1)
                norm_part2(norm_part1(NST - 1))
                xts_next = None
                pend = None

                def state_step(i):
                    nonlocal rpar
                    cur = Rbf[rpar]
                    P.dma('sp', lambda e, cur=cur, i=i: e.dma_start(out=rb_scr[i], in_=cur[:].rearrange("p h e -> p (h e)")),
                          reads=[(('Rbf', rpar), 0), (('Rbf', rpar), 1)], writes=[('rb_scr', i)])
                    if i > 0:
                        rpar ^= 1
                        kv_update(i % CPS, 4, Rbf[rpar], ('Rbf', rpar))

                for i in range(NCHUNK - 1, -1, -1):
                    s_, c = divmod(i, CPS)
                    if c == CPS - 1:
                        emit_casts(1)
                        if s_ > 0:
                            cs_of[s_ - 1] = load_cs(s_ - 1)
                            xts_next = norm_part1(s_ - 1)
                    cst, cskeys = cs_of[s_]
                    kr, kkeys = k_stage(c, wkt, wkk, cst, cskeys, zb[:, :], kz_eng='dve')
                    kt, ktk = nxt("tm512")
                    P.op('act', lambda e, kt=kt, kr=kr: e.activation(out=kt[:], in_=kr[:], func=AF.Identity), kkeys, [ktk])
                    P.dma('sp', lambda e, kt=kt, i=i: e.dma_start(out=k_scr[i], in_=kt[:]), reads=[ktk], writes=[('k_scr', i)])
                    v_stage(c, 0, wv0, wv0k)
                    v_stage(c, 1, wv1, wv1k)
                    P.dma('sp', lambda e, c=c, i=i: e.dma_start(out=v_scr[i], in_=v8[:, c, :]),
                          reads=[('v8', c, 0), ('v8', c, 1)], writes=[('v_scr', i)])
                    if c == 0 and s_ > 0:
                        norm_part2(xts_next)
                    if pend is not None:
                        state_step(pend)
                    pend = i
                state_step(pend)

            P.op('dve', lambda e: e.memset(R32[:], 0.0), [('R32', h) for h in range(4)], [('R32', h) for h in range(4)])
            P.op('pool', lambda e: e.memset(Rbf[0][:], 0.0), [], [(('Rbf', 0), 0), (('Rbf', 0), 1)])
            fpar = 0
            RUN1 = STOP not in ('setup', 'p0')
            gl1 = [(w_in_b, 4, 'w_in_b'), (w_in_b, 5, 'w_in_b'),
                   (w_in_b, 0, 'w_in_b'),
                   (w_in_b, 6, 'w_in_b'), (w_in_b, 7, 'w_in_b'),
                   (w_in_b, 8, 'w_in_b'), (w_in_b, 9, 'w_in_b'),
                   (w_in_b, 10, 'w_in_b'), (w_in_b, 11, 'w_in_b'),
                   (w_in_b, 12, 'w_in_b'), (w_in_b, 13, 'w_in_b'),
                   (wro_b, 0, 'wro_b'), (wso_b, 0, 'wso_b'),
                   (wro_b, 1, 'wro_b'), (wso_b, 1, 'wso_b'),
                   (wout_b, 0, 'wout_b'), (wout_b, 1, 'wout_b')]
            NG = len(gl1)
            glist = gl1 * NST
            loaded = {}

            def ensure_cast(key):
                while key not in P.wr and pending:
                    emit_casts(1)

            def Wg(idx, la=3, cap=None):
                hi = idx + la + 1
                if cap is not None:
                    hi = min(hi, cap + 1)
                for j in range(min(hi, len(glist))):
                    if j not in loaded:
                        for jj in range(j, min(j + 5, len(glist))):
                            ensure_cast((glist[jj][2], glist[jj][1]))
                        scr, g, nm = glist[j]
                        loaded[j] = load_w(scr, g, (nm, g))
                return loaded[idx]

            def pipe(n, stages, lag=1, filler=None):
                st = {}
                for t in range(n + (len(stages) - 1) * lag):
                    for k, f in enumerate(stages):
                        idx = t - k * lag
                        if 0 <= idx < n:
                            st[idx] = f(idx, st.get(idx))
                    if filler is not None:
                        filler(t)

            if RUN1:
                cs_cur = load_cs(0)
                norm_part2(norm_part1(0))
            for s in (range(NST) if RUN1 else []):
                W = lambda idx, cap=None, s=s: Wg(s * NG + idx, cap=(None if cap is None else s * NG + cap))
                cst, cskeys = cs_cur

                for n in range(2):
                    wt, wk = W(n)
                    for c in range(CPS):
                        bt, bk = proj(c, wt, wk, hT, ('hT', c))
                        tg, tgk = nxt("tg")
                        P.op('act', lambda e, tg=tg, bt=bt: e.activation(out=tg[:], in_=bt[:], func=AF.Tanh, scale=0.5), [bk], [tgk])
                        P.op('dve', lambda e, tg=tg, bt=bt, c=c, n=n: e.scalar_tensor_tensor(
                            out=gate8[:, c, n * 512:(n + 1) * 512], in0=tg[:], scalar=1.0, in1=bt[:],
                            op0=ALU.add, op1=ALU.mult), [bk, tgk], [('gate8', c, n)])

                wq, wqk = W(2)
                P.dma('sp', lambda e, s=s: e.dma_start(out=v8[:], in_=v_scr[s * CPS:(s + 1) * CPS].rearrange("c p e -> p c e")),
                      reads=[('v_scr', s * CPS + c) for c in range(CPS)], writes=[('v8', c, n) for c in range(CPS) for n in range(2)])

                def qk_A(idx, _):
                    which, c = divmod(idx, CPS)
                    if which == 0:
                        bt, bk = proj(c, wq, wqk, hT, ('hT', c))
                        kr, krk = nxt("krot")
                        keys = rotary(bt, bk, cst, cskeys, c, kr, krk)
                        qt, qtk = nxt("tm512")
                        P.op('act', lambda e, qt=qt, kr=kr: e.activation(out=qt[:], in_=kr[:], func=AF.Identity), keys, [qtk])
                    else:
                        i = s * CPS + c
                        qt, qtk = nxt("tm512")
                        P.dma('sp', lambda e, qt=qt, i=i: e.dma_start(out=qt[:], in_=k_scr[i]), reads=[('k_scr', i)], writes=[qtk])
                        P.op('pool', lambda e, qt=qt, c=c: e.tensor_tensor(
                            out=kz[:, c, :].rearrange("p (h d) -> p h d", h=4),
                            in0=qt[:].rearrange("p (h d) -> p h d", h=4),
                            in1=zf[:, :].unsqueeze(2).to_broadcast([128, 4, 128]), op=ALU.mult), [qtk, 'zf'], [('kz', c)])
                    return (qt, qtk)

                def qk_B(idx, stt):
                    which, c = divmod(idx, CPS)
                    qt, qtk = stt
                    if which == 0:
                        dsts = [(qkT[:, 0, c, :, :], ('qT', c), None), (qx[:, 0, c, :, :], ('qxf', c), 0), (qx[:, 1, c, :, :], ('qxb', c), 1)]
                        for k, (dst, dkey, v) in enumerate(dsts):
                            bt, bk = getbank()

                            def mmq(e, bt=bt, v=v):
                                for h in range(4):
                                    rhs = identb[:] if v is None else DX[:, v, h, :]
                                    ins = e.matmul(bt[:, h * 128:(h + 1) * 128], lhsT=qt[:, h * 128:(h + 1) * 128], rhs=rhs, start=True, stop=True)
                                return ins
                            P.op('pe', mmq, [qtk, 'identb', 'DX'], [bk])
                            eng = 'dve' if k == 2 else 'act'
                            if eng == 'act':
                                P.op('act', lambda e, bt=bt, dst=dst: e.activation(out=dst, in_=bt[:].rearrange("p (h n) -> p h n", h=4), func=AF.Identity),
                                     [bk], [dkey])
                            else:
                                P.op('dve', lambda e, bt=bt, dst=dst: e.tensor_copy(out=dst, in_=bt[:].rearrange("p (h n) -> p h n", h=4)), [bk], [dkey])
                    else:
                        trt, trk = gettr()
                        P.op('pe', lambda e, trt=trt, qt=qt: [e.transpose(trt[:, h * 128:(h + 1) * 128], qt[:, h * 128:(h + 1) * 128], identb[:])
                                                              for h in range(4)][-1], [qtk, 'identb'], [trk])
                        tr3 = trt[:, 0:512].rearrange("p (h n) -> p h n", h=4)
                        P.op('act', lambda e, tr3=tr3, c=c: e.activation(out=qkT[:, 1, c, :, :], in_=tr3, func=AF.Identity), [trk], [('kT', c)])
                    return stt
                pipe(2 * CPS, [qk_A, qk_B])
                W(3)
                emit_casts(2)

                def load_rb(c):
                    i = s * CPS + c
                    rt, rk = nxt("rbc")
                    P.dma('sp', lambda e, rt=rt, i=i: e.dma_start(out=rt[:], in_=rb_scr[i]), reads=[('rb_scr', i)], writes=[rk])
                    return rt, rk
                rbs = {0: load_rb(0), 1: load_rb(1)}

                def ret_A(c, _):
                    nonlocal fpar
                    i = s * CPS + c
                    if c + 2 < CPS:
                        rbs[c + 2] = load_rb(c + 2)
                    rt, rk = rbs[c]
                    rcur = Rbf[fpar]
                    rkeys = [(('Rbf', fpar), 0), (('Rbf', fpar), 1)]
                    if i < NCHUNK - 1:
                        fpar ^= 1
                        kv_update(c, 0, Rbf[fpar], ('Rbf', fpar))
                    stb, stk = getbank()
                    P.op('pe', lambda e, stb=stb, c=c: [e.matmul(stb[:, h * 128:(h + 1) * 128], lhsT=qkT[:, 1, c, h, :], rhs=qkT[:, 0, c, h, :],
                                                                 start=True, stop=True) for h in range(4)][-1],
                         [('qT', c), ('kT', c)], [stk])
                    pt, ptk = nxt("PT")
                    P.op('dve', lambda e, pt=pt, stb=stb: e.tensor_tensor(out=pt[:], in0=stb[:], in1=DT[:].rearrange("p h n -> p (h n)"), op=ALU.mult),
                         [stk, 'DT'], [ptk])
                    obanks = []
                    for hp in range(2):
                        ob, obk = getbank(pin=True)

                        def omm(e, hp=hp, ob=ob, pt=pt, rcur=rcur, c=c, rt=rt):
                            for hh in range(2):
                                h = hp * 2 + hh
                                o_ = ob[:, hh * 256:(hh + 1) * 256]
                                e.matmul(o_, lhsT=pt[:, h * 128:(h + 1) * 128], rhs=v8[:, c, h * 256:(h + 1) * 256], start=True, stop=False)
                                e.matmul(o_, lhsT=qx[:, 0, c, h, :], rhs=rcur[:, h, :], start=False, stop=False)
                                ins = e.matmul(o_, lhsT=qx[:, 1, c, h, :], rhs=rt[:, h * 256:(h + 1) * 256], start=False, stop=True)
                            return ins
                        P.op('pe', omm, [ptk, ('v8', c, hp), ('qxf', c), ('qxb', c), rk] + rkeys, [obk])
                        obanks.append((ob, obk))
                    return obanks

                def ret_B1(c, obanks):
                    sm, smk = nxt("sm")
                    mv, mvk = nxt("mv")
                    st6, s6k = nxt("st6")
                    on, onk = nxt("on")
                    for h in range(4):
                        ob, obk = obanks[h // 2]
                        osl = ob[:, (h % 2) * 256:(h % 2 + 1) * 256]
                        P.op('dve', lambda e, h=h, osl=osl: e.bn_stats(out=st6[:, h, :], in_=osl), [obk], [(s6k, h)])
                        P.op('dve', lambda e, h=h: e.bn_aggr(out=mv[:, h, :], in_=st6[:, h, :]), [(s6k, h)], [(mvk, h)])
                    rstd_a(mv[:, :, 1], sm[:, 0:4], [(mvk, h) for h in range(4)], (smk, 'rs4'), 1.0)
                    rstd_b(sm[:, 0:4], (smk, 'rs4'))
                    for h in range(4):
                        ob, obk = obanks[h // 2]
                        osl = ob[:, (h % 2) * 256:(h % 2 + 1) * 256]
                        P.op('dve', lambda e, h=h, osl=osl, on=on: e.scalar_tensor_tensor(
                            out=on[:, h * 256:(h + 1) * 256], in0=osl, scalar=mv[:, h, 0:1], in1=gate8[:, c, h * 256:(h + 1) * 256],
                            op0=ALU.subtract, op1=ALU.mult), [obk, (mvk, h), ('gate8', c, h // 2)], [(onk, h)])
                    for ob, obk in obanks:
                        pinned.discard(obk[1])
                    return (on, onk, sm, smk)

                def ret_B2(c, stt):
                    on, onk, sm, smk = stt
                    rstd_c(sm[:, 0:4], (smk, 'rs4'))
                    yt, ytk = nxt("tm1024")
                    for h in range(4):
                        P.op('act', lambda e, h=h, on=on, yt=yt: e.activation(out=yt[:, h * 256:(h + 1) * 256], in_=on[:, h * 256:(h + 1) * 256],
                                                                              func=AF.Identity, scale=sm[:, h:h + 1]),
                             [(onk, h), (smk, 'rs4')], [(ytk, h)] + ([ytk] if h == 0 else []))
                    return (yt, [(ytk, h) for h in range(4)] + [ytk])

                def ret_C(c, stt):
                    yt, ytks = stt
                    trt, trk = gettr()
                    P.op('pe', lambda e, trt=trt, yt=yt: [e.transpose(trt[:, kb * 128:(kb + 1) * 128], yt[:, kb * 128:(kb + 1) * 128], identb[:])
                                                          for kb in range(8)][-1], ytks + ['identb'], [trk])
                    P.op('act', lambda e, trt=trt, c=c: e.activation(out=yrT[:, :, c * 128:(c + 1) * 128],
                                                                     in_=trt[:].rearrange("p (k n) -> p k n", k=8), func=AF.Identity, scale=0.5),
                         [trk], [('yrT', c)])
                    return stt
                def u_item(c, n):
                    wt, wk = W(3 + n, cap=5)
                    bt, bk = proj(c, wt, wk, hT, ('hT', c))
                    P.op('act', lambda e, bt=bt, c=c, n=n: e.activation(out=gate8[:, c, n * 512:(n + 1) * 512], in_=bt[:], func=AF.Gelu_apprx_tanh),
                         [bk], [('gate8', c, n)])

                def u_fill(t):
                    c = t - 1
                    if 0 <= c < CPS:
                        u_item(c, 0)
                        u_item(c, 1)
                rst = {}
                for t in range(CPS + 2):
                    if 0 <= t - 1 < CPS:
                        rst[t - 1] = ret_B1(t - 1, rst[t - 1])
                    if t < CPS:
                        rst[t] = ret_A(t, None)
                    if 0 <= t - 1 < CPS:
                        rst[t - 1] = ret_B2(t - 1, rst[t - 1])
                    if 0 <= t - 2 < CPS:
                        ret_C(t - 2, rst[t - 2])
                    u_fill(t)

                w0, w0k = W(5)
                w1, w1k = W(6)
                qkkeys = [('qT', c) for c in range(CPS)] + [('kT', c) for c in range(CPS)]

                def sv_A1(c):
                    sm, smk = nxt("sm")
                    mv2, mv2k = nxt("mv2")
                    st12, s12k = nxt("st12")
                    gs, gsk = nxt("gsv")
                    for n, (wt, wk) in enumerate(((w0, w0k), (w1, w1k))):
                        bt, bk = proj(c, wt, wk, hT, ('hT', c))
                        P.op('act', lambda e, bt=bt, gs=gs, n=n: e.activation(out=gs[:, n * 512:(n + 1) * 512], in_=bt[:], func=AF.Gelu_apprx_tanh),
                             [bk], [(gsk, n)])
                        P.op('dve', lambda e, gs=gs, n=n: e.bn_stats(out=st12[:, n, :], in_=gs[:, n * 512:(n + 1) * 512]), [(gsk, n)], [(s12k, n)])
                    P.op('dve', lambda e: e.bn_aggr(out=mv2[:], in_=st12[:].rearrange("p a b -> p (a b)")), [(s12k, 0), (s12k, 1)], [mv2k])
                    rstd_a(mv2[:, 1:2], sm[:, 8:9], [mv2k], (smk, 'rs1'), 1.0)
                    P.op('dve', lambda e, gs=gs: e.scalar_tensor_tensor(out=gs[:], in0=gs[:], scalar=mv2[:, 0:1], in1=lng[:],
                                                                        op0=ALU.subtract, op1=ALU.mult), [(gsk, 0), (gsk, 1), mv2k, 'lng'], [(gsk, 0), (gsk, 1)])
                    return (sm, smk, mv2, mv2k, gs, gsk)

                def sv_A2(st):
                    sm, smk, mv2, mv2k, gs, gsk = st
                    rstd_b(sm[:, 8:9], (smk, 'rs1'))

                def sv_A3(st):
                    sm, smk, mv2, mv2k, gs, gsk = st
                    rstd_c(sm[:, 8:9], (smk, 'rs1'))
                    sv, svk = nxt("tm1024")
                    P.op('dve', lambda e, gs=gs, sv=sv: e.scalar_tensor_tensor(out=sv[:], in0=gs[:], scalar=sm[:, 8:9], in1=lnb[:],
                                                                               op0=ALU.mult, op1=ALU.add), [(gsk, 0), (gsk, 1), (smk, 'rs1'), 'lnb'], [svk])
                    return (sv, svk)

                def sv_B(c, stt):
                    sv, svk = stt
                    ys, ysk = nxt("tm1024")
                    for gp in range(2):
                        bt, bk = getbank()
                        P.op('pe', lambda e, bt=bt, sv=sv, gp=gp: [e.matmul(bt[:, gg * 256:(gg + 1) * 256], lhsT=wspT[:, gp * 2 + gg, :],
                                                                          rhs=sv[:, (gp * 2 + gg) * 256:(gp * 2 + gg + 1) * 256], start=True, stop=True)
                                                                 for gg in range(2)][-1], [svk, 'wspT'], [bk])
                        for gg in range(2):
                            g = gp * 2 + gg
                            P.op('dve', lambda e, bt=bt, ys=ys, g=g, gg=gg, c=c: e.scalar_tensor_tensor(
                                out=ys[:, g * 256:(g + 1) * 256], in0=bt[:, gg * 256:(gg + 1) * 256], scalar=bsp[:, g:g + 1],
                                in1=gate8[:, c, g * 256:(g + 1) * 256], op0=ALU.add, op1=ALU.mult),
                                [bk, 'bsp', ('gate8', c, g // 2)], [(ysk, g)] + ([ysk] if g == 0 else []))
                    return (ys, ysk)

                def sv_C(c, stt):
                    ys, ysk = stt
                    trt, trk = gettr()
                    P.op('pe', lambda e, trt=trt, ys=ys: [e.transpose(trt[:, kb * 128:(kb + 1) * 128], ys[:, kb * 128:(kb + 1) * 128], identb[:])
                                                          for kb in range(8)][-1], [(ysk, g) for g in range(4)] + [ysk, 'identb'], [trk])
                    P.op('act', lambda e, trt=trt, c=c: e.activation(out=ysT[:, :, c * 128:(c + 1) * 128],
                                                                     in_=trt[:].rearrange("p (k n) -> p k n", k=8), func=AF.Identity),
                         [trk] + qkkeys, [('ysT', c)] + qkkeys)
                    return stt
                qxkeys = [('qxf', c) for c in range(CPS)] + [('qxb', c) for c in range(CPS)]

                def gate_item(g, c):
                    wt, wk = W(7 + g, cap=9 if g < 3 else None)
                    bt, bk = proj(c, wt, wk, hT, ('hT', c))
                    if g < 2:
                        P.op('act', lambda e, bt=bt: e.activation(out=v8[:, c, g * 512:(g + 1) * 512], in_=bt[:], func=AF.Tanh, scale=0.5),
                             [bk], [('v8', c, g)])
                    else:
                        n = g - 2
                        P.op('act', lambda e, bt=bt: e.activation(out=tas[:, c, n * 512:(n + 1) * 512], in_=bt[:], func=AF.Tanh, scale=0.5),
                             [bk] + qxkeys, [('tas', c, n)] + qxkeys)
                g_items = [(g, c) for g in range(3) for c in range(CPS)]

                def g_fill(t):
                    for _ in range(2):
                        if g_items:
                            gate_item(*g_items.pop(0))
                sst = {}
                for t in range(CPS + 2):
                    a1 = sv_A1(t) if t < CPS else None
                    if 0 <= t - 2 < CPS:
                        sv_C(t - 2, sst[t - 2])
                    g_fill(t)
                    if a1 is not None:
                        sv_A2(a1)
                    if 0 <= t - 1 < CPS:
                        sst[t - 1] = sv_B(t - 1, sst[t - 1])
                    if a1 is not None:
                        sst[t] = sv_A3(a1)
                while g_items:
                    gate_item(*g_items.pop(0))
                for c in range(CPS):
                    gate_item(3, c)

                xts_next = None
                if s + 1 < NST:
                    cs_cur = load_cs(s + 1)
                    xts_next = norm_part1(s + 1)

                for n in range(2):
                    wr, wrk = W(11 + 2 * n)
                    ws, wsk = W(12 + 2 * n)
                    for c in range(CPS):
                        br, brk = proj(c, wr, wrk, yrT, ('yrT', c))
                        bs_, bsk = proj(c, ws, wsk, ysT, [('ysT', c)] + qkkeys)
                        m1, m1k = nxt("m1")
                        m2, m2k = nxt("m2")
                        P.op('dve', lambda e, m1=m1, br=br, c=c, n=n: e.scalar_tensor_tensor(
                            out=m1[:], in0=v8[:, c, n * 512:(n + 1) * 512], scalar=1.0, in1=br[:], op0=ALU.add, op1=ALU.mult),
                            [brk, ('v8', c, n)], [m1k])
                        P.op('dve', lambda e, m2=m2, bs_=bs_, c=c, n=n: e.scalar_tensor_tensor(
                            out=m2[:], in0=tas[:, c, n * 512:(n + 1) * 512], scalar=1.0, in1=bs_[:], op0=ALU.add, op1=ALU.mult),
                            [bsk, ('tas', c, n)] + qxkeys, [m2k])
                        P.op('pool', lambda e, m1=m1, m2=m2, c=c, n=n: e.tensor_tensor(out=gate8[:, c, n * 512:(n + 1) * 512], in0=m1[:], in1=m2[:], op=ALU.add),
                             [m1k, m2k], [('gate8', c, n)])
                for c in range(CPS):
                    trt, trk = gettr()
                    P.op('pe', lambda e, trt=trt, c=c: [e.transpose(trt[:, kb * 128:(kb + 1) * 128], gate8[:, c, kb * 128:(kb + 1) * 128], identb[:])
                                                        for kb in range(8)][-1], [('gate8', c, 0), ('gate8', c, 1), 'identb'], [trk])
                    P.op('act', lambda e, trt=trt, c=c: e.activation(out=mT[:, :, c * 128:(c + 1) * 128],
                                                                     in_=trt[:].rearrange("p (k n) -> p k n", k=8), func=AF.Identity, scale=0.5),
                         [trk], [('mT', c)])

                if xts_next is not None:
                    norm_part2(xts_next)

                wo0, wo0k = W(15)
                wo1, wo1k = W(16)
                def wo_X1(c):
                    r0 = s * T + c * 128
                    xt, xk = nxt("on")
                    sm, smk = nxt("sm")
                    P.dma('sp', lambda e, xt=xt, r0=r0: e.dma_start(out=xt[:], in_=x[r0:r0 + 128, :]), writes=[(xk, h) for h in range(4)])
                    pbs = []
                    for n, (wt, wk) in enumerate(((wo0, wo0k), (wo1, wo1k))):
                        bt, bk = proj(c, wt, wk, mT, ('mT', c))
                        jt, jk = nxt("junk")
                        P.op('act', lambda e, bt=bt, n=n, jt=jt, sm=sm: e.activation(out=jt[:, 0:512], in_=bt[:], func=AF.Square, accum_out=sm[:, 10 + n:11 + n]),
                             [bk], [(smk, 'ssq', n), jk])
                        pbs.append((bt, bk))
                    P.op('dve', lambda e, sm=sm: e.tensor_tensor(out=sm[:, 12:13], in0=sm[:, 10:11], in1=sm[:, 11:12], op=ALU.add), [(smk, 'ssq', 0), (smk, 'ssq', 1)], [(smk, 'ssq2')])
                    rstd_a(sm[:, 12:13], sm[:, 13:14], [(smk, 'ssq2')], (smk, 'rsq'), 1.0 / D)
                    return (r0, xt, xk, sm, smk, pbs)

                def wo_X2(st):
                    r0, xt, xk, sm, smk, pbs = st
                    rstd_b(sm[:, 13:14], (smk, 'rsq'))

                def wo_X3(st):
                    r0, xt, xk, sm, smk, pbs = st
                    rstd_c(sm[:, 13:14], (smk, 'rsq'))
                    xkeys = [(xk, h) for h in range(4)]
                    for n in range(2):
                        bt, bk = pbs[n]
                        t1, t1k = nxt("x1t")
                        P.op('dve', lambda e, bt=bt, t1=t1, n=n, sm=sm: e.scalar_tensor_tensor(
                            out=t1[:], in0=bt[:], scalar=sm[:, 13:14], in1=gpost[:, n * 512:(n + 1) * 512], op0=ALU.mult, op1=ALU.mult),
                            [bk, (smk, 'rsq'), 'gpost'], [t1k])
                        P.op('pool', lambda e, t1=t1, xt=xt, n=n: e.tensor_tensor(out=xt[:, n * 512:(n + 1) * 512], in0=xt[:, n * 512:(n + 1) * 512], in1=t1[:], op=ALU.add),
                             [t1k] + xkeys, xkeys)
                    P.dma('sp', lambda e, xt=xt, r0=r0: e.dma_start(out=x1_scr[r0:r0 + 128, :], in_=xt[:]), reads=xkeys, writes=[('x1s', r0 // 128)])

                wst = {0: wo_X1(0)}
                for c in range(CPS):
                    if c + 1 < CPS:
                        wst[c + 1] = wo_X1(c + 1)
                    wo_X2(wst[c])
                    wo_X3(wst[c])

            emit_casts_until(0)
            P.barrier()
            _run_block(nc, P.take())

        with ExitStack() as p2:
            def sb2(name, shape, dt=F32):
                return sb(name, shape, dt, p2)

            WUP = sb2("WUP", [128, 8, 2 * DFF], BF16)
            WDN = sb2("WDN", [128, NJ, D], BF16)
            X1 = [sb2("X1_%d" % i, [128, 2, D]) for i in range(3)]
            HB = [sb2("HB_%d" % i, [128, 8, 258], BF16) for i in range(2)]
            NSP = 4
            actT = sb2("actT", [128, NJ + NSP, 256], BF16)

            def aslot(j, ja):
                return (j * NJ + ja) % (NJ + NSP)
            rtile("s3", [128, 16], F32, 4, p2)
            gpostffn = sb2("gpostffn", [128, D])
            cw = sb2("cw", [128, 2 * NJ, 3])
            cb = sb2("cb", [128, 2 * NJ])
            P.dma('sp', lambda e: e.dma_start(out=gpostffn[:], in_=rows[1:2, :].to_broadcast([128, D])), writes=['gpostffn'])
            P.dma('sp', lambda e: e.dma_start(out=cw[:], in_=cwl[:, :, :]), writes=['cw'])
            P.dma('sp', lambda e: e.dma_start(out=cb[:], in_=cbl[:, :]), writes=['cb'])
            rtile("xn", [128, D], BF16, 2, p2, "q_")
            rtile("acc", [128, 256], F32, 8, p2)
            rtile("ga", [128, 256], F32, 3, p2)
            rtile("yt", [128, 512], F32, 2, p2)

            NT2 = S // 256 if STOP == 'all' else 0

            def hbkeys(b):
                return [(('HB', b), 0), (('HB', b), 1), (('HB', b), 'L'), (('HB', b), 'R')]

            def front(j):
                r0 = j * 256
                x1t = X1[j % 3]
                x1k = ('X1', j % 3)
                hb = HB[j % 2]
                s3, s3k = nxt("s3")
                P.dma('sp', lambda e: e.dma_start(out=x1t[:], in_=x1_scr[r0:r0 + 256, :].rearrange("(c p) d -> p c d", p=128)),
                      reads=[('x1s', r0 // 128), ('x1s', r0 // 128 + 1)], writes=[(x1k, 0), (x1k, 1)])
                for c in range(2):
                    jt, jk = nxt("junk")
                    P.op('act', lambda e, c=c, jt=jt: e.activation(out=jt[:], in_=x1t[:, c, :], func=AF.Square, accum_out=s3[:, c:c + 1]),
                         [(x1k, c)], [(s3k, c), jk])
                rstd_from(s3[:, 0:2], s3[:, 4:6], 2, [(s3k, 0), (s3k, 1)], (s3k, 'rstd'), 1.0 / D)
                for c in range(2):
                    norm_to_T(x1t[:, c, :], (x1k, c), s3[:, 4 + c:5 + c], 8, hb, (('HB', j % 2), c), 1 + c * 128, rkey=(s3k, 'rstd'))

            def halo_copy(dst_b, dst_col, src_b, src_col, dkey, skey):
                P.op('dve', lambda e: e.tensor_copy(out=HB[dst_b][:, :, dst_col:dst_col + 1], in_=HB[src_b][:, :, src_col:src_col + 1]),
                     [skey], [dkey])

            def halo_zero(b, col, dkey):
                P.op('dve', lambda e: e.memset(HB[b][:, :, col:col + 1], 0.0), [], [dkey])

            def up(j, ja_list):
                hb = HB[j % 2]
                hkeys = hbkeys(j % 2)
                for ja in ja_list:
                    accs = []
                    bts = []
                    for half in range(2):
                        ch = half * NJ + ja
                        bt, bk = getbank()

                        def mm(e, bt=bt, ch=ch):
                            for kb in range(8):
                                ins = e.matmul(bt[:, 0:258], lhsT=WUP[:, kb, ch * 128:(ch + 1) * 128], rhs=hb[:, kb, :],
                                               start=(kb == 0), stop=(kb == 7))
                            return ins
                        P.op('pe', mm, hkeys + [('WUP', ch // 4)], [bk])
                        ac, ack = nxt("acc")
                        P.op('act', lambda e, ac=ac, bt=bt, ch=ch: e.activation(out=ac[:], in_=bt[:, 0:256], func=AF.Identity,
                                                                               bias=cb[:, ch:ch + 1], scale=cw[:, ch, 0:1]), [bk, 'cw', 'cb'], [ack])
                        accs.append((ac, ack))
                        bts.append((bt, bk, ch))
                    if pend_gate[0] is not None:
                        pend_gate[0]()
                        pend_gate[0] = None
                    for tap in (1, 2):
                        for half in range(2):
                            ac, ack = accs[half]
                            bt, bk, ch = bts[half]
                            P.op('dve', lambda e, ac=ac, bt=bt, ch=ch, tap=tap: e.scalar_tensor_tensor(
                                out=ac[:], in0=bt[:, tap:tap + 256], scalar=cw[:, ch, tap:tap + 1], in1=ac[:],
                                op0=ALU.mult, op1=ALU.add), [bk, ack, 'cw'], [ack])
                    pend_gate[0] = (lambda accs=accs, sl=aslot(j, ja): gate(accs, sl))

            pend_gate = [None]

            def gate(accs, sl):
                ga, gak = nxt("ga")
                P.op('act', lambda e, ga=ga, a=accs[0][0]: e.activation(out=ga[:], in_=a[:], func=AF.Gelu_apprx_tanh), [accs[0][1]], [gak])
                P.op('pool', lambda e, ga=ga, b=accs[1][0], sl=sl: e.tensor_tensor(out=actT[:, sl, :], in0=ga[:], in1=b[:], op=ALU.mult),
                     [gak, accs[1][1]], [('actT', sl)])

            def flush_gate():
                if pend_gate[0] is not None:
                    pend_gate[0]()
                    pend_gate[0] = None

            def down(j):
                r0 = j * 256
                x1t = X1[j % 3]
                x1k = ('X1', j % 3)
                for c in range(2):
                    pbs = []
                    s3, s3k = nxt("s3")
                    for n in range(2):
                        bt, bk = getbank()

                        def mmd(e, bt=bt, c=c, n=n):
                            for jj in range(NJ):
                                ins = e.matmul(bt[:], lhsT=actT[:, aslot(j, jj), c * 128:(c + 1) * 128], rhs=WDN[:, jj, n * 512:(n + 1) * 512],
                                               start=(jj == 0), stop=(jj == NJ - 1))
                            return ins
                        P.op('pe', mmd, [('actT', aslot(j, jj)) for jj in range(NJ)] + [('WDN', n)], [bk])
                        jt, jk = nxt("junk")
                        P.op('act', lambda e, bt=bt, n=n, jt=jt, s3=s3: e.activation(out=jt[:, 0:512], in_=bt[:], func=AF.Square, accum_out=s3[:, 8 + n:9 + n]),
                             [bk], [(s3k, 'q', n), jk])
                        pbs.append((bt, bk))
                    P.op('dve', lambda e, s3=s3: e.tensor_tensor(out=s3[:, 10:11], in0=s3[:, 8:9], in1=s3[:, 9:10], op=ALU.add), [(s3k, 'q', 0), (s3k, 'q', 1)], [(s3k, 'q2')])
                    rstd_from(s3[:, 10:11], s3[:, 11:12], 1, [(s3k, 'q2')], (s3k, 'rsq'), 1.0 / D)
                    for n in range(2):
                        bt, bk = pbs[n]
                        yt, ytk = nxt("yt")
                        P.op('dve', lambda e, bt=bt, yt=yt, n=n, s3=s3: e.scalar_tensor_tensor(
                            out=yt[:], in0=bt[:], scalar=s3[:, 11:12], in1=gpostffn[:, n * 512:(n + 1) * 512], op0=ALU.mult, op1=ALU.mult),
                            [bk, (s3k, 'rsq'), 'gpostffn'], [ytk])
                        P.op('pool', lambda e, yt=yt, c=c, n=n: e.tensor_tensor(out=x1t[:, c, n * 512:(n + 1) * 512],
                                                                                 in0=x1t[:, c, n * 512:(n + 1) * 512], in1=yt[:], op=ALU.add),
                             [ytk, (x1k, c)], [(x1k, c)])
                    rr = r0 + c * 128
                    P.dma('sp', lambda e, c=c, rr=rr: e.dma_start(out=out[rr:rr + 128, :], in_=x1t[:, c, :]),
                          reads=[(x1k, c)], writes=[('out', rr // 128)])

            def load_wup(g):
                P.dma('sp', lambda e: e.dma_start(out=WUP[:, :, g * 512:(g + 1) * 512],
                                                  in_=wup_b[:, g * 512:(g + 1) * 512].rearrange("(kb p) c -> p kb c", p=128)),
                      reads=[('wup_b', g)], writes=[('WUP', g)])

            def load_wdn(g):
                P.dma('sp', lambda e: e.dma_start(out=WDN[:, :, g * 512:(g + 1) * 512],
                                                  in_=wdn_b[:, g * 512:(g + 1) * 512].rearrange("(j p) c -> p j c", p=128)),
                      reads=[('wdn_b', g)], writes=[('WDN', g)])

            load_wup(0)
            load_wup(5)
            if NT2:
                front(0)
                halo_zero(0, 0, (('HB', 0), 'L'))
                front(1)
                halo_copy(0, 257, 1, 1, (('HB', 0), 'R'), (('HB', 1), 0))
            for g in [1, 6, 2, 7, 3, 8, 4, 9, 10]:
                load_wup(g)
            load_wdn(0)
            load_wdn(1)
            for j in range(NT2):
                up(j, range(NSP if j > 0 else 0, NJ))
                b, nb = j % 2, (j + 1) % 2
                if j + 1 < NT2:
                    halo_copy(nb, 0, b, 256, (('HB', nb), 'L'), (('HB', b), 1))
                if j + 2 < NT2:
                    front(j + 2)
                    halo_copy(nb, 257, b, 1, (('HB', nb), 'R'), (('HB', b), 0))
                elif j + 1 < NT2:
                    halo_zero(nb, 257, (('HB', nb), 'R'))
                if j + 1 < NT2:
                    up(j + 1, range(0, NSP))
                else:
                    flush_gate()
                down(j)
            P.barrier()
            _run_block(nc, P.take())
    return nc


_NC_CACHE = {}


def _consts():
    half = 64
    inv_freq = (1.0 / (np.float32(10000.0) ** (np.arange(half, dtype=np.float32) / np.float32(half)))).astype(np.float32)
    ang = (np.arange(S, dtype=np.float32)[:, None] * inv_freq[None, :]).astype(np.float32)
    cos_t = np.cos(ang).astype(np.float32)
    sin_t = np.sin(ang).astype(np.float32)
    idx = np.arange(128, dtype=np.float32)
    m = idx[:, None]
    n = idx[None, :]
    Ef = np.maximum(n - m, 0.0)
    Mf = (n >= m).astype(np.float32) * np.float32(SC)
    Eb = np.maximum(m - n, 0.0)
    Mb = (m > n).astype(np.float32) * np.float32(SC)
    N1 = np.broadcast_to(n + 1.0, (128, 128))
    N128 = np.broadcast_to(128.0 - n, (128, 128))
    dconst = np.ascontiguousarray(np.stack([Ef, Mf, Eb, Mb, N1, N128], axis=1).astype(np.float32))
    cvec = np.ascontiguousarray(np.stack([127.0 - idx, idx], axis=1).astype(np.float32))
    ident = np.eye(128, dtype=np.float32)
    return cos_t, sin_t, dconst, cvec, ident


def kernel(x, g_pre_mix, w_in, ret_decay_logit, sgu_ln_g, sgu_ln_b, w_spatial, b_spatial,
           w_ret_o, w_sgu_o, w_out, g_post_mix, g_pre_ffn, w_up, conv_w, conv_b, w_down, g_post_ffn):
    f = lambda a: np.ascontiguousarray(np.asarray(a, dtype=np.float32))
    x = f(x)
    cos_t, sin_t, dconst, cvec, ident = _consts()
    gcols = np.ascontiguousarray(np.concatenate([f(g_pre_mix).reshape(8, 128).T, f(g_pre_ffn).reshape(8, 128).T], axis=1))
    rows = np.ascontiguousarray(np.stack([f(g_post_mix), f(g_post_ffn), f(sgu_ln_g), f(sgu_ln_b)], axis=0))
    logits = f(ret_decay_logit).reshape(1, 8)
    bspT = np.ascontiguousarray(f(b_spatial).T)
    cwl = np.ascontiguousarray(f(conv_w).reshape(3, 2 * NJ, 128).transpose(2, 1, 0))
    cbl = np.ascontiguousarray(f(conv_b).reshape(2 * NJ, 128).T)
    if 'nc' not in _NC_CACHE:
        _NC_CACHE['nc'] = build_nc()
    nc = _NC_CACHE['nc']
    shared = dict(w_in=f(w_in), w_ret_o=f(w_ret_o), w_sgu_o=f(w_sgu_o), w_out=f(w_out), w_up=f(w_up), w_down=f(w_down),
                  gcols=gcols, rows=rows, logits=logits, wsp=f(w_spatial), bspT=bspT, cwl=cwl, cbl=cbl, ident=ident,
                  cos_t=cos_t, sin_t=sin_t, dconst=dconst, cvec=cvec)
    in_maps = [dict(shared, x=x[b]) for b in range(8)]
    res = run_bass_kernel_spmd(nc, in_maps, core_ids=list(range(8)))
    return np.stack([np.asarray(r["out"], dtype=np.float32) for r in res.results], axis=0)
```
